# Optimizing a Trainium2 kernel written in Bass

```python
import jax, jax.numpy as jnp
from jax import lax
import numpy as np

D_MODEL = 2048
BATCH = 4
SEQ = 2048
DEPTH = 4

HEAD_DIM = 64
RWKV_HEADS = 16
NSA_HEADS = 16
D_RWKV = RWKV_HEADS * HEAD_DIM
D_NSA = NSA_HEADS * HEAD_DIM
D_MIX = D_RWKV + D_NSA
DECAY_LORA = 96
AAA_LORA = 96
MV_LORA = 64
GATE_LORA = 256
GN_EPS = 64e-5
NSA_KV_HEADS = 4
NSA_KV = NSA_KV_HEADS * HEAD_DIM
CMP_BLOCK = 32
CMP_STRIDE = 16
CMP_HIDDEN = 256
SEL_BLOCK = 64
N_SELECT = 8
WINDOW = 512
Q_BLOCK = 128
ROPE_THETA = 10000.0
NEG = -1e30
FORCE = 1e4
D_FF = 4 * D_MODEL
LN_EPS = 1e-5
DN_ALPHA = (2 * DEPTH) ** 0.25
DN_BETA = (8 * DEPTH) ** -0.25
RWKV_COLS = 3 * D_RWKV + DECAY_LORA + AAA_LORA + GATE_LORA
NSA_COLS = D_NSA + 6 * NSA_KV + 3 * NSA_HEADS
IN_COLS = RWKV_COLS + NSA_COLS

kernel_name = "hymba_rwkv7_nsa_deepnorm_adaln"


def _layer_norm(x, g, b):
    xf = x.astype(jnp.float32)
    mu = xf.mean(-1, keepdims=True)
    var = jnp.mean(jnp.square(xf - mu), -1, keepdims=True)
    return ((xf - mu) * lax.rsqrt(var + LN_EPS) * g + b).astype(x.dtype)


def _rope(x, pos):
    half = HEAD_DIM // 2
    inv = ROPE_THETA ** (-jnp.arange(half, dtype=jnp.float32) / half)
    ang = pos.astype(jnp.float32)[:, None] * inv[None]
    cos = jnp.cos(ang)[None, :, None, :]
    sin = jnp.sin(ang)[None, :, None, :]
    x1, x2 = x[..., :half], x[..., half:]
    return jnp.concatenate([x1 * cos - x2 * sin, x2 * cos + x1 * sin], -1).astype(x.dtype)


def _token_shift(u):
    return jnp.pad(u, ((0, 0), (1, 0), (0, 0)))[:, :-1]


def _rwkv7_mix(u, v_first, mu, w0, w2, a0, a2, g2, k_k, k_a, r_k, lnx_g, lnx_b, v_res):
    B, T, _ = u.shape
    H, N = RWKV_HEADS, HEAD_DIM
    u = u + (_token_shift(u) - u) * mu
    r, k, v, xw, xa, xg = jnp.split(u, np.cumsum([D_RWKV] * 3 + [DECAY_LORA, AAA_LORA]).tolist(), axis=-1)
    w = -jax.nn.softplus(-(w0 + jnp.tanh(xw) @ w2)) - 0.5
    decay = jnp.exp(-jnp.exp(w.astype(jnp.float32)))
    a = jax.nn.sigmoid(a0 + xa @ a2)
    g = jax.nn.sigmoid(xg) @ g2
    if v_res is None:
        v_first = v
    else:
        v0, v1, v2 = v_res
        v = v + (v_first - v) * jax.nn.sigmoid(v0 + (v @ v1) @ v2)
    heads = lambda z: z.reshape(B, T, H, N).astype(jnp.float32)
    kk = heads(k * k_k)
    kk = kk / jnp.maximum(jnp.sqrt(jnp.sum(kk * kk, -1, keepdims=True)), 1e-12)
    k = k * (1 + (a - 1) * k_a)
    rh, kh, vh, ah, dh = heads(r), heads(k), heads(v), heads(a), heads(decay)

    def step(S, inp):
        r_t, w_t, k_t, v_t, kk_t, a_t = inp
        sa = jnp.einsum('bhvk,bhk->bhv', S, -kk_t)
        S = S * w_t[:, :, None, :] + sa[..., None] * (kk_t * a_t)[:, :, None, :] + v_t[..., None] * k_t[:, :, None, :]
        return S, jnp.einsum('bhvk,bhk->bhv', S, r_t)

    xs = tuple(jnp.moveaxis(z, 1, 0) for z in (rh, dh, kh, vh, kk, ah))
    _, o = lax.scan(step, jnp.zeros((B, H, N, N), jnp.float32), xs)
    o = jnp.moveaxis(o, 0, 1)
    mu_o = o.mean(-1, keepdims=True)
    var_o = jnp.mean(jnp.square(o - mu_o), -1, keepdims=True)
    o = ((o - mu_o) * lax.rsqrt(var_o + GN_EPS)).reshape(B, T, D_RWKV) * lnx_g + lnx_b
    bonus = jnp.sum(rh * kh * r_k, -1, keepdims=True) * vh
    o = (o + bonus.reshape(B, T, D_RWKV)) * g
    return o.astype(u.dtype), v_first


def _overlap_matrix(n_cmp, n_sel):
    pos = np.arange(n_cmp)[:, None] * CMP_STRIDE + np.arange(CMP_BLOCK)[None]
    m = ((pos // SEL_BLOCK)[..., None] == np.arange(n_sel)).sum(1) / CMP_BLOCK
    return jnp.asarray(m, jnp.float32)


def _compress(kv, pos_emb, w1, w2):
    B, T, G, N = kv.shape
    n_cmp = (T - CMP_BLOCK) // CMP_STRIDE + 1
    idx = np.arange(n_cmp)[:, None] * CMP_STRIDE + np.arange(CMP_BLOCK)[None]
    blocks = kv[:, idx] + pos_emb[:, None, :]
    blocks = jnp.transpose(blocks, (0, 3, 1, 2, 4)).reshape(B, G, n_cmp, CMP_BLOCK * N)
    return jax.nn.gelu(blocks @ w1) @ w2


def _nsa_mix(u, cmp_pos, cmp_w1, cmp_w2):
    B, T, _ = u.shape
    H, G, N = NSA_HEADS, NSA_KV_HEADS, HEAD_DIM
    HPG = H // G
    scale = N ** -0.5
    q, k_c, v_c, k_s, v_s, k_w, v_w, gate = jnp.split(u, np.cumsum([D_NSA] + [NSA_KV] * 6).tolist(), axis=-1)
    pos = jnp.arange(T)
    q = _rope(q.reshape(B, T, H, N), pos)
    k_c, k_s, k_w = (_rope(z.reshape(B, T, G, N), pos) for z in (k_c, k_s, k_w))
    v_c, v_s, v_w = (z.reshape(B, T, G, N) for z in (v_c, v_s, v_w))
    qg = q.reshape(B, T, G, HPG, N).transpose(0, 2, 3, 1, 4)

    kc = _compress(k_c, cmp_pos[0], cmp_w1[0], cmp_w2[0])
    vc = _compress(v_c, cmp_pos[1], cmp_w1[1], cmp_w2[1])
    n_cmp = kc.shape[2]
    cmask = (jnp.arange(n_cmp) * CMP_STRIDE + CMP_BLOCK - 1)[None, :] <= pos[:, None]
    s = jnp.einsum('bghtd,bgcd->bghtc', qg, kc).astype(jnp.float32) * scale
    p_cmp = jax.nn.softmax(jnp.where(cmask, s, NEG), -1) * cmask
    o_cmp = jnp.einsum('bghtc,bgcd->bghtd', p_cmp.astype(vc.dtype), vc)

    n_sel = T // SEL_BLOCK
    k_top = min(N_SELECT, n_sel)
    imp = jnp.einsum('bgtc,cs->bgts', p_cmp.sum(2), _overlap_matrix(n_cmp, n_sel))
    blk = jnp.arange(n_sel)[None]
    cur = (pos // SEL_BLOCK)[:, None]
    forced = (blk == 0) | (blk == cur) | (blk == cur - 1)
    causal = blk * SEL_BLOCK <= pos[:, None]
    imp = jnp.where(forced, FORCE, jnp.where(causal, imp, -1.0))
    _, sel_idx = lax.top_k(imp, k_top)

    ks_blocks = k_s.reshape(B, n_sel, SEL_BLOCK, G, N).transpose(0, 3, 1, 2, 4)
    vs_blocks = v_s.reshape(B, n_sel, SEL_BLOCK, G, N).transpose(0, 3, 1, 2, 4)
    kw_pad = jnp.pad(k_w, ((0, 0), (WINDOW, 0), (0, 0), (0, 0)))
    vw_pad = jnp.pad(v_w, ((0, 0), (WINDOW, 0), (0, 0), (0, 0)))
    n_q = T // Q_BLOCK
    q_blocks = qg.reshape(B, G, HPG, n_q, Q_BLOCK, N).transpose(3, 0, 1, 2, 4, 5)
    idx_blocks = sel_idx.reshape(B, G, n_q, Q_BLOCK, k_top).transpose(2, 0, 1, 3, 4)
    b_ix = jnp.arange(B)[:, None, None, None]
    g_ix = jnp.arange(G)[None, :, None, None]

    def block_attn(args):
        qi, qb, ib = args
        t = qi * Q_BLOCK + jnp.arange(Q_BLOCK)
        ksel = ks_blocks[b_ix, g_ix, ib]
        vsel = vs_blocks[b_ix, g_ix, ib]
        kpos = ib[..., None] * SEL_BLOCK + jnp.arange(SEL_BLOCK)
        smask = kpos <= t[:, None, None]
        ss = jnp.einsum('bghqd,bgqkld->bghqkl', qb, ksel).astype(jnp.float32) * scale
        ss = jnp.where(smask[:, :, None], ss, NEG).reshape(B, G, HPG, Q_BLOCK, -1)
        ps = jax.nn.softmax(ss, -1).reshape(B, G, HPG, Q_BLOCK, k_top, SEL_BLOCK)
        o_s = jnp.einsum('bghqkl,bgqkld->bghqd', ps.astype(vsel.dtype), vsel)
        start = qi * Q_BLOCK
        kw = lax.dynamic_slice_in_dim(kw_pad, start, Q_BLOCK + WINDOW, axis=1)
        vw = lax.dynamic_slice_in_dim(vw_pad, start, Q_BLOCK + WINDOW, axis=1)
        spos = start - WINDOW + jnp.arange(Q_BLOCK + WINDOW)
        wmask = (spos[None] <= t[:, None]) & (t[:, None] - spos[None] < WINDOW) & (spos[None] >= 0)
        sw = jnp.einsum('bghqd,bsgd->bghqs', qb, kw).astype(jnp.float32) * scale
        pw = jax.nn.softmax(jnp.where(wmask, sw, NEG), -1)
        o_w = jnp.einsum('bghqs,bsgd->bghqd', pw.astype(vw.dtype), vw)
        return o_s, o_w

    o_slc, o_win = lax.map(block_attn, (jnp.arange(n_q), q_blocks, idx_blocks))
    o_cmp = o_cmp.transpose(0, 3, 1, 2, 4).reshape(B, T, H, N)
    o_slc = o_slc.transpose(1, 0, 4, 2, 3, 5).reshape(B, T, H, N)
    o_win = o_win.transpose(1, 0, 4, 2, 3, 5).reshape(B, T, H, N)
    gate = jax.nn.sigmoid(gate).reshape(B, T, H, 3)
    o = gate[..., 0:1] * o_cmp + gate[..., 1:2] * o_slc + gate[..., 2:3] * o_win
    return o.reshape(B, T, D_NSA).astype(u.dtype)


def setup_inputs(seed: int = 0) -> dict:
    key = jax.random.key(seed)
    keys = iter(jax.random.split(key, 40))
    nrm = lambda shape, s: jax.random.normal(next(keys), shape, jnp.float32) * s
    unif = lambda shape, lo, hi: jax.random.uniform(next(keys), shape, jnp.float32, lo, hi)
    L, D = DEPTH, D_MODEL
    return {
        "x": nrm((BATCH, SEQ, D), 1.0),
        "c": nrm((BATCH, D), 1.0),
        "w_ada": nrm((L, D, 6 * D), 0.2 * D ** -0.5),
        "b_ada": nrm((L, 6 * D), 0.02),
        "w_in": nrm((L, D, IN_COLS), D ** -0.5),
        "rwkv_mu": unif((L, RWKV_COLS), 0.0, 1.0),
        "rwkv_w0": unif((L, D_RWKV), -6.5, -1.5),
        "rwkv_w2": nrm((L, DECAY_LORA, D_RWKV), 0.5 * DECAY_LORA ** -0.5),
        "rwkv_a0": nrm((L, D_RWKV), 0.1),
        "rwkv_a2": nrm((L, AAA_LORA, D_RWKV), 0.5 * AAA_LORA ** -0.5),
        "rwkv_g2": nrm((L, GATE_LORA, D_RWKV), GATE_LORA ** -0.5),
        "rwkv_k_k": 0.85 + nrm((L, D_RWKV), 0.05),
        "rwkv_k_a": 1.0 + nrm((L, D_RWKV), 0.05),
        "rwkv_r_k": nrm((L, RWKV_HEADS, HEAD_DIM), 0.1),
        "rwkv_lnx_g": 1.0 + nrm((L, D_RWKV), 0.05),
        "rwkv_lnx_b": nrm((L, D_RWKV), 0.02),
        "rwkv_v0": nrm((L - 1, D_RWKV), 0.1),
        "rwkv_v1": nrm((L - 1, D_RWKV, MV_LORA), D_RWKV ** -0.5),
        "rwkv_v2": nrm((L - 1, MV_LORA, D_RWKV), 0.5 * MV_LORA ** -0.5),
        "nsa_cmp_pos": nrm((L, 2, CMP_BLOCK, HEAD_DIM), 0.1),
        "nsa_cmp_w1": nrm((L, 2, CMP_BLOCK * HEAD_DIM, CMP_HIDDEN), (CMP_BLOCK * HEAD_DIM) ** -0.5),
        "nsa_cmp_w2": nrm((L, 2, CMP_HIDDEN, HEAD_DIM), CMP_HIDDEN ** -0.5),
        "w_out": nrm((L, D_MIX, D), DN_BETA * D_MIX ** -0.5),
        "ln1_g": 1.0 + nrm((L, D), 0.05),
        "ln1_b": nrm((L, D), 0.02),
        "mlp_w1": nrm((L, D, D_FF), D ** -0.5),
        "mlp_w2": nrm((L, D_FF, D), DN_BETA * D_FF ** -0.5),
        "ln2_g": 1.0 + nrm((L, D), 0.05),
        "ln2_b": nrm((L, D), 0.02),
    }


def reference(x, c, w_ada, b_ada, w_in, rwkv_mu, rwkv_w0, rwkv_w2, rwkv_a0, rwkv_a2, rwkv_g2,
              rwkv_k_k, rwkv_k_a, rwkv_r_k, rwkv_lnx_g, rwkv_lnx_b, rwkv_v0, rwkv_v1, rwkv_v2,
              nsa_cmp_pos, nsa_cmp_w1, nsa_cmp_w2, w_out, ln1_g, ln1_b, mlp_w1, mlp_w2, ln2_g, ln2_b):
    cond = jax.nn.silu(c)
    v_first = None
    for i in range(DEPTH):
        mod = cond @ w_ada[i] + b_ada[i]
        sh_m, sc_m, gt_m, sh_f, sc_f, gt_f = [m[:, None, :] for m in jnp.split(mod, 6, axis=-1)]
        h = x * (1 + sc_m) + sh_m
        u = h @ w_in[i]
        v_res = None if i == 0 else (rwkv_v0[i - 1], rwkv_v1[i - 1], rwkv_v2[i - 1])
        o_rwkv, v_first = _rwkv7_mix(u[..., :RWKV_COLS], v_first, rwkv_mu[i], rwkv_w0[i], rwkv_w2[i],
                                     rwkv_a0[i], rwkv_a2[i], rwkv_g2[i], rwkv_k_k[i], rwkv_k_a[i],
                                     rwkv_r_k[i], rwkv_lnx_g[i], rwkv_lnx_b[i], v_res)
        o_nsa = _nsa_mix(u[..., RWKV_COLS:], nsa_cmp_pos[i], nsa_cmp_w1[i], nsa_cmp_w2[i])
        y = jnp.concatenate([o_rwkv, o_nsa], axis=-1) @ w_out[i]
        x = _layer_norm(DN_ALPHA * x + (1 + gt_m) * y, ln1_g[i], ln1_b[i])
        h = x * (1 + sc_f) + sh_f
        y = jnp.square(jax.nn.relu(h @ mlp_w1[i])) @ mlp_w2[i]
        x = _layer_norm(DN_ALPHA * x + (1 + gt_f) * y, ln2_g[i], ln2_b[i])
    return x
```

```python
import numpy as np
from contextlib import ExitStack
import concourse.bass as bass
import concourse.mybir as mybir
from concourse.bass_utils import run_bass_kernel_spmd

F32 = mybir.dt.float32
F32R = mybir.dt.float32r
R_ = lambda ap: ap.bitcast(F32R)
BF16 = mybir.dt.bfloat16
AF = mybir.ActivationFunctionType
ALU = mybir.AluOpType

D = 2048
T = 2048
L = 4
NKC = 16
DFF = 8192
IN_COLS = 6128
C_R, C_K, C_V, C_XW, C_XA, C_XG = 0, 1024, 2048, 3072, 3168, 3264
NB = 3520
C_Q, C_KC, C_VC, C_KS, C_VS, C_KW, C_VW, C_GATE = NB, NB + 1024, NB + 1280, NB + 1536, NB + 1792, NB + 2048, NB + 2304, NB + 2560
DN_ALPHA = (2 * L) ** 0.25
LN_EPS = 1e-5
GN_EPS = 64e-5

VOFF = {}
_cur = 0
for _n, _w in [("b_ada", 96), ("mu_r", 8), ("mu_k", 8), ("mu_v", 8), ("mu_w", 1), ("mu_a", 1), ("mu_g", 2),
               ("w0", 8), ("a0", 8), ("k_k", 8), ("k_a", 8), ("r_k", 8), ("lnx_g", 8), ("lnx_b", 8), ("v0", 8),
               ("ln1_g", 16), ("ln1_b", 16), ("ln2_g", 16), ("ln2_b", 16)]:
    VOFF[_n] = _cur
    _cur += _w
NV = _cur
V64 = {}
_cur = 0
for _n in ["mu_r", "mu_k", "mu_v", "w0", "a0", "k_k", "k_a", "r_k", "lnx_g", "lnx_b", "v0", "om_r", "om_k", "om_v", "om_ka"]:
    V64[_n] = _cur
    _cur += 16
NV64 = _cur


class Buf:
    def __init__(self, t, name="", excl=False):
        self.t = t
        self.name = name
        self.w = None
        self.r = []
        self.excl = excl

    def __getitem__(self, k):
        return self.t[k]


class Eng:
    def __init__(self, name, e, sem):
        self.name = name
        self.e = e
        self.sem = sem
        self.count = 0
        self.waited = {}


class Slot:
    def __init__(self, sem, key):
        self.sem = sem
        self.key = key
        self.val = 0


class Ctx:
    def __init__(self, nc, stack):
        self.nc = nc
        mk = lambda n: stack.enter_context(nc.semaphore(n))
        self.pe = Eng("pe", nc.tensor, mk("s_pe"))
        self.act = Eng("act", nc.scalar, mk("s_act"))
        self.dve = Eng("dve", nc.vector, mk("s_dve"))
        self.pool = Eng("pool", nc.gpsimd, mk("s_pool"))
        self.sp = Eng("sp", nc.sync, None)
        self.engs = [self.pe, self.act, self.dve, self.pool, self.sp]
        self.slots = {
            "sp": [Slot(mk(f"d_sp{i}"), f"d_sp{i}") for i in range(24)],
            "pool": [Slot(mk(f"d_pl{i}"), f"d_pl{i}") for i in range(24)],
            "act": [Slot(mk(f"d_ac{i}"), f"d_ac{i}") for i in range(8)],
        }
        self.slot_i = {"sp": 0, "pool": 0, "act": 0}
        self.n_instr = 0

    def _wait(self, eng, tok):
        if tok is None:
            return
        sem, val, key, _ = tok
        if eng.waited.get(key, 0) >= val:
            return
        eng.e.wait_ge(sem, val)
        eng.waited[key] = val
        self.n_instr += 1

    def _deps(self, eng, outs, ins):
        for b in ins:
            self._wait(eng, b.w)
            if b.excl:
                for r in b.r:
                    if r[3] is not eng:
                        self._wait(eng, r)
        for b in outs:
            if b.w is not None and b.w[3] is not eng:
                self._wait(eng, b.w)
            for r in b.r:
                if r[3] is not eng:
                    self._wait(eng, r)

    def _commit(self, tok, outs, ins):
        for b in outs:
            b.w = tok
            b.r = []
        for b in ins:
            if len(b.r) > 12:
                last = {}
                for r in b.r:
                    if r[2] not in last or last[r[2]][1] < r[1]:
                        last[r[2]] = r
                b.r = list(last.values())
            b.r.append(tok)

    def op(self, eng, fn, outs=(), ins=()):
        self._deps(eng, outs, ins)
        instr = fn()
        eng.count += 1
        instr.then_inc(eng.sem, 1)
        self.n_instr += 1
        tok = (eng.sem, eng.count, eng.name, eng)
        self._commit(tok, outs, ins)
        return tok

    def dma(self, q, out_ap, in_ap, outs=(), ins=(), **kw):
        self._deps(q, outs, ins)
        sl = self.slots[q.name]
        s = sl[self.slot_i[q.name] % len(sl)]
        self.slot_i[q.name] += 1
        if s.val > 0:
            self._wait(q, (s.sem, s.val, s.key, None))
        instr = q.e.dma_start(out=out_ap, in_=in_ap, **kw)
        s.val += 16
        instr.then_inc(s.sem, 16)
        self.n_instr += 1
        tok = (s.sem, s.val, s.key, None)
        self._commit(tok, outs, ins)
        return tok

    def all_tokens(self):
        toks = []
        for e in self.engs:
            if e.sem is not None and e.count > 0:
                toks.append((e.sem, e.count, e.name, e))
        for sl in self.slots.values():
            for s in sl:
                if s.val > 0:
                    toks.append((s.sem, s.val, s.key, None))
        return toks

    def barrier(self, engines=None):
        toks = self.all_tokens()
        for e in (engines or self.engs):
            for t in toks:
                if t[3] is not e:
                    self._wait(e, t)


class Prog:
    def __init__(self, n_layers=L, debug=None):
        self.n_layers = n_layers
        self.debug = debug or {}
        self.nc = nc = bass.Bass("TRN2", target_bir_lowering=False)
        dt = lambda name, shape, kind, dtype=F32: nc.dram_tensor(name, list(shape), dtype, kind=kind).ap()
        IN = "ExternalInput"
        self.xT_in = dt("xT", [D, T], IN)
        self.cvec = dt("cvec", [128, NKC], IN)
        self.vecs = dt("vecs", [L, 128, NV], IN)
        self.w_ada = dt("w_ada", [L, D, 6 * D], IN)
        self.w_in = dt("w_in", [L, D, IN_COLS], IN)
        self.w_out = dt("w_out", [L, D, D], IN)
        self.mlp_w1 = dt("mlp_w1", [L, D, DFF], IN)
        self.mlp_w2 = dt("mlp_w2", [L, DFF, D], IN)
        self.rw_w2 = dt("rwkv_w2", [L, 96, 1024], IN)
        self.rw_a2 = dt("rwkv_a2", [L, 96, 1024], IN)
        self.rw_g2 = dt("rwkv_g2", [L, 256, 1024], IN)
        self.rw_v1 = dt("rwkv_v1", [L - 1, 1024, 64], IN)
        self.rw_v2 = dt("rwkv_v2", [L - 1, 64, 1024], IN)
        self.cmp_pos = dt("nsa_cmp_pos", [L, 2, 32, 64], IN)
        self.cmp_w1 = dt("nsa_cmp_w1", [L, 2, 2048, 256], IN)
        self.cmp_w2 = dt("nsa_cmp_w2", [L, 2, 256, 64], IN)
        self.consts = dt("consts", [128, NCONST], IN)
        self.vec64 = dt("vec64", [L, 64, NV64], IN)
        self.nsac = dt("nsac", [128, NNSAC], IN)
        self.cmp_pos2 = dt("cmp_pos2", [L, 2, 128, 16], IN)
        self.rope = dt("rope", [2, 128, T], IN)
        self.outT = dt("outT", [D, T], "ExternalOutput")
        self.d_x = Buf(dt("d_x", [D, T], "Internal"), "d_x")
        self.d_u = [Buf(dt(f"d_u{i}", [128, T], "Internal")[:, :], f"d_u{i}") for i in range(0)]
        self.d_uT = dt("d_uT", [6144, T], "Internal")
        self.d_u_rows = {}
        self.d_o = dt("d_o", [D, T], "Internal")
        self.d_o_rows = {}
        self.d_y = dt("d_y", [D, T], "Internal")
        self.d_y_rows = {}
        self.d_vf = dt("d_vf", [1024, T], "Internal")
        self.d_lw = dt("d_lw", [1024, T], "Internal")
        self.d_alr = dt("d_alr", [1024, T], "Internal")
        self.d_g = dt("d_g", [1024, T], "Internal")
        self.d_v = dt("d_v", [1024, T], "Internal")
        self.d_ar0 = dt("d_ar0", [1024, T], "Internal")
        self.d_ar1 = dt("d_ar1", [1024, T], "Internal")
        self.d_bk0 = dt("d_bk0", [1024, T], "Internal")
        self.d_bk1 = dt("d_bk1", [1024, T], "Internal")
        self.d_bon = dt("d_bon", [1024, T], "Internal")
        self.d_gc = dt("d_gc", [1024, 16], "Internal")
        self.d_vf_rows = {}
        self.dbg_out = {}
        self.dbg_in = {}
        for name, shape in self.debug.get("ins", {}).items():
            self.dbg_in[name] = dt("dbgin_" + name, shape, IN)
        for name, shape in self.debug.get("outs", {}).items():
            self.dbg_out[name] = dt("dbg_" + name, shape, "ExternalOutput")

    def rows(self, table, base, r0, n):
        key = (r0, n)
        if key not in table:
            table[key] = Buf(base[r0:r0 + n, :], f"rows{r0}")
        return table[key]


NCONST = 0
CONST_OFF = {}


def _add_const(name, arr):
    global NCONST
    CONST_OFF[name] = (NCONST, arr.shape[1])
    NCONST += arr.shape[1]
    return arr


def build_consts():
    global NCONST, CONST_OFF
    NCONST = 0
    CONST_OFF = {}
    parts = []
    p = np.arange(128)[:, None]
    f = np.arange(128)[None, :]
    parts.append(_add_const("ident", (p == f).astype(np.float32)))
    parts.append(_add_const("ones", np.ones((128, 128), np.float32)))
    parts.append(_add_const("bd64", ((p // 64) == (f // 64)).astype(np.float32)))
    parts.append(_add_const("SL", (p > f).astype(np.float32)))
    parts.append(_add_const("SU", (p < f).astype(np.float32)))
    parts.append(_add_const("UI", (p <= f).astype(np.float32)))
    return np.concatenate(parts, axis=1)


_CONSTS = build_consts()

NSAC = {}


def build_nsa_consts():
    parts = []
    cur = [0]

    def add(name, arr):
        a = np.zeros((128, arr.shape[1]), np.float32)
        a[:arr.shape[0]] = arr
        NSAC[name] = (cur[0], arr.shape[1])
        cur[0] += arr.shape[1]
        parts.append(a)

    c = np.arange(127)[:, None]
    t = np.arange(T)[None, :]
    add("cmaskT", ((16 * c + 31) <= t).astype(np.float32))
    pos = np.arange(127)[:, None] * 16 + np.arange(32)[None]
    ov = ((pos // 64)[..., None] == np.arange(32)).sum(1) / 32.0
    add("ov", ov.astype(np.float32))
    tt = np.arange(T)
    blk = np.arange(32)[None]
    cur_b = (tt // 64)[:, None]
    forced = (blk == 0) | (blk == cur_b) | (blk == cur_b - 1)
    causal = blk * 64 <= tt[:, None]
    Mc = (causal & ~forced).astype(np.float32)
    Ma = np.where(forced, 1e4, np.where(causal, 0.0, -1.0)).astype(np.float32)
    add("Mc", Mc.reshape(16, 128, 32).transpose(1, 0, 2).reshape(128, 512))
    add("Ma", Ma.reshape(16, 128, 32).transpose(1, 0, 2).reshape(128, 512))
    j = np.arange(32)[:, None, None]
    kt = np.arange(16)[None, :, None]
    s_ = np.arange(128)[None, None, :]
    add("expand", (j == 2 * kt + s_ // 64).astype(np.float32).reshape(32, 2048))
    return np.concatenate(parts, axis=1)


_NSA_CONSTS = build_nsa_consts()
NNSAC = _NSA_CONSTS.shape[1]


AX = mybir.AxisListType


def cs_(name, n=128, rows=128):
    o = CONST_OFF[name][0]
    return slice(o, o + n)


def emit_rwkv(P, cx, li, psum, vec, cst, sb):
    nc = P.nc
    pe, act, dve, pool, sp = cx.pe, cx.act, cx.dve, cx.pool, cx.sp
    bi = [0]

    ada_next = getattr(P, "ada_next", None)
    NBANK = 7 if ada_next is not None else 8

    def bank():
        bi[0] += 1
        return psum[bi[0] % NBANK]

    d_uT = P.d_uT
    ident = lambda n=128, m=128: cst[0:n, CONST_OFF["ident"][0]:CONST_OFF["ident"][0] + m]
    bd64 = cst[:, CONST_OFF["bd64"][0]:CONST_OFF["bd64"][0] + 128]
    mSL = cst[:, cs_("SL")]
    mSU = cst[:, cs_("SU")]
    mUI = cst[:, cs_("UI")]
    NBK = 8
    CW = 0.6065306597126334

    def chunks(ap, n=8):
        return [Buf(ap[c * 128:(c + 1) * 128, :], f"dr{c}") for c in range(n)]
    D_lw, D_alr, D_g, D_v, D_vf = chunks(P.d_lw), chunks(P.d_alr), chunks(P.d_g), chunks(P.d_v), chunks(P.d_vf)
    D_AR0, D_AR1, D_BK0, D_BK1 = chunks(P.d_ar0), chunks(P.d_ar1), chunks(P.d_bk0), chunks(P.d_bk1)
    D_bon, D_gC = chunks(P.d_bon), chunks(P.d_gc)
    with ExitStack() as ph:
        v64 = sb("v64", [64, NV64], F32, ph)
        cx.dma(sp, v64[:, :], P.vec64[li], outs=[v64])
        omL = sb("omL", [128, 36], F32, ph)
        o_ = VOFF["mu_w"]
        cx.op(dve, lambda: nc.vector.tensor_scalar(omL[:, 0:4], vec[:, o_:o_ + 4], -1.0, 1.0, ALU.mult, ALU.add), outs=[omL], ins=[vec])
        for c0_, nm_ in ((4, "mu_v"), (12, "mu_r"), (20, "mu_k"), (28, "k_a")):
            cx.op(dve, lambda c0_=c0_, nm_=nm_: nc.vector.tensor_scalar(omL[:, c0_:c0_ + 8], vec[:, VOFF[nm_]:VOFF[nm_] + 8], -1.0, 1.0, ALU.mult, ALU.add),
                  outs=[omL], ins=[vec])
        rmask = sb("rmask", [128, T], BF16, ph)
        cx.op(pool, lambda: nc.gpsimd.memset(rmask[:, :], 1.0), outs=[rmask])
        cx.op(pool, lambda: nc.gpsimd.memset(rmask.t.rearrange("p (b c) -> p b c", c=128)[:, :, 0:1], 0.0), outs=[rmask])

        def head_vec(name, hd):
            return v64[:, V64[name] + hd:V64[name] + hd + 1]

        def mk_lerp(raw, e2):
            def load_lerp(dst_buf, dst_ap, row0, n, mu_ap, om_ap):
                cx.dma(sp, raw[0:n, :], d_uT[row0:row0 + n, :], outs=[raw])
                cx.op(act, lambda: nc.scalar.activation(e2[0:n, :], raw[0:n, :], AF.Identity, scale=om_ap), outs=[e2], ins=[raw] + [v64, omL])
                cx.op(dve, lambda: nc.vector.scalar_tensor_tensor(dst_ap(slice(1, T)), raw[0:n, 0:T - 1], mu_ap, e2[0:n, 1:T], ALU.mult, ALU.add),
                      outs=[dst_buf], ins=[raw, e2, v64, vec])
                cx.op(dve, lambda: nc.vector.tensor_copy(dst_ap(slice(0, 1)), e2[0:n, 0:1]), outs=[dst_buf], ins=[e2])
            return load_lerp

        mw, ma, mg = VOFF["mu_w"], VOFF["mu_a"], VOFF["mu_g"]
        with ExitStack() as ph2:
            raw = sb("raw", [128, T], F32, ph2)
            e1 = sb("e1", [128, T], F32, ph2)
            e2 = sb("e2", [128, T], F32, ph2)
            stg = [sb(f"lstg{i}", [128, T], F32, ph2) for i in range(2)]
            w2s = sb("w2s", [96, 1024], BF16, ph2)
            a2s = sb("a2s", [96, 1024], BF16, ph2)
            g2s = sb("g2s", [128, 2, 1024], BF16, ph2)
            tw = sb("tw", [96, T], BF16, ph2)
            xa = sb("xa", [96, T], BF16, ph2)
            sg = sb("sg", [128, 2, T], BF16, ph2)
            cx.dma(pool, w2s[:, :], P.rw_w2[li], outs=[w2s])
            cx.dma(pool, a2s[:, :], P.rw_a2[li], outs=[a2s])
            cx.dma(pool, g2s[:, :, :], P.rw_g2[li].rearrange("(c p) n -> p c n", p=128), outs=[g2s])
            load_lerp = mk_lerp(raw, e2)
            load_lerp(e1, lambda c: e1[0:96, c], C_XW, 96, vec[0:96, mw:mw + 1], omL[0:96, 0:1])
            cx.op(act, lambda: nc.scalar.activation(tw[:, :], e1[0:96, :], AF.Tanh), outs=[tw], ins=[e1])
            load_lerp(e1, lambda c: e1[0:96, c], C_XA, 96, vec[0:96, ma:ma + 1], omL[0:96, 1:2])
            cx.op(act, lambda: nc.scalar.copy(xa[:, :], e1[0:96, :]), outs=[xa], ins=[e1])
            for c2 in range(2):
                load_lerp(e1, lambda c, c2=c2: e1[:, c], C_XG + c2 * 128, 128, vec[:, mg + c2:mg + c2 + 1], omL[:, 2 + c2:3 + c2])
                cx.op(act, lambda c2=c2: nc.scalar.activation(sg[:, c2, :], e1[:, :], AF.Sigmoid), outs=[sg], ins=[e1])
            ns = [0]
            for cc in range(8):
                csl = slice(cc * 128, (cc + 1) * 128)
                for kind in range(3):
                    st_ = stg[ns[0] % 2]
                    ns[0] += 1
                    for tt in range(4):
                        tsl = slice(tt * 512, (tt + 1) * 512)
                        pb = bank()
                        if kind == 0:
                            cx.op(pe, lambda pb=pb, tsl=tsl: nc.tensor.matmul(pb[:, :], w2s[:, csl], tw[:, tsl], start=True, stop=True), outs=[pb], ins=[w2s, tw])
                            cx.op(act, lambda pb=pb, tsl=tsl, st_=st_: nc.scalar.activation(st_[:, tsl], pb[:, :], AF.Sigmoid, bias=vec[:, VOFF["w0"] + cc:VOFF["w0"] + cc + 1]),
                                  outs=[st_], ins=[pb, vec])
                        elif kind == 1:
                            cx.op(pe, lambda pb=pb, tsl=tsl: nc.tensor.matmul(pb[:, :], a2s[:, csl], xa[:, tsl], start=True, stop=True), outs=[pb], ins=[a2s, xa])
                            cx.op(act, lambda pb=pb, tsl=tsl, st_=st_: nc.scalar.activation(st_[:, tsl], pb[:, :], AF.Sigmoid, bias=vec[:, VOFF["a0"] + cc:VOFF["a0"] + cc + 1]),
                                  outs=[st_], ins=[pb, vec])
                        else:
                            for c2 in range(2):
                                cx.op(pe, lambda pb=pb, tsl=tsl, c2=c2: nc.tensor.matmul(pb[:, :], g2s[:, c2, csl], sg[:, c2, tsl], start=(c2 == 0), stop=(c2 == 1)),
                                      outs=[pb], ins=[g2s, sg])
                            cx.op(dve, lambda pb=pb, tsl=tsl, st_=st_: nc.vector.tensor_copy(st_[:, tsl], pb[:, :]), outs=[st_], ins=[pb])
                    dst = (D_lw, D_alr, D_g)[kind][cc]
                    cx.dma(sp, dst[:, :], st_[:, :], outs=[dst], ins=[st_])
            if li == 0:
                for cc in range(8):
                    st_ = stg[ns[0] % 2]
                    ns[0] += 1
                    load_lerp(st_, lambda c, st_=st_: st_[:, c], C_V + cc * 128, 128, vec[:, VOFF["mu_v"] + cc:VOFF["mu_v"] + cc + 1], omL[:, 4 + cc:5 + cc])
                    cx.dma(sp, D_vf[cc][:, :], st_[:, :], outs=[D_vf[cc]], ins=[st_])
            else:
                v1s = sb("v1s", [128, 8, 64], F32, ph2)
                v2s = sb("v2s", [64, 1024], F32, ph2)
                vv1 = sb("vv1", [64, T], F32, ph2)
                cx.dma(sp, v1s[:, :, :], P.rw_v1[li - 1].rearrange("(c p) n -> p c n", p=128), outs=[v1s])
                cx.dma(sp, v2s[:, :], P.rw_v2[li - 1], outs=[v2s])
                for cc in range(8):
                    st_ = stg[ns[0] % 2]
                    ns[0] += 1
                    load_lerp(st_, lambda c, st_=st_: st_[:, c], C_V + cc * 128, 128, vec[:, VOFF["mu_v"] + cc:VOFF["mu_v"] + cc + 1], omL[:, 4 + cc:5 + cc])
                    cx.dma(sp, D_v[cc][:, :], st_[:, :], outs=[D_v[cc]], ins=[st_])
                    for tt in range(4):
                        cx.op(pe, lambda tt=tt, cc=cc, st_=st_: nc.tensor.matmul(psum[4 + tt][0:64, :], v1s[:, cc, :], st_[:, tt * 512:(tt + 1) * 512],
                                                                                start=(cc == 0), stop=(cc == 7)), outs=[psum[4 + tt]], ins=[v1s, st_])
                for tt in range(4):
                    cx.op(act, lambda tt=tt: nc.scalar.copy(vv1[:, tt * 512:(tt + 1) * 512], psum[4 + tt][0:64, :]), outs=[vv1], ins=[psum[4 + tt]])
                for cc in range(8):
                    csl = slice(cc * 128, (cc + 1) * 128)
                    st_ = stg[ns[0] % 2]
                    ns[0] += 1
                    for tt in range(4):
                        tsl = slice(tt * 512, (tt + 1) * 512)
                        pb = bank()
                        cx.op(pe, lambda pb=pb, tsl=tsl: nc.tensor.matmul(pb[:, :], v2s[:, csl], vv1[:, tsl], start=True, stop=True), outs=[pb], ins=[v2s, vv1])
                        cx.op(act, lambda pb=pb, tsl=tsl, st_=st_: nc.scalar.activation(st_[:, tsl], pb[:, :], AF.Sigmoid, bias=vec[:, VOFF["v0"] + cc:VOFF["v0"] + cc + 1]),
                              outs=[st_], ins=[pb, vec])
                    cx.dma(sp, raw[:, :], D_vf[cc][:, :], outs=[raw], ins=[D_vf[cc]])
                    cx.dma(sp, e1[:, :], D_v[cc][:, :], outs=[e1], ins=[D_v[cc]])
                    cx.op(dve, lambda: nc.vector.tensor_tensor(raw[:, :], raw[:, :], e1[:, :], ALU.subtract), outs=[raw], ins=[raw, e1])
                    cx.op(dve, lambda st_=st_: nc.vector.tensor_tensor(raw[:, :], raw[:, :], st_[:, :], ALU.mult), outs=[raw], ins=[raw, st_])
                    cx.op(dve, lambda st_=st_: nc.vector.tensor_tensor(st_[:, :], e1[:, :], raw[:, :], ALU.add), outs=[st_], ins=[raw, e1])
                    cx.dma(sp, D_v[cc][:, :], st_[:, :], outs=[D_v[cc]], ins=[st_])
            cx.barrier()
        Dvsrc = D_vf if li == 0 else D_v
        with ExitStack() as ph2:
            raw = sb("raw", [128, T], F32, ph2)
            e1 = sb("e1", [128, T], F32, ph2)
            e2 = sb("e2", [128, T], F32, ph2)
            rp = sb("rp", [128, T], F32, ph2)
            kp = sb("kp", [128, T], F32, ph2)
            kk = sb("kk", [128, T], F32, ph2)
            alr = sb("alr", [128, T], F32, ph2)
            lw = sb("lw", [128, T], F32, ph2)
            cum = sb("cum", [128, T], F32, ph2)
            vfin = sb("vfin", [128, T], F32, ph2)
            ARo = sb("ARo", [128, 2, T], F32, ph2)
            BKo = sb("BKo", [128, 2, T], F32, ph2)
            bono = sb("bono", [128, T], F32, ph2)
            gCo = sb("gCo", [128, 16], F32, ph2)
            load_lerp = mk_lerp(raw, e2)
            for cc in range(8):
                cv = lambda nm_: vec[:, VOFF[nm_] + cc:VOFF[nm_] + cc + 1]
                load_lerp(rp, lambda c: rp[:, c], C_R + cc * 128, 128, cv("mu_r"), omL[:, 12 + cc:13 + cc])
                load_lerp(kp, lambda c: kp[:, c], C_K + cc * 128, 128, cv("mu_k"), omL[:, 20 + cc:21 + cc])
                cx.dma(sp, vfin[:, :], Dvsrc[cc][:, :], outs=[vfin], ins=[Dvsrc[cc]])
                cx.dma(sp, lw[:, :], D_lw[cc][:, :], outs=[lw], ins=[D_lw[cc]])
                cx.dma(sp, alr[:, :], D_alr[cc][:, :], outs=[alr], ins=[D_alr[cc]])
                cx.op(dve, lambda: nc.vector.tensor_scalar(kk[:, :], kp[:, :], cv("k_k"), None, ALU.mult), outs=[kk], ins=[kp, vec])
                cx.op(act, lambda: nc.scalar.activation(e1[:, :], kk[:, :], AF.Square), outs=[e1], ins=[kk])
                cx.op(dve, lambda: nc.vector.tensor_tensor_scan(cum[:, :], rmask[:, :], lw[:, :], 0.0, ALU.mult, ALU.add), outs=[cum], ins=[rmask, lw])
                cumv = cum.t.rearrange("p (b c) -> p b c", c=128)
                cx.op(dve, lambda: nc.vector.tensor_tensor(raw[:, :].rearrange("p (b c) -> p b c", c=128), cumv,
                                                           cumv[:, :, 127:128].broadcast_to([128, 16, 128]), ALU.subtract), outs=[raw], ins=[cum])
                for tt in range(4):
                    tsl = slice(tt * 512, (tt + 1) * 512)
                    pb = bank()
                    cx.op(pe, lambda pb=pb, tsl=tsl: nc.tensor.matmul(pb[:, :], bd64, e1[:, tsl], start=True, stop=True), outs=[pb], ins=[cst, e1])
                    cx.op(act, lambda pb=pb, tsl=tsl: nc.scalar.activation(e2[:, tsl], pb[:, :], AF.Sqrt), outs=[e2], ins=[pb])
                cx.op(act, lambda: nc.scalar.activation(gCo[:, :], cumv[:, :, 127], AF.Exp, scale=-CW), outs=[gCo], ins=[cum])
                cx.op(act, lambda: nc.scalar.activation(ARo[:, 1, :], raw[:, :], AF.Exp, scale=-CW), outs=[ARo], ins=[raw])
                cx.op(act, lambda: nc.scalar.activation(BKo[:, 1, :], raw[:, :], AF.Exp, scale=CW), outs=[BKo], ins=[raw])
                cx.op(dve, lambda: nc.vector.tensor_tensor(cum[:, :], raw[:, :], lw[:, :], ALU.subtract), outs=[cum], ins=[raw, lw, gCo])
                cx.op(act, lambda: nc.scalar.activation(ARo[:, 0, :], cum[:, :], AF.Exp, scale=-CW), outs=[ARo], ins=[cum])
                cx.op(dve, lambda: nc.vector.tensor_tensor(ARo[:, 1, :], ARo[:, 1, :], rp[:, :], ALU.mult), outs=[ARo], ins=[ARo, rp])
                cx.op(dve, lambda: nc.vector.tensor_scalar(e2[:, :], e2[:, :], 1e-12, None, ALU.max), outs=[e2], ins=[e2])
                cx.op(dve, lambda: nc.vector.reciprocal(e2[:, :], e2[:, :]), outs=[e2], ins=[e2])
                cx.op(dve, lambda: nc.vector.tensor_tensor(kk[:, :], kk[:, :], e2[:, :], ALU.mult), outs=[kk], ins=[kk, e2])
                cx.op(dve, lambda: nc.vector.tensor_scalar(e1[:, :], alr[:, :], cv("k_a"), omL[:, 28 + cc:29 + cc], ALU.mult, ALU.add), outs=[e1], ins=[alr, vec, omL])
                cx.op(dve, lambda: nc.vector.tensor_tensor(kp[:, :], kp[:, :], e1[:, :], ALU.mult), outs=[kp], ins=[kp, e1])
                cx.op(dve, lambda: nc.vector.scalar_tensor_tensor(e1[:, :], rp[:, :], cv("r_k"), kp[:, :], ALU.mult, ALU.mult), outs=[e1], ins=[rp, kp, vec])
                for tt in range(4):
                    tsl = slice(tt * 512, (tt + 1) * 512)
                    pb = bank()
                    cx.op(pe, lambda pb=pb, tsl=tsl: nc.tensor.matmul(pb[:, :], bd64, e1[:, tsl], start=True, stop=True), outs=[pb], ins=[cst, e1])
                    cx.op(dve, lambda pb=pb, tsl=tsl: nc.vector.tensor_tensor(bono[:, tsl], pb[:, :], vfin[:, tsl], ALU.mult), outs=[bono], ins=[pb, vfin])
                cx.op(dve, lambda: nc.vector.scalar_tensor_tensor(ARo[:, 0, :], kk[:, :], -1.0, ARo[:, 0, :], ALU.mult, ALU.mult), outs=[ARo], ins=[kk, ARo])
                cx.op(dve, lambda: nc.vector.tensor_tensor(e2[:, :], kk[:, :], alr[:, :], ALU.mult), outs=[e2], ins=[kk, alr])
                cx.op(dve, lambda: nc.vector.tensor_tensor(BKo[:, 0, :], e2[:, :], BKo[:, 1, :], ALU.mult), outs=[BKo], ins=[e2, BKo])
                cx.op(dve, lambda: nc.vector.tensor_tensor(BKo[:, 1, :], BKo[:, 1, :], kp[:, :], ALU.mult), outs=[BKo], ins=[BKo, kp])
                cx.dma(sp, D_AR0[cc][:, :], ARo[:, 0, :], outs=[D_AR0[cc]], ins=[ARo])
                cx.dma(sp, D_AR1[cc][:, :], ARo[:, 1, :], outs=[D_AR1[cc]], ins=[ARo])
                cx.dma(sp, D_BK0[cc][:, :], BKo[:, 0, :], outs=[D_BK0[cc]], ins=[BKo])
                cx.dma(sp, D_BK1[cc][:, :], BKo[:, 1, :], outs=[D_BK1[cc]], ins=[BKo])
                cx.dma(sp, D_bon[cc][:, :], bono[:, :], outs=[D_bon[cc]], ins=[bono])
                cx.dma(sp, D_gC[cc].t[:, 0:16], gCo[:, :], outs=[D_gC[cc]], ins=[gCo])
            cx.barrier()

        vp = sb("vp", [64, T], F32, ph)
        g_t = [sb(f"g_t{i}", [64, T], F32, ph) for i in range(2)]
        bon = [sb(f"bon{i}", [64, T], F32, ph) for i in range(2)]
        gC = [sb(f"gC{i}", [64, 17], F32, ph) for i in range(2)]
        for i in range(2):
            cx.op(pool, lambda i=i: nc.gpsimd.memset(gC[i][:, :], 1.0), outs=[gC[i]])
        AR = sb("AR", [64, 2, T], F32, ph)
        BK = sb("BK", [64, 2, T], F32, ph)
        Oall = sb("Oall", [128, 16, 64], F32, ph)
        st1 = sb("st1", [128, 16], F32, ph)
        st2 = sb("st2", [128, 16], F32, ph)
        st3 = sb("st3", [128, 16], F32, ph)
        Hall = sb("Hall", [64, 16, 64], F32, ph)
        FTs = [sb(f"FT{i}", [64, 16, 64], F32, ph) for i in range(2)]
        Jgs = [sb(f"Jg{i}", [64, 16, 64], F32, ph) for i in range(2)]
        RwTs = [sb(f"RwT{i}", [64, 16, 128], F32, ph) for i in range(2)]
        Zss = [sb(f"Zs{i}", [128, 16, 64], F32, ph) for i in range(2)]
        ef = [sb(f"ef{i}", [64, 512], F32, ph) for i in range(2)]
        tk8 = sb("tk8", [128, NBK, 256], F32, ph)
        PA8 = sb("PA8", [128, NBK, 2, 128], F32, ph)
        QA8 = sb("QA8", [128, NBK, 2, 128], F32, ph)
        Ark8 = sb("Ark8", [128, NBK, 128], F32, ph)
        Pm8 = [sb(f"Pm8_{j}", [128, NBK, 128], F32, ph) for j in range(2)]
        Qm8 = [sb(f"Qm8_{j}", [128, NBK, 128], F32, ph) for j in range(2)]
        Mm8 = [sb(f"Mm8_{j}", [128, NBK, 128], F32, ph) for j in range(2)]
        Gm8 = sb("Gm8", [128, NBK, 128], F32, ph)
        Wt8 = sb("Wt8", [128, NBK, 64], F32, ph)
        Ys8 = sb("Ys8", [128, NBK, 64], F32, ph)
        quad = lambda t, q: Buf(t.t[:, 4 * q:4 * q + 4], t.name + f"q{q}")
        PmQ = [[quad(Pm8[j], q) for q in range(2)] for j in range(2)]
        QmQ = [[quad(Qm8[j], q) for q in range(2)] for j in range(2)]
        MmQ = [[quad(Mm8[j], q) for q in range(2)] for j in range(2)]
        su_ui = cst[:, CONST_OFF["SU"][0]:CONST_OFF["SU"][0] + 256].rearrange("p (m c) -> p m c", m=2)

        def loads(hd, par):
            cc, half = hd // 2, (hd % 2) * 64
            hsl = slice(half, half + 64)
            cx.dma(sp, AR[:, 0, :], D_AR0[cc].t[hsl, :], outs=[AR], ins=[D_AR0[cc]])
            cx.dma(sp, AR[:, 1, :], D_AR1[cc].t[hsl, :], outs=[AR], ins=[D_AR1[cc]])
            cx.dma(sp, BK[:, 0, :], D_BK0[cc].t[hsl, :], outs=[BK], ins=[D_BK0[cc]])
            cx.dma(sp, BK[:, 1, :], D_BK1[cc].t[hsl, :], outs=[BK], ins=[D_BK1[cc]])
            cx.dma(sp, vp[:, :], Dvsrc[cc].t[hsl, :], outs=[vp], ins=[Dvsrc[cc]])
            cx.dma(sp, bon[par][:, :], D_bon[cc].t[hsl, :], outs=[bon[par]], ins=[D_bon[cc]])
            cx.dma(sp, g_t[par][:, :], D_g[cc].t[hsl, :], outs=[g_t[par]], ins=[D_g[cc]])
            cx.dma(sp, gC[par][:, 0:16], D_gC[cc].t[hsl, 0:16], outs=[gC[par]], ins=[D_gC[cc]])
            cx.op(dve, lambda: nc.vector.tensor_copy(R_(AR[:, :, :]), AR[:, :, :]), outs=[AR], ins=[AR])
            cx.op(act, lambda: nc.scalar.copy(R_(BK[:, :, :]), BK[:, :, :]), outs=[BK], ins=[BK])

        def B_gen(hd, par):
            FT, Jg, RwT, Zs = FTs[par], Jgs[par], RwTs[par], Zss[par]
            gC_cur = gC[par]
            for b0 in range(0, 16, NBK):
                tok0 = b0 * 128
                yield
                for p in range(4):
                    B = bank()
                    for i2 in range(2):
                        i = 2 * p + i2
                        cs = slice(tok0 + i * 128, tok0 + (i + 1) * 128)
                        for n_, src_ap in enumerate((AR[:, 0, cs], BK[:, 0, cs], BK[:, 1, cs], vp[:, cs])):
                            cx.op(pe, lambda B=B, i2=i2, n_=n_, src_ap=src_ap: nc.tensor.transpose(
                                B[:, i2 * 256 + n_ * 64:i2 * 256 + (n_ + 1) * 64], src_ap, ident(64, 64)), outs=[B], ins=[AR, BK, vp, cst])
                    cx.op(act, lambda B=B, p=p: nc.scalar.copy(R_(tk8[:, 2 * p:2 * p + 2, :]), B[:, :].rearrange("p (b c) -> p b c", b=2)), outs=[tk8], ins=[B])
                yield
                for p in range(4):
                    B = bank()
                    for i2 in range(2):
                        i = 2 * p + i2
                        cs = slice(tok0 + i * 128, tok0 + (i + 1) * 128)
                        cx.op(pe, lambda B=B, i2=i2, cs=cs: nc.tensor.matmul(B[:, i2 * 256:(i2 + 1) * 256], R_(AR[:, 0, cs]), R_(BK[:, :, cs]), start=True, stop=True),
                              outs=[B], ins=[AR, BK])
                    cx.op(dve, lambda B=B, p=p: nc.vector.tensor_tensor(
                        R_(PA8[:, 2 * p:2 * p + 2, :, :]), B[:, :].rearrange("p (b m c) -> p b m c", b=2, m=2),
                        mSL.unsqueeze(1).unsqueeze(1).broadcast_to([128, 2, 2, 128]), ALU.mult), outs=[PA8], ins=[B, cst])
                yield
                for p in range(4):
                    B = bank()
                    for i2 in range(2):
                        i = 2 * p + i2
                        cs = slice(tok0 + i * 128, tok0 + (i + 1) * 128)
                        cx.op(pe, lambda B=B, i2=i2, cs=cs: nc.tensor.matmul(B[:, i2 * 256:(i2 + 1) * 256], R_(BK[:, 0, cs]), R_(AR[:, :, cs]), start=True, stop=True),
                              outs=[B], ins=[AR, BK])
                    cx.op(dve, lambda B=B, p=p: nc.vector.tensor_tensor(
                        R_(QA8[:, 2 * p:2 * p + 2, :, :]), B[:, :].rearrange("p (b m c) -> p b m c", b=2, m=2),
                        su_ui.unsqueeze(1).broadcast_to([128, 2, 2, 128]), ALU.mult), outs=[QA8], ins=[B, cst])
                yield
                for q in range(2):
                    B = bank()
                    for i4 in range(4):
                        i = 4 * q + i4
                        cs = slice(tok0 + i * 128, tok0 + (i + 1) * 128)
                        cx.op(pe, lambda B=B, i4=i4, cs=cs: nc.tensor.matmul(B[:, i4 * 128:(i4 + 1) * 128], R_(BK[:, 1, cs]), R_(AR[:, 1, cs]), start=True, stop=True),
                              outs=[B], ins=[AR, BK])
                    cx.op(dve, lambda B=B, q=q: nc.vector.tensor_tensor(
                        R_(Ark8[:, 4 * q:4 * q + 4, :]), B[:, :].rearrange("p (b c) -> p b c", b=4),
                        mUI.unsqueeze(1).broadcast_to([128, 4, 128]), ALU.mult), outs=[Ark8], ins=[B, cst])
                for q in range(2):
                    cx.op(dve, lambda q=q: nc.vector.tensor_tensor(
                        R_(Mm8[0][:, 4 * q:4 * q + 4, :]), QA8[:, 4 * q:4 * q + 4, 0, :], ident().unsqueeze(1).broadcast_to([128, 4, 128]), ALU.add),
                        outs=[MmQ[0][q]], ins=[QA8, cst])
                yield
                def Pj(j, i):
                    return PA8[:, i, 0, :] if j == 0 else Pm8[j % 2][:, i, :]

                def Qj(j, i):
                    return QA8[:, i, 0, :] if j == 0 else Qm8[j % 2][:, i, :]

                def Pb(j, q):
                    return [PA8] if j == 0 else [PmQ[j % 2][q]]

                def Qb(j, q):
                    return [QA8] if j == 0 else [QmQ[j % 2][q]]

                def emit_pq(j):
                    n = (j + 1) % 2
                    for q in range(2):
                        B = bank()
                        for i4 in range(4):
                            i = 4 * q + i4
                            cx.op(pe, lambda B=B, i4=i4, i=i, j=j: nc.tensor.matmul(B[:, i4 * 128:(i4 + 1) * 128], R_(Qj(j, i)), R_(Pj(j, i)), start=True, stop=True),
                                  outs=[B], ins=Pb(j, q) + Qb(j, q))
                        cx.op(act, lambda B=B, q=q, n=n: nc.scalar.copy(R_(Pm8[n][:, 4 * q:4 * q + 4, :]), B[:, :].rearrange("p (b c) -> p b c", b=4)),
                              outs=[PmQ[n][q]], ins=[B])
                    if j < 5:
                        for q in range(2):
                            B = bank()
                            for i4 in range(4):
                                i = 4 * q + i4
                                cx.op(pe, lambda B=B, i4=i4, i=i, j=j: nc.tensor.matmul(B[:, i4 * 128:(i4 + 1) * 128], R_(Pj(j, i)), R_(Qj(j, i)), start=True, stop=True),
                                      outs=[B], ins=Pb(j, q) + Qb(j, q))
                            if q == 0:
                                cx.op(act, lambda B=B, q=q, n=n: nc.scalar.copy(R_(Qm8[n][:, 4 * q:4 * q + 4, :]), B[:, :].rearrange("p (b c) -> p b c", b=4)),
                                      outs=[QmQ[n][q]], ins=[B])
                            else:
                                cx.op(dve, lambda B=B, q=q, n=n: nc.vector.tensor_copy(R_(Qm8[n][:, 4 * q:4 * q + 4, :]), B[:, :].rearrange("p (b c) -> p b c", b=4)),
                                      outs=[QmQ[n][q]], ins=[B])

                def emit_m(j):
                    n = (j + 1) % 2
                    for q in range(2):
                        B = bank()
                        for i4 in range(4):
                            i = 4 * q + i4
                            cx.op(pe, lambda B=B, i4=i4, i=i, j=j, n=n: nc.tensor.matmul(B[:, i4 * 128:(i4 + 1) * 128], R_(Pm8[n][:, i, :]), R_(Mm8[j % 2][:, i, :]),
                                                                                    start=True, stop=True), outs=[B], ins=[PmQ[n][q], MmQ[j % 2][q]])
                        cx.op(dve, lambda B=B, q=q, j=j, n=n: nc.vector.tensor_tensor(
                            R_(Mm8[n][:, 4 * q:4 * q + 4, :]), B[:, :].rearrange("p (b c) -> p b c", b=4), Mm8[j % 2][:, 4 * q:4 * q + 4, :], ALU.add),
                            outs=[MmQ[n][q]], ins=[B, MmQ[j % 2][q]])

                emit_pq(0)
                yield
                for j in range(6):
                    if j + 1 < 6:
                        emit_pq(j + 1)
                        yield
                    emit_m(j)
                    yield
                Mf = Mm8[0]
                MfQ = MmQ[0]
                yield
                B = bank()
                for i in range(NBK):
                    cx.op(pe, lambda B=B, i=i: nc.tensor.matmul(B[:, i * 64:(i + 1) * 64], R_(Mf[:, i, :]), R_(tk8[:, i, 0:64]), start=True, stop=True),
                          outs=[B], ins=[MfQ[i // 4], tk8])
                cx.op(act, lambda B=B: nc.scalar.copy(Wt8[:, :, :], B[:, :].rearrange("p (b c) -> p b c", b=8)), outs=[Wt8], ins=[B])
                yield
                for q in range(2):
                    B = bank()
                    for i4 in range(4):
                        i = 4 * q + i4
                        cx.op(pe, lambda B=B, i4=i4, i=i: nc.tensor.matmul(B[:, i4 * 128:(i4 + 1) * 128], R_(PA8[:, i, 1, :]), R_(Mf[:, i, :]), start=True, stop=True),
                              outs=[B], ins=[PA8, MfQ[q]])
                    cx.op(dve, lambda B=B, q=q: nc.vector.tensor_copy(R_(Gm8[:, 4 * q:4 * q + 4, :]), B[:, :].rearrange("p (b c) -> p b c", b=4)), outs=[Gm8], ins=[B])
                yield
                B = bank()
                for i in range(NBK):
                    cx.op(pe, lambda B=B, i=i: nc.tensor.matmul(B[:, i * 64:(i + 1) * 64], R_(Gm8[:, i, :]), R_(tk8[:, i, 192:256]), start=True, stop=True),
                          outs=[B], ins=[Gm8, tk8])
                cx.op(act, lambda B=B: nc.scalar.copy(R_(Ys8[:, :, :]), B[:, :].rearrange("p (b c) -> p b c", b=8)), outs=[Ys8], ins=[B])
                yield
                B = bank()
                for i in range(NBK):
                    cx.op(pe, lambda B=B, i=i: nc.tensor.matmul(B[0:64, i * 64:(i + 1) * 64], Wt8[:, i, :], tk8[:, i, 64:128], start=True, stop=True),
                          outs=[B], ins=[Wt8, tk8])
                cx.op(dve, lambda B=B, b0=b0: nc.vector.tensor_tensor(FT[:, b0:b0 + 8, :], B[0:64, :].rearrange("p (b c) -> p b c", b=8),
                                                                      ident(64, 64).unsqueeze(1).broadcast_to([64, 8, 64]), ALU.add), outs=[FT], ins=[B, cst])
                yield
                B = bank()
                for i in range(NBK):
                    cx.op(pe, lambda B=B, i=i: nc.tensor.matmul(B[0:64, i * 64:(i + 1) * 64], tk8[:, i, 64:128], Ys8[:, i, :], start=True, stop=False),
                          outs=[B], ins=[Ys8, tk8])
                    cx.op(pe, lambda B=B, i=i: nc.tensor.matmul(B[0:64, i * 64:(i + 1) * 64], tk8[:, i, 128:192], tk8[:, i, 192:256], start=False, stop=True),
                          outs=[B], ins=[tk8])
                cx.op(dve, lambda B=B, b0=b0: nc.vector.tensor_tensor(Jg[:, b0:b0 + 8, :], B[0:64, :].rearrange("p (b c) -> p b c", b=8),
                                                                      gC_cur[:, b0 + 1:b0 + 9].unsqueeze(2).broadcast_to([64, 8, 64]), ALU.mult), outs=[Jg], ins=[B, gC_cur])
                yield
                for q in range(2):
                    B = bank()
                    for i4 in range(4):
                        i = 4 * q + i4
                        cx.op(pe, lambda B=B, i4=i4, i=i: nc.tensor.matmul(B[0:64, i4 * 128:(i4 + 1) * 128], Wt8[:, i, :], QA8[:, i, 1, :], start=True, stop=True),
                              outs=[B], ins=[Wt8, QA8])
                    t0_ = tok0 + q * 512
                    cx.op(dve, lambda B=B, q=q, b0=b0, t0_=t0_: nc.vector.tensor_tensor(
                        RwT[:, b0 + 4 * q:b0 + 4 * q + 4, :], B[0:64, :].rearrange("p (b c) -> p b c", b=4),
                        AR[:, 1, t0_:t0_ + 512].rearrange("p (b c) -> p b c", b=4), ALU.add), outs=[RwT], ins=[B, AR])
                yield
                B = bank()
                for i in range(NBK):
                    cx.op(pe, lambda B=B, i=i: nc.tensor.matmul(B[:, i * 64:(i + 1) * 64], R_(QA8[:, i, 1, :]), R_(Ys8[:, i, :]), start=True, stop=False),
                          outs=[B], ins=[QA8, Ys8])
                    cx.op(pe, lambda B=B, i=i: nc.tensor.matmul(B[:, i * 64:(i + 1) * 64], R_(Ark8[:, i, :]), R_(tk8[:, i, 192:256]), start=False, stop=True),
                          outs=[B], ins=[Ark8, tk8])
                cx.op(act, lambda B=B, b0=b0: nc.scalar.copy(Zs[:, b0:b0 + 8, :], B[:, :].rearrange("p (b c) -> p b c", b=8)), outs=[Zs], ins=[B])

            yield

        def tail_gen(hd, par):
            gCp = gC[par]
            FT, Jg, RwT, Zs = FTs[par], Jgs[par], RwTs[par], Zss[par]
            Osq = Zs
            Hb = [Buf(Hall.t[:, c, :], f"H{c}") for c in range(16)]
            cx.op(dve, lambda: nc.vector.memset(Hb[0][:, :], 0.0), outs=[Hb[0]], ins=[Hall])
            cx.op(dve, lambda: nc.vector.tensor_copy(Hb[1][:, :], Jg[:, 0, :]), outs=[Hb[1]], ins=[Jg, Hall])
            for c in range(1, 15):
                B = bank()
                cx.op(pe, lambda c=c, B=B: nc.tensor.matmul(B[0:64, 0:64], FT[:, c, :], Hb[c][:, :], start=True, stop=True), outs=[B], ins=[FT, Hb[c]])
                cx.op(dve, lambda c=c, B=B: nc.vector.scalar_tensor_tensor(Hb[c + 1][:, :], B[0:64, 0:64], gCp[:, c + 1:c + 2], Jg[:, c, :], ALU.mult, ALU.add),
                      outs=[Hb[c + 1]], ins=[B, gCp, Jg])
                yield
            for c0 in (0, 8):
                B = bank()
                for i in range(8):
                    c = c0 + i
                    cx.op(pe, lambda c=c, i=i, B=B: nc.tensor.matmul(B[:, i * 64:(i + 1) * 64], RwT[:, c, :], Hb[c][:, :], start=True, stop=True),
                          outs=[B], ins=[RwT, Hb[c]])
                cx.op(dve, lambda c0=c0, B=B: nc.vector.tensor_tensor(Oall[:, c0:c0 + 8, :], B[:, :].rearrange("p (b c) -> p b c", b=8), Zs[:, c0:c0 + 8, :], ALU.add),
                      outs=[Oall], ins=[B, Zs])
                yield
            cx.op(dve, lambda: nc.vector.tensor_reduce(st1[:, :], Oall[:, :, :], AX.X, ALU.add), outs=[st1], ins=[Oall] + Hb)
            cx.op(act, lambda: nc.scalar.activation(Osq[:, :, :], Oall[:, :, :], AF.Square), outs=[Osq], ins=[Oall])
            cx.op(dve, lambda: nc.vector.tensor_reduce(st2[:, :], Osq[:, :, :], AX.X, ALU.add), outs=[st2], ins=[Osq])
            yield
            cx.op(dve, lambda: nc.vector.tensor_scalar(st1[:, :], st1[:, :], 1.0 / 64, None, ALU.mult), outs=[st1], ins=[st1])
            cx.op(dve, lambda: nc.vector.tensor_tensor(st3[:, :], st1[:, :], st1[:, :], ALU.mult), outs=[st3], ins=[st1])
            cx.op(dve, lambda: nc.vector.scalar_tensor_tensor(st2[:, :], st2[:, :], 1.0 / 64, st3[:, :], ALU.mult, ALU.subtract), outs=[st2], ins=[st2, st3])
            cx.op(dve, lambda: nc.vector.tensor_scalar(st2[:, :], st2[:, :], GN_EPS, None, ALU.add), outs=[st2], ins=[st2])
            cx.op(act, lambda: nc.scalar.activation(st3[:, :], st2[:, :], AF.Sqrt), outs=[st3], ins=[st2])
            cx.op(dve, lambda: nc.vector.reciprocal(st2[:, :], st3[:, :]), outs=[st2], ins=[st3])
            yield
            for blk in range(16):
                cx.op(dve, lambda blk=blk: nc.vector.tensor_scalar(Osq[:, blk, :], Oall[:, blk, :], st1[:, blk:blk + 1], st2[:, blk:blk + 1],
                                                                  ALU.subtract, ALU.mult), outs=[Osq], ins=[Oall, st1, st2])
                if blk % 4 == 3:
                    yield
            for tt in range(4):
                pb = bank()
                efb = ef[tt % 2]
                for b4 in range(4):
                    blk = tt * 4 + b4
                    cx.op(pe, lambda pb=pb, b4=b4, blk=blk: nc.tensor.transpose(pb[0:64, b4 * 128:(b4 + 1) * 128], Osq[:, blk, :], ident()),
                          outs=[pb], ins=[Osq, cst])
                tsl = slice(tt * 512, (tt + 1) * 512)
                cx.op(act, lambda pb=pb, efb=efb: nc.scalar.activation(efb[:, :], pb[0:64, :], AF.Identity, bias=head_vec("lnx_b", hd),
                                                                       scale=head_vec("lnx_g", hd)), outs=[efb], ins=[pb, v64])
                cx.op(dve, lambda efb=efb, tsl=tsl: nc.vector.tensor_tensor(efb[:, :], efb[:, :], bon[par][:, tsl], ALU.add), outs=[efb], ins=[efb, bon[par]])
                cx.op(dve, lambda efb=efb, tsl=tsl: nc.vector.tensor_tensor(efb[:, :], efb[:, :], g_t[par][:, tsl], ALU.mult), outs=[efb], ins=[efb, g_t[par]])
                cx.dma(sp, P.d_o[hd * 64:(hd + 1) * 64, tsl], efb[:, :], ins=[efb])
                yield


        def drain(gen, n=None):
            if gen is None:
                return None
            k = 0
            while n is None or k < n:
                try:
                    next(gen)
                except StopIteration:
                    return None
                k += 1
            return gen

        ada = None
        if ada_next is not None:
            wa2 = [sb(f"wa2_{i}", [128, NKC, 128], BF16, ph) for i in range(2)]

            glist = [(ada_next, g, g) for g in range(96)]
            if getattr(P, "split0", False) and li == 0:
                glist = [(0, 32 + g, 96 + g) for g in range(64)] + glist

            def ada_gen():
                def issue(n):
                    wb = wa2[n % 2]
                    lyr, ch, _ = glist[n]
                    cx.dma(pool, wb[:, :, :], P.w_ada[lyr][:, ch * 128:(ch + 1) * 128].rearrange("(kc p) n -> p kc n", p=128), outs=[wb])
                issue(0)
                yield
                for n in range(len(glist)):
                    if n + 1 < len(glist):
                        issue(n + 1)
                    wb = wa2[n % 2]
                    col = glist[n][2]
                    for kc in range(NKC):
                        cx.op(pe, lambda col=col, kc=kc, wb=wb: nc.tensor.matmul(psum[7][:, col:col + 1], wb[:, kc, :], P.cond_bf[:, kc:kc + 1],
                                                                              start=(kc == 0), stop=(kc == NKC - 1)), outs=[psum[7]], ins=[wb, P.cond_bf])
                    yield
            ada = ada_gen()

        prev_tail = None
        nheads = P.debug.get("rwkv_heads") or 16
        for hd in range(nheads):
            par = hd % 2
            loads(hd, par)
            kq = 0
            for _ in B_gen(hd, par):
                kq += 1
                if kq % 2 == 0:
                    prev_tail = drain(prev_tail, 1)
                if kq % 4 == 0:
                    ada = drain(ada, 1)
            prev_tail = drain(prev_tail)
            prev_tail = tail_gen(hd, par)
        drain(prev_tail)
        if ada_next is not None:
            drain(ada)
            cx.op(act, lambda: nc.scalar.copy(P.modraw[:, :], psum[7][:, 0:96]), outs=[P.modraw], ins=[psum[7]])
            if getattr(P, "split0", False) and li == 0:
                cx.op(act, lambda: nc.scalar.copy(P.modraw0[:, :], psum[7][:, 96:160]), outs=[P.modraw0], ins=[psum[7]])
        cx.barrier()


def emit_nsa(P, cx, li, psum, vec, cst, sb):
    nc = P.nc
    pe, act, dve, pool, sp = cx.pe, cx.act, cx.dve, cx.pool, cx.sp
    d_uT = P.d_uT
    SCALE = 0.125
    ident = lambda n=128, m=128: cst[0:n, CONST_OFF["ident"][0]:CONST_OFF["ident"][0] + m]
    mUI = cst[:, cs_("UI")]
    mSL = cst[:, cs_("SL")]
    msc = [0]

    def misc():
        msc[0] += 1
        return psum[6]

    with ExitStack() as ph:
        nsac = sb("nsac", [128, NNSAC], F32, ph)
        cx.dma(sp, nsac[:, :], P.nsac[:, :], outs=[nsac])
        ncs = lambda name: slice(NSAC[name][0], NSAC[name][0] + NSAC[name][1])
        cosT = sb("cosT", [64, T], F32, ph)
        sinT = sb("sinT", [64, T], F32, ph)
        cx.dma(sp, cosT[:, :], P.rope[0][0:64, :], outs=[cosT])
        cx.dma(sp, sinT[:, :], P.rope[1][0:64, :], outs=[sinT])
        raw = sb("nraw", [128, T], F32, ph)
        rot = sb("nrot", [64, T], F32, ph)
        t1 = sb("nt1", [64, T], F32, ph)
        kcr = sb("kcr", [64, T], F32, ph)
        K2 = sb("K2", [128, T], F32, ph)
        sets = [dict(qr=sb(f"qr{i}", [64, 4, T], BF16, ph), ksr=sb(f"ksr{i}", [64, T], BF16, ph), kwr=sb(f"kwr{i}", [64, T], BF16, ph),
                     kcT=sb(f"kcT{i}", [64, 128], BF16, ph), vs_aug=sb(f"vs_aug{i}", [128, 16, 65], BF16, ph),
                     vw_aug=sb(f"vw_aug{i}", [128, 16, 65], BF16, ph), vc_aug=sb(f"vc_aug{i}", [128, 97], BF16, ph)) for i in range(2)]
        w1s = sb("w1s", [128, 16, 256], F32, ph)
        w2s = sb("w2s_n", [128, 2, 64], F32, ph)
        pos2 = sb("pos2", [128, 16], F32, ph)
        cbias = sb("cbias", [128, 2], F32, ph)
        Gt = sb("Gt", [128, 2, 128], F32, ph)
        gx = sb("gx", [128, 2, 128], F32, ph)
        gy = sb("gy", [128, 2, 128], F32, ph)
        gsb = sb("gsb", [128, 16, 48], F32, ph)
        Eb = [sb(f"Eb{i}", [128, 4, 128], BF16, ph) for i in range(6)]
        mk = [sb(f"mk{i}", [128, 128], BF16, ph) for i in range(3)]
        selTs = [sb(f"selT{i}", [32, 128], BF16, ph) for i in range(2)]
        mk4 = [sb(f"mk4_{i}", [128, 4, 128], BF16, ph) for i in range(2)]
        mUIb = sb("mUIb", [128, 128], BF16, ph)
        expb = sb("expb", [32, 16, 128], BF16, ph)
        impt = sb("impt", [128, 32], F32, ph)
        impf = sb("impf", [128, 32], F32, ph)
        top8 = sb("top8", [128, 8], F32, ph)
        selm = sb("selm", [128, 32], F32, ph)
        rds = [sb(f"rd{i}", [128, 12], F32, ph) for i in range(2)]
        cf = sb("cf", [128, 12], F32, ph)
        Rcs = [sb(f"Rc{i}", [128, 4, 97], F32, ph) for i in range(2)]
        otok = sb("otok", [128, 4, 64], F32, ph)
        ofm = sb("ofm", [128, 2, T], F32, ph)

        cx.op(dve, lambda: nc.vector.tensor_copy(expb[:, :, :], nsac[0:32, ncs("expand")].rearrange("p (k s) -> p k s", s=128)), outs=[expb], ins=[nsac])
        cx.op(dve, lambda: nc.vector.tensor_copy(mUIb[:, :], mUI), outs=[mUIb], ins=[cst])
        cx.dma(sp, raw[0:48, :], d_uT[C_GATE:C_GATE + 48, :], outs=[raw])
        for i in range(16):
            pb = misc()
            cx.op(pe, lambda pb=pb, i=i: nc.tensor.transpose(pb[:, 0:48], raw[0:48, i * 128:(i + 1) * 128], ident(48, 48)), outs=[pb], ins=[raw, cst])
            cx.op(act, lambda pb=pb, i=i: nc.scalar.activation(gsb[:, i, :], pb[:, 0:48], AF.Sigmoid), outs=[gsb], ins=[pb])

        def load_rope(row0, out_buf, out_ap):
            cx.dma(sp, raw[0:64, :], d_uT[row0:row0 + 64, :], outs=[raw])
            cx.dma(sp, rot[0:32, :], d_uT[row0 + 32:row0 + 64, :], outs=[rot])
            cx.dma(sp, rot[32:64, :], d_uT[row0:row0 + 32, :], outs=[rot])
            cx.op(dve, lambda: nc.vector.tensor_tensor(t1[:, :], raw[0:64, :], cosT[:, :], ALU.mult), outs=[t1], ins=[raw, cosT])
            cx.op(dve, lambda: nc.vector.tensor_tensor(rot[:, :], rot[:, :], sinT[:, :], ALU.mult), outs=[rot], ins=[rot, sinT])
            cx.op(dve, lambda: nc.vector.tensor_tensor(out_ap, t1[:, :], rot[:, :], ALU.add), outs=[out_buf], ins=[t1, rot])

        def v_aug_build(row0, dst):
            cx.dma(sp, raw[0:64, :], d_uT[row0:row0 + 64, :], outs=[raw])
            cx.op(pool, lambda: nc.gpsimd.memset(dst[:, :, 64:65], 1.0), outs=[dst])
            for i in range(16):
                pb = misc()
                cx.op(pe, lambda pb=pb, i=i: nc.tensor.transpose(pb[:, 0:64], raw[0:64, i * 128:(i + 1) * 128], ident(64, 64)), outs=[pb], ins=[raw, cst])
                cx.op(act, lambda pb=pb, i=i: nc.scalar.copy(dst[:, i, 0:64], pb[:, 0:64]), outs=[dst], ins=[pb])

        def compress(kv, g):
            cx.dma(sp, w1s[:, :, :], P.cmp_w1[li][kv].rearrange("(a p) h -> p a h", p=128), outs=[w1s])
            cx.dma(sp, w2s[:, :, :], P.cmp_w2[li][kv].rearrange("(c p) d -> p c d", p=128), outs=[w2s])
            cx.dma(sp, pos2[:, :], P.cmp_pos2[li][kv], outs=[pos2])
            K2v = K2.t.rearrange("p (c s) -> p c s", s=16)
            for ch in range(2):
                pbias = misc()
                for a in range(16):
                    cx.op(pe, lambda a=a, ch=ch, pbias=pbias: nc.tensor.matmul(pbias[:, 0:1], w1s[:, a, ch * 128:(ch + 1) * 128], pos2[:, a:a + 1],
                                                                               start=(a == 0), stop=(a == 15)), outs=[pbias], ins=[w1s, pos2])
                cx.op(dve, lambda ch=ch, pbias=pbias: nc.vector.tensor_copy(cbias[:, ch:ch + 1], pbias[:, 0:1]), outs=[cbias], ins=[pbias])
                pb = misc()
                for a in range(16):
                    rhs = K2v[:, 0:127, 2 * a] if a < 8 else K2v[:, 1:128, 2 * a - 16]
                    cx.op(pe, lambda a=a, ch=ch, pb=pb, rhs=rhs: nc.tensor.matmul(pb[:, 0:127], w1s[:, a, ch * 128:(ch + 1) * 128], rhs,
                                                                                  start=(a == 0), stop=(a == 15)), outs=[pb], ins=[w1s, K2])
                cx.op(act, lambda ch=ch, pb=pb: nc.scalar.activation(gx[:, ch, 0:127], pb[:, 0:127], AF.Identity, bias=cbias[:, ch:ch + 1]),
                      outs=[gx], ins=[pb, cbias])
            cx.op(pool, lambda: nc.gpsimd.tensor_tensor(gy[:, :, 0:127], gx[:, :, 0:127], gx[:, :, 0:127], ALU.mult), outs=[gy], ins=[gx])
            cx.op(dve, lambda: nc.vector.tensor_scalar(gy[:, :, 0:127], gy[:, :, 0:127], 0.044715, 1.0, ALU.mult, ALU.add), outs=[gy], ins=[gy])
            cx.op(dve, lambda: nc.vector.tensor_tensor(gy[:, :, 0:127], gy[:, :, 0:127], gx[:, :, 0:127], ALU.mult), outs=[gy], ins=[gy, gx])
            cx.op(act, lambda: nc.scalar.activation(gy[:, :, 0:127], gy[:, :, 0:127], AF.Tanh, scale=0.7978845608028654), outs=[gy], ins=[gy])
            cx.op(dve, lambda: nc.vector.tensor_scalar(gy[:, :, 0:127], gy[:, :, 0:127], 0.5, 0.5, ALU.mult, ALU.add), outs=[gy], ins=[gy])
            cx.op(dve, lambda: nc.vector.tensor_tensor(Gt[:, :, 0:127], gy[:, :, 0:127], gx[:, :, 0:127], ALU.mult), outs=[Gt], ins=[gy, gx])

        def setup_gen(g, S):
            qr, ksr, kwr, kcT, vs_aug, vw_aug, vc_aug = (S[k_] for k_ in ('qr', 'ksr', 'kwr', 'kcT', 'vs_aug', 'vw_aug', 'vc_aug'))
            for h in range(4):
                load_rope(C_Q + (4 * g + h) * 64, qr, qr[:, h, :])
                yield
            load_rope(C_KS + g * 64, ksr, ksr[:, :])
            yield
            load_rope(C_KW + g * 64, kwr, kwr[:, :])
            yield
            load_rope(C_KC + g * 64, kcr, kcr[:, :])
            yield
            v_aug_build(C_VS + g * 64, vs_aug)
            yield
            v_aug_build(C_VW + g * 64, vw_aug)
            yield
            cx.op(pool, lambda: nc.gpsimd.memset(K2[:, :], 0.0), outs=[K2])
            cx.op(dve, lambda: nc.vector.tensor_copy(K2[0:64, :], kcr[:, :]), outs=[K2], ins=[kcr])
            cx.dma(sp, K2[64:128, 0:T - 1], kcr[:, 1:T], outs=[K2], ins=[kcr])
            compress(0, g)
            yield
            pb = misc()
            for ch in range(2):
                cx.op(pe, lambda ch=ch, pb=pb: nc.tensor.matmul(pb[0:64, 0:127], w2s[:, ch, :], Gt[:, ch, 0:127], start=(ch == 0), stop=(ch == 1)),
                      outs=[pb], ins=[w2s, Gt])
            cx.op(act, lambda pb=pb: nc.scalar.copy(kcT[:, 0:127], pb[0:64, 0:127]), outs=[kcT], ins=[pb])
            cx.op(pool, lambda: nc.gpsimd.memset(K2[:, :], 0.0), outs=[K2])
            cx.dma(sp, K2[0:64, :], d_uT[C_VC + g * 64:C_VC + (g + 1) * 64, :], outs=[K2])
            cx.dma(sp, K2[64:128, 0:T - 1], d_uT[C_VC + g * 64:C_VC + (g + 1) * 64, 1:T], outs=[K2])
            compress(1, g)
            yield
            pb = misc()
            for ch in range(2):
                cx.op(pe, lambda ch=ch, pb=pb: nc.tensor.matmul(pb[0:127, 0:64], Gt[:, ch, 0:127], w2s[:, ch, :], start=(ch == 0), stop=(ch == 1)),
                      outs=[pb], ins=[w2s, Gt])
            cx.op(pool, lambda: nc.gpsimd.memset(vc_aug[:, :], 0.0), outs=[vc_aug])
            cx.op(act, lambda pb=pb: nc.scalar.copy(vc_aug[0:127, 0:64], pb[0:127, 0:64]), outs=[vc_aug], ins=[pb])
            cx.op(pool, lambda: nc.gpsimd.memset(vc_aug[0:127, 64:65], 1.0), outs=[vc_aug])
            cx.op(dve, lambda: nc.vector.tensor_copy(vc_aug[0:127, 65:97], nsac[0:127, ncs("ov")]), outs=[vc_aug], ins=[nsac])

            yield

        def attention(g, S, side):
            qr, ksr, kwr, kcT, vs_aug, vw_aug, vc_aug = (S[k_] for k_ in ('qr', 'ksr', 'kwr', 'kcT', 'vs_aug', 'vw_aug', 'vc_aug'))
            ne = [0]

            pend = []
            ST_BANKS = [0, 1, 2, 7]

            def flush(keep=0):
                while len(pend) > keep:
                    pend.pop(0)()

            def attn_tile(klhsT, tq, mask_ap, vrhs, acc, first, kparts=128):
                stp = psum[ST_BANKS[ne[0] % 4]]
                E = Eb[ne[0] % 6]
                ne[0] += 1
                cx.op(pe, lambda: nc.tensor.matmul(stp[0:kparts, :], klhsT, qr[:, :, tq], start=True, stop=True), outs=[stp], ins=[qr, ksr, kwr, kcT])
                cx.op(act, lambda: nc.scalar.activation(E[0:kparts, :, :], stp[0:kparts, :].rearrange("p (h t) -> p h t", h=4), AF.Exp, scale=SCALE),
                      outs=[E], ins=[stp])
                if mask_ap is not None:
                    cx.op(dve, lambda: nc.vector.tensor_tensor(E[0:kparts, :, :], E[0:kparts, :, :], mask_ap, ALU.mult), outs=[E], ins=[E, mk4[0], mk4[1], cst, nsac])

                def back():
                    W = vrhs.shape[-1]
                    for h in range(4):
                        cx.op(pe, lambda h=h: nc.tensor.matmul(acc[:, h * W:(h + 1) * W], E[0:kparts, h, :], vrhs,
                                                               start=(first and h == 0), stop=True, skip_group_check=True),
                              outs=[acc], ins=[E, vs_aug, vw_aug, vc_aug])
                pend.append(back)
                flush(keep=3)

            def bc4(ap2d, kparts=128):
                return ap2d.unsqueeze(1).broadcast_to([kparts, 4, 128])

            def pre(i, par):
                tq = slice(i * 128, (i + 1) * 128)
                Rc_, rd_, selT_ = Rcs[par], rds[par], selTs[par]
                accc = psum[5]
                attn_tile(kcT[:, 0:127], tq, bc4(nsac[0:127, NSAC["cmaskT"][0] + i * 128:NSAC["cmaskT"][0] + (i + 1) * 128], 127),
                          vc_aug[0:127, :], accc, True, kparts=127)
                flush()
                cx.op(act, lambda: nc.scalar.copy(Rc_[:, :, :], accc[:, 0:388].rearrange("p (h c) -> p h c", h=4)), outs=[Rc_], ins=[accc])
                cx.op(dve, lambda: nc.vector.tensor_scalar(rd_[:, 0:4], Rc_[:, :, 64], 1e-30, None, ALU.max), outs=[rd_], ins=[Rc_])
                cx.op(dve, lambda: nc.vector.reciprocal(rd_[:, 0:4], rd_[:, 0:4]), outs=[rd_], ins=[rd_])
                for h in range(4):
                    if h == 0:
                        cx.op(dve, lambda h=h: nc.vector.tensor_scalar(impt[:, :], Rc_[:, h, 65:97], rd_[:, h:h + 1], None, ALU.mult), outs=[impt], ins=[Rc_, rd_])
                    else:
                        cx.op(dve, lambda h=h: nc.vector.scalar_tensor_tensor(impt[:, :], Rc_[:, h, 65:97], rd_[:, h:h + 1], impt[:, :], ALU.mult, ALU.add),
                              outs=[impt], ins=[Rc_, rd_, impt])
                mc0, ma0 = NSAC["Mc"][0] + i * 32, NSAC["Ma"][0] + i * 32
                cx.op(dve, lambda: nc.vector.tensor_tensor(impf[:, :], impt[:, :], nsac[:, mc0:mc0 + 32], ALU.mult), outs=[impf], ins=[impt, nsac])
                cx.op(dve, lambda: nc.vector.tensor_tensor(impf[:, :], impf[:, :], nsac[:, ma0:ma0 + 32], ALU.add), outs=[impf], ins=[impf, nsac])
                cx.op(dve, lambda: nc.vector.max(top8[:, :], impf[:, :]), outs=[top8], ins=[impf])
                cx.op(dve, lambda: nc.vector.tensor_scalar(selm[:, :], impf[:, :], top8[:, 7:8], None, ALU.is_ge), outs=[selm], ins=[impf, top8])
                pb = misc()
                cx.op(pe, lambda pb=pb: nc.tensor.transpose(pb[0:32, 0:128], selm[:, :], ident()), outs=[pb], ins=[selm, cst])
                cx.op(act, lambda pb=pb: nc.scalar.copy(selT_[:, :], pb[0:32, 0:128]), outs=[selT_], ins=[pb])

            nmk = [0]

            def main(i, par):
                tq = slice(i * 128, (i + 1) * 128)
                Rc_, rd_, selT_ = Rcs[par], rds[par], selTs[par]
                accw = psum[4]
                k0 = max(0, i - 4)
                for kt in range(k0, i + 1):
                    if kt == i:
                        m_ap = bc4(mUI)
                    elif kt == i - 4:
                        m_ap = bc4(mSL)
                    else:
                        m_ap = None
                    attn_tile(kwr[:, kt * 128:(kt + 1) * 128], tq, m_ap, vw_aug[:, kt, :], accw, kt == k0)
                accs = psum[3]
                for k4 in range(0, i + 1, 4):
                    kts = list(range(k4, min(k4 + 4, i + 1)))
                    mp = psum[6]
                    mkb = mk4[nmk[0] % 2]
                    nmk[0] += 1
                    for j_, kt in enumerate(kts):
                        cx.op(pe, lambda kt=kt, j_=j_: nc.tensor.matmul(mp[:, j_ * 128:(j_ + 1) * 128], expb[:, kt, :], selT_[:, :], start=True, stop=True),
                              outs=[mp], ins=[expb, selT_])
                    n_ = len(kts)
                    cx.op(act, lambda mkb=mkb, n_=n_: nc.scalar.copy(mkb[:, 0:n_, :], mp[:, 0:n_ * 128].rearrange("p (k s) -> p k s", s=128)), outs=[mkb], ins=[mp])
                    if kts[-1] == i:
                        cx.op(pool, lambda mkb=mkb, n_=n_: nc.gpsimd.tensor_tensor(mkb[:, n_ - 1, :], mkb[:, n_ - 1, :], mUIb[:, :], ALU.mult), outs=[mkb], ins=[mkb, mUIb])
                    for j_, kt in enumerate(kts):
                        attn_tile(ksr[:, kt * 128:(kt + 1) * 128], tq, bc4(mkb[:, j_, :]), vs_aug[:, kt, :], accs, kt == 0)
                flush()
                cx.op(dve, lambda: nc.vector.tensor_scalar(rd_[:, 4:8], accs[:, 0:260].rearrange("p (h c) -> p h c", h=4)[:, :, 64], 1e-30, None, ALU.max),
                      outs=[rd_], ins=[accs])
                cx.op(dve, lambda: nc.vector.tensor_scalar(rd_[:, 8:12], accw[:, 0:260].rearrange("p (h c) -> p h c", h=4)[:, :, 64], 1e-30, None, ALU.max),
                      outs=[rd_], ins=[accw])
                cx.op(dve, lambda: nc.vector.reciprocal(rd_[:, 4:12], rd_[:, 4:12]), outs=[rd_], ins=[rd_])
                gview = gsb[:, i, g * 12:(g + 1) * 12].rearrange("p (h b) -> p b h", b=3)
                cx.op(dve, lambda: nc.vector.tensor_tensor(cf[:, :].rearrange("p (b h) -> p b h", b=3), rd_[:, :].rearrange("p (b h) -> p b h", b=3), gview, ALU.mult),
                      outs=[cf], ins=[rd_, gsb])
                for h in range(4):
                    cx.op(dve, lambda h=h: nc.vector.tensor_scalar(otok[:, h, :], Rc_[:, h, 0:64], cf[:, h:h + 1], None, ALU.mult), outs=[otok], ins=[Rc_, cf])
                    cx.op(dve, lambda h=h: nc.vector.scalar_tensor_tensor(otok[:, h, :], accs[:, h * 65:h * 65 + 64], cf[:, 4 + h:5 + h], otok[:, h, :], ALU.mult, ALU.add),
                          outs=[otok], ins=[accs, cf, otok])
                    cx.op(dve, lambda h=h: nc.vector.scalar_tensor_tensor(otok[:, h, :], accw[:, h * 65:h * 65 + 64], cf[:, 8 + h:9 + h], otok[:, h, :], ALU.mult, ALU.add),
                          outs=[otok], ins=[accw, cf, otok])
                for hp in range(2):
                    pb = misc()
                    cx.op(pe, lambda pb=pb, hp=hp: nc.tensor.transpose(pb[:, 0:128], otok[:, 2 * hp:2 * hp + 2, :], ident()), outs=[pb], ins=[otok, cst])
                    cx.op(act, lambda pb=pb, hp=hp: nc.scalar.copy(ofm[:, hp, tq], pb[:, 0:128]), outs=[ofm], ins=[pb])

            ntile = P.debug.get("nsa_tiles") or 16
            pre(0, 0)
            for i in range(ntile):
                if i + 1 < ntile:
                    pre(i + 1, (i + 1) % 2)
                main(i, i % 2)
                side = drain_n(side, 1)
            for hp in range(2):
                r0 = 1024 + g * 256 + hp * 128
                cx.dma(sp, P.d_o[r0:r0 + 128, :], ofm[:, hp, :], ins=[ofm])
            return side

        def drain_n(gen, n=None):
            if gen is None:
                return None
            k = 0
            while n is None or k < n:
                try:
                    next(gen)
                except StopIteration:
                    return None
                k += 1
            return gen

        ngroups = P.debug.get("nsa_groups") or 4
        drain_n(setup_gen(0, sets[0]))
        for g in range(ngroups):
            side = setup_gen(g + 1, sets[(g + 1) % 2]) if g + 1 < ngroups else None
            side = attention(g, sets[g % 2], side)
            drain_n(side)


def emit_mixers(P, cx, li, psum, vec, cst, mod, sb):
    if not P.debug.get("skip_rwkv"):
        emit_rwkv(P, cx, li, psum, vec, cst, sb)
        cx.barrier()
    if not P.debug.get("skip_nsa"):
        emit_nsa(P, cx, li, psum, vec, cst, sb)
        cx.barrier()


def build(n_layers=L, debug=None):
    P = Prog(n_layers, debug)
    nc = P.nc
    with ExitStack() as top:
        cx = Ctx(nc, top)
        _uid = [0]

        def sb(name, shape, dtype=F32, st=top):
            _uid[0] += 1
            return Buf(st.enter_context(nc.sbuf_tensor(f"{name}_{_uid[0]}", list(shape), dtype)), name)
        psum = [Buf(top.enter_context(nc.psum_tensor(f"ps{i}", [128, 512], F32)), f"ps{i}", excl=True) for i in range(8)]
        pe, act, dve, pool, sp = cx.pe, cx.act, cx.dve, cx.pool, cx.sp

        vec = sb("vec", [128, NV])
        cond = sb("cond", [128, NKC])
        mod = sb("mod", [128, 96])
        modraw = sb("modraw", [128, 96])
        modraw0 = sb("modraw0", [128, 64])
        P.modraw0 = modraw0
        cond_bf = sb("cond_bf", [128, NKC], BF16)
        cst = sb("cst", [128, NCONST])
        cx.dma(sp, cst[:, :], P.consts[:, :], outs=[cst])
        cx.dma(sp, cond[:, :], P.cvec[:, :], outs=[cond])
        cx.op(act, lambda: nc.scalar.activation(cond[:, :], cond[:, :], AF.Silu), outs=[cond], ins=[cond])
        cx.op(act, lambda: nc.scalar.copy(cond_bf[:, :], cond[:, :]), outs=[cond_bf], ins=[cond])
        P.cond_bf = cond_bf
        P.modraw = modraw
        ada_done = [False]

        d_x = P.d_x
        x_src = Buf(P.xT_in, "xT_in")

        def urows(r0, n):
            return P.rows(P.d_u_rows, P.d_uT, r0, n)

        for li in range(n_layers):
            cx.dma(sp, vec[:, :], P.vecs[li], outs=[vec])
            with ExitStack() as ph:
                mps = psum[0]
                split0 = (li == 0 and n_layers > 1 and not P.debug)
                P.split0 = split0
                if not ada_done[0]:
                    wa = [sb(f"wa{i}", [128, NKC, 256], F32, ph) for i in range(2)]
                    for g in range(16 if split0 else 48):
                        wb = wa[g % 2]
                        cx.dma(sp, wb[:, :, :], P.w_ada[li][:, g * 256:(g + 1) * 256].rearrange("(kc p) n -> p kc n", p=128), outs=[wb])
                        for j2 in range(2):
                            j = g * 2 + j2
                            for kc in range(NKC):
                                cx.op(pe, lambda kc=kc, j=j, j2=j2, wb=wb: nc.tensor.matmul(
                                    mps[:, j:j + 1], wb[:, kc, j2 * 128:(j2 + 1) * 128], cond[:, kc:kc + 1],
                                    start=(kc == 0), stop=(kc == NKC - 1)), outs=[mps], ins=[wb, cond])
                o = VOFF["b_ada"]
                if split0:
                    cx.op(dve, lambda: nc.vector.tensor_tensor(mod[:, 0:32], mps[:, 0:32], vec[:, o:o + 32], ALU.add),
                          outs=[mod], ins=[mps, vec])
                elif not ada_done[0]:
                    cx.op(dve, lambda: nc.vector.tensor_tensor(mod[:, :], mps[:, 0:96], vec[:, o:o + 96], ALU.add),
                          outs=[mod], ins=[mps, vec])
                else:
                    cx.op(dve, lambda: nc.vector.tensor_tensor(mod[:, :], modraw[:, :], vec[:, o:o + 96], ALU.add),
                          outs=[mod], ins=[modraw, vec])
                for c0 in ((16,) if split0 else (16, 32, 64, 80)):
                    cx.op(dve, lambda c0=c0: nc.vector.tensor_scalar(mod[:, c0:c0 + 16], mod[:, c0:c0 + 16], 1.0, None, ALU.add),
                          outs=[mod], ins=[mod])
            cx.barrier()
            if "mod" in P.dbg_out and li == P.debug.get("layer", 0):
                cx.dma(sp, P.dbg_out["mod"][:, :], mod[:, :], ins=[mod])

            src = x_src if li == 0 else d_x
            with ExitStack() as ph:
                h = sb("h", [128, NKC, T], BF16, ph)
                xt = [sb(f"xt{i}", [128, NKC, 512], F32, ph) for i in range(1)]
                wbuf = [sb(f"wb{i}", [128, NKC, 512], BF16, ph) for i in range(2)]
                stage = [sb(f"stg{i}", [128, T], F32, ph) for i in range(2)]
                for tt in range(4):
                    xb = xt[0]
                    cx.dma(sp, xb[:, :, :], src.t[:, tt * 512:(tt + 1) * 512].rearrange("(kc p) t -> p kc t", p=128),
                           outs=[xb], ins=[src])
                    for kc in range(NKC):
                        if kc % 2 == 0:
                            cx.op(dve, lambda kc=kc, xb=xb, tt=tt: nc.vector.tensor_scalar(
                                h[:, kc, tt * 512:(tt + 1) * 512], xb[:, kc, :], mod[:, 16 + kc:17 + kc], mod[:, kc:kc + 1],
                                ALU.mult, ALU.add), outs=[h], ins=[xb, mod])
                        else:
                            cx.op(act, lambda kc=kc, xb=xb, tt=tt: nc.scalar.activation(
                                h[:, kc, tt * 512:(tt + 1) * 512], xb[:, kc, :], AF.Identity,
                                bias=mod[:, kc:kc + 1], scale=mod[:, 16 + kc:17 + kc]), outs=[h], ins=[xb, mod])
                groups = []
                c = 0
                while c < IN_COLS:
                    n = min(512, IN_COLS - c)
                    if c == C_XW:
                        groups.append((c, 448, [(0, 96), (96, 96), (192, 128), (320, 128)]))
                        c += 448
                        continue
                    chunks = [(m, min(128, n - m)) for m in range(0, n, 128)]
                    groups.append((c, n, chunks))
                    c += n
                nstage = 0
                npsum = 0
                for gi, (c0, n, chunks) in enumerate(groups):
                    wb = wbuf[gi % 2]
                    cx.dma(pool, wb[:, :, 0:n], P.w_in[li][:, c0:c0 + n].rearrange("(kc p) n -> p kc n", p=128), outs=[wb])
                    for (m0, msz) in chunks:
                        st_ = stage[nstage % 2]
                        nstage += 1
                        for tt in range(4):
                            pb = psum[npsum % 4]
                            npsum += 1
                            for kc in range(NKC):
                                cx.op(pe, lambda kc=kc, pb=pb, wb=wb, m0=m0, msz=msz, tt=tt: nc.tensor.matmul(
                                    pb[0:msz, :], wb[:, kc, m0:m0 + msz], h[:, kc, tt * 512:(tt + 1) * 512],
                                    start=(kc == 0), stop=(kc == NKC - 1)), outs=[pb], ins=[wb, h])
                            if npsum % 2 == 0:
                                cx.op(act, lambda pb=pb, st_=st_, msz=msz, tt=tt: nc.scalar.copy(
                                    st_[0:msz, tt * 512:(tt + 1) * 512], pb[0:msz, :]), outs=[st_], ins=[pb])
                            else:
                                cx.op(dve, lambda pb=pb, st_=st_, msz=msz, tt=tt: nc.vector.tensor_copy(
                                    st_[0:msz, tt * 512:(tt + 1) * 512], pb[0:msz, :]), outs=[st_], ins=[pb])
                        ur = urows(c0 + m0, msz)
                        cx.dma(sp, ur[:, :], st_[0:msz, :], outs=[ur], ins=[st_])
            cx.barrier()
            if "uT" in P.dbg_out and li == P.debug.get("layer", 0):
                for r0 in range(0, 6144, 128):
                    n = min(128, IN_COLS - r0)
                    if n <= 0:
                        break
                    cx.dma(sp, P.dbg_out["uT"][r0:r0 + n, :], P.d_uT[r0:r0 + n, :])
                cx.barrier()
            if P.debug.get("stop") == "A":
                break

            if "feed_o" in P.debug and li == 0:
                for r0 in range(0, D, 128):
                    cx.dma(sp, P.d_o[r0:r0 + 128, :], P.dbg_in["o"][r0:r0 + 128, :])
                cx.barrier()
            else:
                P.ada_next = (li + 1) if (li + 1 < n_layers and not P.debug) else None
                emit_mixers(P, cx, li, psum, vec, cst, mod, sb)
                ada_done[0] = P.ada_next is not None
                if P.split0:
                    ob = VOFF["b_ada"]
                    cx.op(dve, lambda: nc.vector.tensor_tensor(mod[:, 32:96], modraw0[:, 0:64], vec[:, ob + 32:ob + 96], ALU.add),
                          outs=[mod], ins=[modraw0, vec])
                    for c0 in (32, 64, 80):
                        cx.op(dve, lambda c0=c0: nc.vector.tensor_scalar(mod[:, c0:c0 + 16], mod[:, c0:c0 + 16], 1.0, None, ALU.add),
                              outs=[mod], ins=[mod])
                cx.barrier()
            if "oT" in P.dbg_out and li == P.debug.get("layer", 0):
                for r0 in range(0, D, 128):
                    cx.dma(sp, P.dbg_out["oT"][r0:r0 + 128, :], P.d_o[r0:r0 + 128, :])
                cx.barrier()
            if P.debug.get("stop") == "B":
                break

            with ExitStack() as ph:
                o_sb = sb("o_sb", [128, NKC, T], BF16, ph)
                wbuf = [sb(f"wb{i}", [128, NKC, 512], BF16, ph) for i in range(2)]
                stage = [sb(f"stg{i}", [128, T], F32, ph) for i in range(2)]
                for q4 in range(4):
                    cx.dma(pool, o_sb[:, q4 * 4:(q4 + 1) * 4, :],
                           P.d_o[q4 * 512:(q4 + 1) * 512, :].rearrange("(kc p) t -> p kc t", p=128), outs=[o_sb])
                nstage = 0
                npsum = 0
                for gi in range(4):
                    wb = wbuf[gi % 2]
                    cx.dma(pool, wb[:, :, :], P.w_out[li][:, gi * 512:(gi + 1) * 512].rearrange("(kc p) n -> p kc n", p=128), outs=[wb])
                    for m in range(4):
                        st_ = stage[nstage % 2]
                        nstage += 1
                        for tt in range(4):
                            pb = psum[npsum % 4]
                            npsum += 1
                            for kc in range(NKC):
                                cx.op(pe, lambda kc=kc, pb=pb, wb=wb, m=m, tt=tt: nc.tensor.matmul(
                                    pb[:, :], wb[:, kc, m * 128:(m + 1) * 128], o_sb[:, kc, tt * 512:(tt + 1) * 512],
                                    start=(kc == 0), stop=(kc == NKC - 1)), outs=[pb], ins=[wb, o_sb])
                            if npsum % 2 == 0:
                                cx.op(act, lambda pb=pb, st_=st_, tt=tt: nc.scalar.copy(
                                    st_[:, tt * 512:(tt + 1) * 512], pb[:, :]), outs=[st_], ins=[pb])
                            else:
                                cx.op(dve, lambda pb=pb, st_=st_, tt=tt: nc.vector.tensor_copy(
                                    st_[:, tt * 512:(tt + 1) * 512], pb[:, :]), outs=[st_], ins=[pb])
                        cx.dma(sp, P.d_y[(gi * 4 + m) * 128:(gi * 4 + m + 1) * 128, :], st_[:, :], ins=[st_])
            cx.barrier()

            dst = P.outT if li == n_layers - 1 else P.d_x.t
            with ExitStack() as ph:
                A = sb("tA", [128, NKC, 512], F32, ph)
                Bt = sb("tB", [128, NKC, 512], F32, ph)
                h1 = sb("h1", [128, NKC, 512], BF16, ph)
                hid = sb("hid", [128, 64, 512], BF16, ph)
                wbuf = [sb(f"wb{i}", [128, NKC, 512], BF16, ph) for i in range(2)]
                mean = sb("mean", [128, 512], F32, ph)
                var = sb("var", [128, 512], F32, ph)
                tmp = sb("tmpv", [128, 512], F32, ph)
                rstd = sb("rstd", [128, 512], F32, ph)
                sq = [sb(f"sq{i}", [128, 512], F32, ph) for i in range(2)]
                rl = [sb(f"rl{i}", [128, 512], F32, ph) for i in range(2)]
                zr = [sb(f"zr{i}", [128, 512], F32, ph) for i in range(2)]
                ones_r = sb("ones_r", [128, 128], F32, ph)
                cx.op(dve, lambda: nc.vector.tensor_copy(R_(ones_r[:, :]), cst[:, CONST_OFF["ones"][0]:CONST_OFF["ones"][0] + 128]), outs=[ones_r], ins=[cst])
                Ak = [Buf(A.t[:, kc, :], f"A{kc}") for kc in range(NKC)]
                Bk = [Buf(Bt.t[:, kc, :], f"B{kc}") for kc in range(NKC)]
                h1k = [Buf(h1.t[:, kc, :], f"h1{kc}") for kc in range(NKC)]
                hidk = [Buf(hid.t[:, kc, :], f"hid{kc}") for kc in range(64)]
                ones_ap = cst[:, CONST_OFF["ones"][0]:CONST_OFF["ones"][0] + 128]
                nw = [0]

                def layer_norm(gcol, goff, boff, sc_col, sh_col, make_h):
                    s1, s2 = psum[0], psum[1]
                    for kc in range(NKC):
                        cx.op(dve, lambda kc=kc: nc.vector.scalar_tensor_tensor(
                            Bk[kc][:, :], Ak[kc][:, :], mod[:, gcol + kc:gcol + kc + 1], Bk[kc][:, :], ALU.mult, ALU.add),
                            outs=[Bk[kc]], ins=[Ak[kc], Bk[kc], mod])
                        sqb = sq[kc % 2]
                        zrb = zr[kc % 2]
                        cx.op(act, lambda kc=kc, zrb=zrb: nc.scalar.copy(R_(zrb[:, :]), Bk[kc][:, :]), outs=[zrb], ins=[Bk[kc]])
                        cx.op(act, lambda kc=kc, sqb=sqb: nc.scalar.activation(R_(sqb[:, :]), Bk[kc][:, :], AF.Square),
                              outs=[sqb], ins=[Bk[kc]])
                        cx.op(pe, lambda kc=kc, zrb=zrb: nc.tensor.matmul(s1[:, :], R_(ones_r[:, :]), R_(zrb[:, :]), start=(kc == 0), stop=(kc == NKC - 1)),
                              outs=[s1], ins=[ones_r, zrb])
                        cx.op(pe, lambda kc=kc, sqb=sqb: nc.tensor.matmul(s2[:, :], R_(ones_r[:, :]), R_(sqb[:, :]), start=(kc == 0), stop=(kc == NKC - 1)),
                              outs=[s2], ins=[ones_r, sqb])
                    cx.op(act, lambda: nc.scalar.activation(mean[:, :], s1[:, :], AF.Copy, scale=1.0 / D), outs=[mean], ins=[s1])
                    cx.op(act, lambda: nc.scalar.activation(var[:, :], s2[:, :], AF.Copy, scale=1.0 / D), outs=[var], ins=[s2])
                    cx.op(dve, lambda: nc.vector.tensor_tensor(tmp[:, :], mean[:, :], mean[:, :], ALU.mult), outs=[tmp], ins=[mean])
                    cx.op(dve, lambda: nc.vector.tensor_tensor(var[:, :], var[:, :], tmp[:, :], ALU.subtract), outs=[var], ins=[var, tmp])
                    cx.op(dve, lambda: nc.vector.tensor_scalar(var[:, :], var[:, :], LN_EPS, None, ALU.add), outs=[var], ins=[var])
                    cx.op(act, lambda: nc.scalar.activation(tmp[:, :], var[:, :], AF.Sqrt), outs=[tmp], ins=[var])
                    cx.op(dve, lambda: nc.vector.reciprocal(rstd[:, :], tmp[:, :]), outs=[rstd], ins=[tmp])
                    for kc in range(NKC):
                        cx.op(dve, lambda kc=kc: nc.vector.tensor_tensor(Bk[kc][:, :], Bk[kc][:, :], mean[:, :], ALU.subtract),
                              outs=[Bk[kc]], ins=[Bk[kc], mean])
                        cx.op(dve, lambda kc=kc: nc.vector.tensor_tensor(Bk[kc][:, :], Bk[kc][:, :], rstd[:, :], ALU.mult),
                              outs=[Bk[kc]], ins=[Bk[kc], rstd])
                        cx.op(act, lambda kc=kc: nc.scalar.activation(
                            Bk[kc][:, :], Bk[kc][:, :], AF.Identity, bias=vec[:, boff + kc:boff + kc + 1], scale=vec[:, goff + kc:goff + kc + 1]),
                            outs=[Bk[kc]], ins=[Bk[kc], vec])
                        if make_h:
                            cx.op(dve, lambda kc=kc: nc.vector.tensor_scalar(
                                h1k[kc][:, :], Bk[kc][:, :], mod[:, sc_col + kc:sc_col + kc + 1], mod[:, sh_col + kc:sh_col + kc + 1],
                                ALU.mult, ALU.add), outs=[h1k[kc]], ins=[Bk[kc], mod])

                for tt in range(4):
                    tsl = slice(tt * 512, (tt + 1) * 512)
                    cx.dma(sp, A[:, :, :], P.d_y[:, tsl].rearrange("(kc p) t -> p kc t", p=128), outs=Ak)
                    cx.dma(sp, Bt[:, :, :], src.t[:, tsl].rearrange("(kc p) t -> p kc t", p=128), outs=Bk, ins=[src])
                    for kc in range(NKC):
                        cx.op(act, lambda kc=kc: nc.scalar.activation(Bk[kc][:, :], Bk[kc][:, :], AF.Copy, scale=DN_ALPHA),
                              outs=[Bk[kc]], ins=[Bk[kc]])
                    layer_norm(32, VOFF["ln1_g"], VOFF["ln1_b"], 64, 48, True)
                    if "x1T" in P.dbg_out and li == P.debug.get("layer", 0):
                        cx.dma(sp, P.dbg_out["x1T"][:, tsl].rearrange("(kc p) t -> p kc t", p=128), Bt[:, :, :], ins=Bk)
                    for g in range(16):
                        wb = wbuf[nw[0] % 2]
                        nw[0] += 1
                        cx.dma(pool, wb[:, :, :], P.mlp_w1[li][:, g * 512:(g + 1) * 512].rearrange("(kc p) n -> p kc n", p=128), outs=[wb])
                        for m in range(4):
                            pb = psum[2 + (m % 2)]
                            for kc in range(NKC):
                                cx.op(pe, lambda kc=kc, pb=pb, wb=wb, m=m: nc.tensor.matmul(
                                    pb[:, :], wb[:, kc, m * 128:(m + 1) * 128], h1k[kc][:, :],
                                    start=(kc == 0), stop=(kc == NKC - 1)), outs=[pb], ins=[wb, h1k[kc]])
                            rb = rl[m % 2]
                            cx.op(act, lambda pb=pb, rb=rb: nc.scalar.activation(rb[:, :], pb[:, :], AF.Relu), outs=[rb], ins=[pb])
                            hk = hidk[g * 4 + m]
                            cx.op(dve, lambda rb=rb, hk=hk: nc.vector.tensor_tensor(hk[:, :], rb[:, :], rb[:, :], ALU.mult),
                                  outs=[hk], ins=[rb])
                    for cg in range(4):
                        for kq in range(4):
                            wb = wbuf[nw[0] % 2]
                            nw[0] += 1
                            cx.dma(pool, wb[:, :, :], P.mlp_w2[li][kq * 2048:(kq + 1) * 2048, cg * 512:(cg + 1) * 512].rearrange(
                                "(kc p) n -> p kc n", p=128), outs=[wb])
                            for m in range(4):
                                pb = psum[4 + m]
                                for kc in range(NKC):
                                    cx.op(pe, lambda kc=kc, pb=pb, wb=wb, m=m, kq=kq: nc.tensor.matmul(
                                        pb[:, :], wb[:, kc, m * 128:(m + 1) * 128], hidk[kq * 16 + kc][:, :],
                                        start=(kq == 0 and kc == 0), stop=(kq == 3 and kc == NKC - 1)),
                                        outs=[pb], ins=[wb, hidk[kq * 16 + kc]])
                        for m in range(4):
                            pb = psum[4 + m]
                            ak = Ak[cg * 4 + m]
                            if m % 2 == 0:
                                cx.op(act, lambda pb=pb, ak=ak: nc.scalar.copy(ak[:, :], pb[:, :]), outs=[ak], ins=[pb])
                            else:
                                cx.op(dve, lambda pb=pb, ak=ak: nc.vector.tensor_copy(ak[:, :], pb[:, :]), outs=[ak], ins=[pb])
                    for kc in range(NKC):
                        cx.op(act, lambda kc=kc: nc.scalar.activation(Bk[kc][:, :], Bk[kc][:, :], AF.Copy, scale=DN_ALPHA),
                              outs=[Bk[kc]], ins=[Bk[kc]])
                    layer_norm(80, VOFF["ln2_g"], VOFF["ln2_b"], 0, 0, False)
                    cx.dma(sp, dst[:, tsl].rearrange("(kc p) t -> p kc t", p=128), Bt[:, :, :], outs=[d_x], ins=Bk)
            cx.barrier()

        cx.barrier()
        P.n_instr = cx.n_instr
    return P


def pack_vecs(inp):
    vecs = np.zeros((L, 128, NV), np.float32)

    def put(name, v, li):
        v = np.asarray(v, np.float32).reshape(-1)
        n = (len(v) + 127) // 128
        pad = np.zeros(n * 128, np.float32)
        pad[:len(v)] = v
        vecs[li, :, VOFF[name]:VOFF[name] + n] = pad.reshape(n, 128).T

    for li in range(L):
        put("b_ada", inp["b_ada"][li], li)
        mu = inp["rwkv_mu"][li]
        put("mu_r", mu[0:1024], li)
        put("mu_k", mu[1024:2048], li)
        put("mu_v", mu[2048:3072], li)
        put("mu_w", mu[3072:3168], li)
        put("mu_a", mu[3168:3264], li)
        put("mu_g", mu[3264:3520], li)
        put("w0", inp["rwkv_w0"][li], li)
        put("a0", inp["rwkv_a0"][li], li)
        put("k_k", inp["rwkv_k_k"][li], li)
        put("k_a", inp["rwkv_k_a"][li], li)
        put("r_k", inp["rwkv_r_k"][li], li)
        put("lnx_g", inp["rwkv_lnx_g"][li], li)
        put("lnx_b", inp["rwkv_lnx_b"][li], li)
        if li > 0:
            put("v0", inp["rwkv_v0"][li - 1], li)
        put("ln1_g", inp["ln1_g"][li], li)
        put("ln1_b", inp["ln1_b"][li], li)
        put("ln2_g", inp["ln2_g"][li], li)
        put("ln2_b", inp["ln2_b"][li], li)
    return vecs


def pack_vec64(inp):
    v64 = np.zeros((L, 64, NV64), np.float32)
    for li in range(L):
        mu = inp["rwkv_mu"][li]
        src = {"mu_r": mu[0:1024], "mu_k": mu[1024:2048], "mu_v": mu[2048:3072], "w0": inp["rwkv_w0"][li],
               "a0": inp["rwkv_a0"][li], "k_k": inp["rwkv_k_k"][li], "k_a": inp["rwkv_k_a"][li],
               "r_k": inp["rwkv_r_k"][li].reshape(-1), "lnx_g": inp["rwkv_lnx_g"][li], "lnx_b": inp["rwkv_lnx_b"][li]}
        if li > 0:
            src["v0"] = inp["rwkv_v0"][li - 1]
        for n, v in src.items():
            v64[li, :, V64[n]:V64[n] + 16] = np.asarray(v, np.float32).reshape(16, 64).T
    return v64


def rope_tables():
    half = 32
    inv = 10000.0 ** (-np.arange(half, dtype=np.float32) / half)
    ang = np.arange(T, dtype=np.float32)[None, :] * inv[:, None]
    cos = np.cos(ang).astype(np.float32)
    sin = np.sin(ang).astype(np.float32)
    cosT = np.concatenate([cos, cos, cos, cos], 0)
    sinT = np.concatenate([-sin, sin, -sin, sin], 0)
    return np.stack([cosT, sinT], 0).astype(np.float32)


def make_in_maps(inp, batches):
    f = lambda a: np.ascontiguousarray(np.asarray(a, np.float32))
    shared = {
        "vecs": pack_vecs(inp), "w_ada": f(inp["w_ada"]), "w_in": f(inp["w_in"]), "w_out": f(inp["w_out"]),
        "mlp_w1": f(inp["mlp_w1"]), "mlp_w2": f(inp["mlp_w2"]), "rwkv_w2": f(inp["rwkv_w2"]),
        "rwkv_a2": f(inp["rwkv_a2"]), "rwkv_g2": f(inp["rwkv_g2"]), "rwkv_v1": f(inp["rwkv_v1"]),
        "rwkv_v2": f(inp["rwkv_v2"]), "nsa_cmp_pos": f(inp["nsa_cmp_pos"]), "nsa_cmp_w1": f(inp["nsa_cmp_w1"]),
        "nsa_cmp_w2": f(inp["nsa_cmp_w2"]), "consts": _CONSTS, "rope": rope_tables(), "vec64": pack_vec64(inp), "nsac": _NSA_CONSTS,
        "cmp_pos2": np.ascontiguousarray(np.asarray(inp["nsa_cmp_pos"], np.float32).reshape(L, 2, 16, 2, 64).transpose(0, 1, 3, 4, 2).reshape(L, 2, 128, 16)),
    }
    maps = []
    for b in batches:
        m = dict(shared)
        m["xT"] = np.ascontiguousarray(np.asarray(inp["x"][b], np.float32).T)
        m["cvec"] = np.ascontiguousarray(np.asarray(inp["c"][b], np.float32).reshape(NKC, 128).T)
        maps.append(m)
    return maps


def kernel(**inputs):
    P = build()
    maps = make_in_maps(inputs, list(range(4)))
    res = run_bass_kernel_spmd(P.nc, maps, core_ids=list(range(4)))
    out = np.stack([np.ascontiguousarray(r["outT"].T) for r in res.results], 0)
    return out.astype(np.float32)
```

```python
import numpy as np
from contextlib import ExitStack
import concourse.bass as bass
import concourse.mybir as mybir
from concourse.bass_utils import run_bass_kernel_spmd

F32 = mybir.dt.float32
F32R = mybir.dt.float32r
R_ = lambda ap: ap.bitcast(F32R)
BF16 = mybir.dt.bfloat16
AF = mybir.ActivationFunctionType
ALU = mybir.AluOpType

D = 2048
T = 2048
L = 4
NKC = 16
DFF = 8192
IN_COLS = 6128
C_R, C_K, C_V, C_XW, C_XA, C_XG = 0, 1024, 2048, 3072, 3168, 3264
NB = 3520
C_Q, C_KC, C_VC, C_KS, C_VS, C_KW, C_VW, C_GATE = NB, NB + 1024, NB + 1280, NB + 1536, NB + 1792, NB + 2048, NB + 2304, NB + 2560
DN_ALPHA = (2 * L) ** 0.25
LN_EPS = 1e-5
GN_EPS = 64e-5

VOFF = {}
_cur = 0
for _n, _w in [("b_ada", 96), ("mu_r", 8), ("mu_k", 8), ("mu_v", 8), ("mu_w", 1), ("mu_a", 1), ("mu_g", 2),
               ("w0", 8), ("a0", 8), ("k_k", 8), ("k_a", 8), ("r_k", 8), ("lnx_g", 8), ("lnx_b", 8), ("v0", 8),
               ("ln1_g", 16), ("ln1_b", 16), ("ln2_g", 16), ("ln2_b", 16)]:
    VOFF[_n] = _cur
    _cur += _w
NV = _cur
V64 = {}
_cur = 0
for _n in ["mu_r", "mu_k", "mu_v", "w0", "a0", "k_k", "k_a", "r_k", "lnx_g", "lnx_b", "v0", "om_r", "om_k", "om_v", "om_ka"]:
    V64[_n] = _cur
    _cur += 16
NV64 = _cur


class Buf:
    def __init__(self, t, name="", excl=False):
        self.t = t
        self.name = name
        self.w = None
        self.r = []
        self.excl = excl

    def __getitem__(self, k):
        return self.t[k]


class Eng:
    def __init__(self, name, e, sem):
        self.name = name
        self.e = e
        self.sem = sem
        self.count = 0
        self.waited = {}


class Slot:
    def __init__(self, sem, key):
        self.sem = sem
        self.key = key
        self.val = 0


class Ctx:
    def __init__(self, nc, stack):
        self.nc = nc
        mk = lambda n: stack.enter_context(nc.semaphore(n))
        self.pe = Eng("pe", nc.tensor, mk("s_pe"))
        self.act = Eng("act", nc.scalar, mk("s_act"))
        self.dve = Eng("dve", nc.vector, mk("s_dve"))
        self.pool = Eng("pool", nc.gpsimd, mk("s_pool"))
        self.sp = Eng("sp", nc.sync, None)
        self.engs = [self.pe, self.act, self.dve, self.pool, self.sp]
        self.slots = {
            "sp": [Slot(mk(f"d_sp{i}"), f"d_sp{i}") for i in range(24)],
            "pool": [Slot(mk(f"d_pl{i}"), f"d_pl{i}") for i in range(24)],
            "act": [Slot(mk(f"d_ac{i}"), f"d_ac{i}") for i in range(8)],
        }
        self.slot_i = {"sp": 0, "pool": 0, "act": 0}
        self.n_instr = 0

    def _wait(self, eng, tok):
        if tok is None:
            return
        sem, val, key, _ = tok
        if eng.waited.get(key, 0) >= val:
            return
        eng.e.wait_ge(sem, val)
        eng.waited[key] = val
        self.n_instr += 1

    def _deps(self, eng, outs, ins):
        for b in ins:
            self._wait(eng, b.w)
            if b.excl:
                for r in b.r:
                    if r[3] is not eng:
                        self._wait(eng, r)
        for b in outs:
            if b.w is not None and b.w[3] is not eng:
                self._wait(eng, b.w)
            for r in b.r:
                if r[3] is not eng:
                    self._wait(eng, r)

    def _commit(self, tok, outs, ins):
        for b in outs:
            b.w = tok
            b.r = []
        for b in ins:
            if len(b.r) > 12:
                last = {}
                for r in b.r:
                    if r[2] not in last or last[r[2]][1] < r[1]:
                        last[r[2]] = r
                b.r = list(last.values())
            b.r.append(tok)

    def op(self, eng, fn, outs=(), ins=()):
        self._deps(eng, outs, ins)
        instr = fn()
        eng.count += 1
        instr.then_inc(eng.sem, 1)
        self.n_instr += 1
        tok = (eng.sem, eng.count, eng.name, eng)
        self._commit(tok, outs, ins)
        return tok

    def dma(self, q, out_ap, in_ap, outs=(), ins=(), **kw):
        self._deps(q, outs, ins)
        sl = self.slots[q.name]
        s = sl[self.slot_i[q.name] % len(sl)]
        self.slot_i[q.name] += 1
        if s.val > 0:
            self._wait(q, (s.sem, s.val, s.key, None))
        instr = q.e.dma_start(out=out_ap, in_=in_ap, **kw)
        s.val += 16
        instr.then_inc(s.sem, 16)
        self.n_instr += 1
        tok = (s.sem, s.val, s.key, None)
        self._commit(tok, outs, ins)
        return tok

    def all_tokens(self):
        toks = []
        for e in self.engs:
            if e.sem is not None and e.count > 0:
                toks.append((e.sem, e.count, e.name, e))
        for sl in self.slots.values():
            for s in sl:
                if s.val > 0:
                    toks.append((s.sem, s.val, s.key, None))
        return toks

    def barrier(self, engines=None):
        toks = self.all_tokens()
        for e in (engines or self.engs):
            for t in toks:
                if t[3] is not e:
                    self._wait(e, t)


class Prog:
    def __init__(self, n_layers=L, debug=None):
        self.n_layers = n_layers
        self.debug = debug or {}
        self.nc = nc = bass.Bass("TRN2", target_bir_lowering=False)
        dt = lambda name, shape, kind, dtype=F32: nc.dram_tensor(name, list(shape), dtype, kind=kind).ap()
        IN = "ExternalInput"
        self.xT_in = dt("xT", [D, T], IN)
        self.cvec = dt("cvec", [128, NKC], IN)
        self.vecs = dt("vecs", [L, 128, NV], IN)
        self.w_ada = dt("w_ada", [L, D, 6 * D], IN)
        self.w_in = dt("w_in", [L, D, IN_COLS], IN)
        self.w_out = dt("w_out", [L, D, D], IN)
        self.mlp_w1 = dt("mlp_w1", [L, D, DFF], IN)
        self.mlp_w2 = dt("mlp_w2", [L, DFF, D], IN)
        self.rw_w2 = dt("rwkv_w2", [L, 96, 1024], IN)
        self.rw_a2 = dt("rwkv_a2", [L, 96, 1024], IN)
        self.rw_g2 = dt("rwkv_g2", [L, 256, 1024], IN)
        self.rw_v1 = dt("rwkv_v1", [L - 1, 1024, 64], IN)
        self.rw_v2 = dt("rwkv_v2", [L - 1, 64, 1024], IN)
        self.cmp_pos = dt("nsa_cmp_pos", [L, 2, 32, 64], IN)
        self.cmp_w1 = dt("nsa_cmp_w1", [L, 2, 2048, 256], IN)
        self.cmp_w2 = dt("nsa_cmp_w2", [L, 2, 256, 64], IN)
        self.consts = dt("consts", [128, NCONST], IN)
        self.vec64 = dt("vec64", [L, 64, NV64], IN)
        self.nsac = dt("nsac", [128, NNSAC], IN)
        self.cmp_pos2 = dt("cmp_pos2", [L, 2, 128, 16], IN)
        self.rope = dt("rope", [2, 128, T], IN)
        self.outT = dt("outT", [D, T], "ExternalOutput")
        self.d_x = Buf(dt("d_x", [D, T], "Internal"), "d_x")
        self.d_u = [Buf(dt(f"d_u{i}", [128, T], "Internal")[:, :], f"d_u{i}") for i in range(0)]
        self.d_uT = dt("d_uT", [6144, T], "Internal")
        self.d_u_rows = {}
        self.d_o = dt("d_o", [D, T], "Internal")
        self.d_o_rows = {}
        self.d_y = dt("d_y", [D, T], "Internal")
        self.d_y_rows = {}
        self.d_vf = dt("d_vf", [1024, T], "Internal")
        self.d_lw = dt("d_lw", [1024, T], "Internal")
        self.d_alr = dt("d_alr", [1024, T], "Internal")
        self.d_g = dt("d_g", [1024, T], "Internal")
        self.d_v = dt("d_v", [1024, T], "Internal")
        self.d_ar0 = dt("d_ar0", [1024, T], "Internal")
        self.d_ar1 = dt("d_ar1", [1024, T], "Internal")
        self.d_bk0 = dt("d_bk0", [1024, T], "Internal")
        self.d_bk1 = dt("d_bk1", [1024, T], "Internal")
        self.d_bon = dt("d_bon", [1024, T], "Internal")
        self.d_gc = dt("d_gc", [1024, 16], "Internal")
        self.d_vf_rows = {}
        self.dbg_out = {}
        self.dbg_in = {}
        for name, shape in self.debug.get("ins", {}).items():
            self.dbg_in[name] = dt("dbgin_" + name, shape, IN)
        for name, shape in self.debug.get("outs", {}).items():
            self.dbg_out[name] = dt("dbg_" + name, shape, "ExternalOutput")

    def rows(self, table, base, r0, n):
        key = (r0, n)
        if key not in table:
            table[key] = Buf(base[r0:r0 + n, :], f"rows{r0}")
        return table[key]


NCONST = 0
CONST_OFF = {}


def _add_const(name, arr):
    global NCONST
    CONST_OFF[name] = (NCONST, arr.shape[1])
    NCONST += arr.shape[1]
    return arr


def build_consts():
    global NCONST, CONST_OFF
    NCONST = 0
    CONST_OFF = {}
    parts = []
    p = np.arange(128)[:, None]
    f = np.arange(128)[None, :]
    parts.append(_add_const("ident", (p == f).astype(np.float32)))
    parts.append(_add_const("ones", np.ones((128, 128), np.float32)))
    parts.append(_add_const("bd64", ((p // 64) == (f // 64)).astype(np.float32)))
    parts.append(_add_const("SL", (p > f).astype(np.float32)))
    parts.append(_add_const("SU", (p < f).astype(np.float32)))
    parts.append(_add_const("UI", (p <= f).astype(np.float32)))
    return np.concatenate(parts, axis=1)


_CONSTS = build_consts()

NSAC = {}


def build_nsa_consts():
    parts = []
    cur = [0]

    def add(name, arr):
        a = np.zeros((128, arr.shape[1]), np.float32)
        a[:arr.shape[0]] = arr
        NSAC[name] = (cur[0], arr.shape[1])
        cur[0] += arr.shape[1]
        parts.append(a)

    c = np.arange(127)[:, None]
    t = np.arange(T)[None, :]
    add("cmaskT", ((16 * c + 31) <= t).astype(np.float32))
    pos = np.arange(127)[:, None] * 16 + np.arange(32)[None]
    ov = ((pos // 64)[..., None] == np.arange(32)).sum(1) / 32.0
    add("ov", ov.astype(np.float32))
    tt = np.arange(T)
    blk = np.arange(32)[None]
    cur_b = (tt // 64)[:, None]
    forced = (blk == 0) | (blk == cur_b) | (blk == cur_b - 1)
    causal = blk * 64 <= tt[:, None]
    Mc = (causal & ~forced).astype(np.float32)
    Ma = np.where(forced, 1e4, np.where(causal, 0.0, -1.0)).astype(np.float32)
    add("Mc", Mc.reshape(16, 128, 32).transpose(1, 0, 2).reshape(128, 512))
    add("Ma", Ma.reshape(16, 128, 32).transpose(1, 0, 2).reshape(128, 512))
    j = np.arange(32)[:, None, None]
    kt = np.arange(16)[None, :, None]
    s_ = np.arange(128)[None, None, :]
    add("expand", (j == 2 * kt + s_ // 64).astype(np.float32).reshape(32, 2048))
    return np.concatenate(parts, axis=1)


_NSA_CONSTS = build_nsa_consts()
NNSAC = _NSA_CONSTS.shape[1]


AX = mybir.AxisListType


def cs_(name, n=128, rows=128):
    o = CONST_OFF[name][0]
    return slice(o, o + n)


def emit_rwkv(P, cx, li, psum, vec, cst, sb):
    nc = P.nc
    pe, act, dve, pool, sp = cx.pe, cx.act, cx.dve, cx.pool, cx.sp
    bi = [0]

    ada_next = getattr(P, "ada_next", None)
    NBANK = 7 if ada_next is not None else 8

    def bank():
        bi[0] += 1
        return psum[bi[0] % NBANK]

    d_uT = P.d_uT
    ident = lambda n=128, m=128: cst[0:n, CONST_OFF["ident"][0]:CONST_OFF["ident"][0] + m]
    bd64 = cst[:, CONST_OFF["bd64"][0]:CONST_OFF["bd64"][0] + 128]
    mSL = cst[:, cs_("SL")]
    mSU = cst[:, cs_("SU")]
    mUI = cst[:, cs_("UI")]
    NBK = 8
    CW = 0.6065306597126334

    def chunks(ap, n=8):
        return [Buf(ap[c * 128:(c + 1) * 128, :], f"dr{c}") for c in range(n)]
    D_lw, D_alr, D_g, D_v, D_vf = chunks(P.d_lw), chunks(P.d_alr), chunks(P.d_g), chunks(P.d_v), chunks(P.d_vf)
    D_AR0, D_AR1, D_BK0, D_BK1 = chunks(P.d_ar0), chunks(P.d_ar1), chunks(P.d_bk0), chunks(P.d_bk1)
    D_bon, D_gC = chunks(P.d_bon), chunks(P.d_gc)
    with ExitStack() as ph:
        v64 = sb("v64", [64, NV64], F32, ph)
        cx.dma(sp, v64[:, :], P.vec64[li], outs=[v64])
        omL = sb("omL", [128, 36], F32, ph)
        o_ = VOFF["mu_w"]
        cx.op(dve, lambda: nc.vector.tensor_scalar(omL[:, 0:4], vec[:, o_:o_ + 4], -1.0, 1.0, ALU.mult, ALU.add), outs=[omL], ins=[vec])
        for c0_, nm_ in ((4, "mu_v"), (12, "mu_r"), (20, "mu_k"), (28, "k_a")):
            cx.op(dve, lambda c0_=c0_, nm_=nm_: nc.vector.tensor_scalar(omL[:, c0_:c0_ + 8], vec[:, VOFF[nm_]:VOFF[nm_] + 8], -1.0, 1.0, ALU.mult, ALU.add),
                  outs=[omL], ins=[vec])
        rmask = sb("rmask", [128, T], BF16, ph)
        cx.op(pool, lambda: nc.gpsimd.memset(rmask[:, :], 1.0), outs=[rmask])
        cx.op(pool, lambda: nc.gpsimd.memset(rmask.t.rearrange("p (b c) -> p b c", c=128)[:, :, 0:1], 0.0), outs=[rmask])

        def head_vec(name, hd):
            return v64[:, V64[name] + hd:V64[name] + hd + 1]

        def mk_lerp(raw, e2):
            def load_lerp(dst_buf, dst_ap, row0, n, mu_ap, om_ap):
                cx.dma(sp, raw[0:n, :], d_uT[row0:row0 + n, :], outs=[raw])
                cx.op(act, lambda: nc.scalar.activation(e2[0:n, :], raw[0:n, :], AF.Identity, scale=om_ap), outs=[e2], ins=[raw] + [v64, omL])
                cx.op(dve, lambda: nc.vector.scalar_tensor_tensor(dst_ap(slice(1, T)), raw[0:n, 0:T - 1], mu_ap, e2[0:n, 1:T], ALU.mult, ALU.add),
                      outs=[dst_buf], ins=[raw, e2, v64, vec])
                cx.op(dve, lambda: nc.vector.tensor_copy(dst_ap(slice(0, 1)), e2[0:n, 0:1]), outs=[dst_buf], ins=[e2])
            return load_lerp

        mw, ma, mg = VOFF["mu_w"], VOFF["mu_a"], VOFF["mu_g"]
        with ExitStack() as ph2:
            raw = sb("raw", [128, T], F32, ph2)
            e1 = sb("e1", [128, T], F32, ph2)
            e2 = sb("e2", [128, T], F32, ph2)
            stg = [sb(f"lstg{i}", [128, T], F32, ph2) for i in range(2)]
            w2s = sb("w2s", [96, 1024], BF16, ph2)
            a2s = sb("a2s", [96, 1024], BF16, ph2)
            g2s = sb("g2s", [128, 2, 1024], BF16, ph2)
            tw = sb("tw", [96, T], BF16, ph2)
            xa = sb("xa", [96, T], BF16, ph2)
            sg = sb("sg", [128, 2, T], BF16, ph2)
            cx.dma(pool, w2s[:, :], P.rw_w2[li], outs=[w2s])
            cx.dma(pool, a2s[:, :], P.rw_a2[li], outs=[a2s])
            cx.dma(pool, g2s[:, :, :], P.rw_g2[li].rearrange("(c p) n -> p c n", p=128), outs=[g2s])
            load_lerp = mk_lerp(raw, e2)
            load_lerp(e1, lambda c: e1[0:96, c], C_XW, 96, vec[0:96, mw:mw + 1], omL[0:96, 0:1])
            cx.op(act, lambda: nc.scalar.activation(tw[:, :], e1[0:96, :], AF.Tanh), outs=[tw], ins=[e1])
            load_lerp(e1, lambda c: e1[0:96, c], C_XA, 96, vec[0:96, ma:ma + 1], omL[0:96, 1:2])
            cx.op(act, lambda: nc.scalar.copy(xa[:, :], e1[0:96, :]), outs=[xa], ins=[e1])
            for c2 in range(2):
                load_lerp(e1, lambda c, c2=c2: e1[:, c], C_XG + c2 * 128, 128, vec[:, mg + c2:mg + c2 + 1], omL[:, 2 + c2:3 + c2])
                cx.op(act, lambda c2=c2: nc.scalar.activation(sg[:, c2, :], e1[:, :], AF.Sigmoid), outs=[sg], ins=[e1])
            ns = [0]
            for cc in range(8):
                csl = slice(cc * 128, (cc + 1) * 128)
                for kind in range(3):
                    st_ = stg[ns[0] % 2]
                    ns[0] += 1
                    for tt in range(4):
                        tsl = slice(tt * 512, (tt + 1) * 512)
                        pb = bank()
                        if kind == 0:
                            cx.op(pe, lambda pb=pb, tsl=tsl: nc.tensor.matmul(pb[:, :], w2s[:, csl], tw[:, tsl], start=True, stop=True), outs=[pb], ins=[w2s, tw])
                            cx.op(act, lambda pb=pb, tsl=tsl, st_=st_: nc.scalar.activation(st_[:, tsl], pb[:, :], AF.Sigmoid, bias=vec[:, VOFF["w0"] + cc:VOFF["w0"] + cc + 1]),
                                  outs=[st_], ins=[pb, vec])
                        elif kind == 1:
                            cx.op(pe, lambda pb=pb, tsl=tsl: nc.tensor.matmul(pb[:, :], a2s[:, csl], xa[:, tsl], start=True, stop=True), outs=[pb], ins=[a2s, xa])
                            cx.op(act, lambda pb=pb, tsl=tsl, st_=st_: nc.scalar.activation(st_[:, tsl], pb[:, :], AF.Sigmoid, bias=vec[:, VOFF["a0"] + cc:VOFF["a0"] + cc + 1]),
                                  outs=[st_], ins=[pb, vec])
                        else:
                            for c2 in range(2):
                                cx.op(pe, lambda pb=pb, tsl=tsl, c2=c2: nc.tensor.matmul(pb[:, :], g2s[:, c2, csl], sg[:, c2, tsl], start=(c2 == 0), stop=(c2 == 1)),
                                      outs=[pb], ins=[g2s, sg])
                            cx.op(dve, lambda pb=pb, tsl=tsl, st_=st_: nc.vector.tensor_copy(st_[:, tsl], pb[:, :]), outs=[st_], ins=[pb])
                    dst = (D_lw, D_alr, D_g)[kind][cc]
                    cx.dma(sp, dst[:, :], st_[:, :], outs=[dst], ins=[st_])
            if li == 0:
                for cc in range(8):
                    st_ = stg[ns[0] % 2]
                    ns[0] += 1
                    load_lerp(st_, lambda c, st_=st_: st_[:, c], C_V + cc * 128, 128, vec[:, VOFF["mu_v"] + cc:VOFF["mu_v"] + cc + 1], omL[:, 4 + cc:5 + cc])
                    cx.dma(sp, D_vf[cc][:, :], st_[:, :], outs=[D_vf[cc]], ins=[st_])
            else:
                v1s = sb("v1s", [128, 8, 64], F32, ph2)
                v2s = sb("v2s", [64, 1024], F32, ph2)
                vv1 = sb("vv1", [64, T], F32, ph2)
                cx.dma(sp, v1s[:, :, :], P.rw_v1[li - 1].rearrange("(c p) n -> p c n", p=128), outs=[v1s])
                cx.dma(sp, v2s[:, :], P.rw_v2[li - 1], outs=[v2s])
                for cc in range(8):
                    st_ = stg[ns[0] % 2]
                    ns[0] += 1
                    load_lerp(st_, lambda c, st_=st_: st_[:, c], C_V + cc * 128, 128, vec[:, VOFF["mu_v"] + cc:VOFF["mu_v"] + cc + 1], omL[:, 4 + cc:5 + cc])
                    cx.dma(sp, D_v[cc][:, :], st_[:, :], outs=[D_v[cc]], ins=[st_])
                    for tt in range(4):
                        cx.op(pe, lambda tt=tt, cc=cc, st_=st_: nc.tensor.matmul(psum[4 + tt][0:64, :], v1s[:, cc, :], st_[:, tt * 512:(tt + 1) * 512],
                                                                                start=(cc == 0), stop=(cc == 7)), outs=[psum[4 + tt]], ins=[v1s, st_])
                for tt in range(4):
                    cx.op(act, lambda tt=tt: nc.scalar.copy(vv1[:, tt * 512:(tt + 1) * 512], psum[4 + tt][0:64, :]), outs=[vv1], ins=[psum[4 + tt]])
                for cc in range(8):
                    csl = slice(cc * 128, (cc + 1) * 128)
                    st_ = stg[ns[0] % 2]
                    ns[0] += 1
                    for tt in range(4):
                        tsl = slice(tt * 512, (tt + 1) * 512)
                        pb = bank()
                        cx.op(pe, lambda pb=pb, tsl=tsl: nc.tensor.matmul(pb[:, :], v2s[:, csl], vv1[:, tsl], start=True, stop=True), outs=[pb], ins=[v2s, vv1])
                        cx.op(act, lambda pb=pb, tsl=tsl, st_=st_: nc.scalar.activation(st_[:, tsl], pb[:, :], AF.Sigmoid, bias=vec[:, VOFF["v0"] + cc:VOFF["v0"] + cc + 1]),
                              outs=[st_], ins=[pb, vec])
                    cx.dma(sp, raw[:, :], D_vf[cc][:, :], outs=[raw], ins=[D_vf[cc]])
                    cx.dma(sp, e1[:, :], D_v[cc][:, :], outs=[e1], ins=[D_v[cc]])
                    cx.op(dve, lambda: nc.vector.tensor_tensor(raw[:, :], raw[:, :], e1[:, :], ALU.subtract), outs=[raw], ins=[raw, e1])
                    cx.op(dve, lambda st_=st_: nc.vector.tensor_tensor(raw[:, :], raw[:, :], st_[:, :], ALU.mult), outs=[raw], ins=[raw, st_])
                    cx.op(dve, lambda st_=st_: nc.vector.tensor_tensor(st_[:, :], e1[:, :], raw[:, :], ALU.add), outs=[st_], ins=[raw, e1])
                    cx.dma(sp, D_v[cc][:, :], st_[:, :], outs=[D_v[cc]], ins=[st_])
            cx.barrier()
        Dvsrc = D_vf if li == 0 else D_v
        with ExitStack() as ph2:
            raw = sb("raw", [128, T], F32, ph2)
            e1 = sb("e1", [128, T], F32, ph2)
            e2 = sb("e2", [128, T], F32, ph2)
            rp = sb("rp", [128, T], F32, ph2)
            kp = sb("kp", [128, T], F32, ph2)
            kk = sb("kk", [128, T], F32, ph2)
            alr = sb("alr", [128, T], F32, ph2)
            lw = sb("lw", [128, T], F32, ph2)
            cum = sb("cum", [128, T], F32, ph2)
            vfin = sb("vfin", [128, T], F32, ph2)
            ARo = sb("ARo", [128, 2, T], F32, ph2)
            BKo = sb("BKo", [128, 2, T], F32, ph2)
            bono = sb("bono", [128, T], F32, ph2)
            gCo = sb("gCo", [128, 16], F32, ph2)
            load_lerp = mk_lerp(raw, e2)
            for cc in range(8):
                cv = lambda nm_: vec[:, VOFF[nm_] + cc:VOFF[nm_] + cc + 1]
                load_lerp(rp, lambda c: rp[:, c], C_R + cc * 128, 128, cv("mu_r"), omL[:, 12 + cc:13 + cc])
                load_lerp(kp, lambda c: kp[:, c], C_K + cc * 128, 128, cv("mu_k"), omL[:, 20 + cc:21 + cc])
                cx.dma(sp, vfin[:, :], Dvsrc[cc][:, :], outs=[vfin], ins=[Dvsrc[cc]])
                cx.dma(sp, lw[:, :], D_lw[cc][:, :], outs=[lw], ins=[D_lw[cc]])
                cx.dma(sp, alr[:, :], D_alr[cc][:, :], outs=[alr], ins=[D_alr[cc]])
                cx.op(dve, lambda: nc.vector.tensor_scalar(kk[:, :], kp[:, :], cv("k_k"), None, ALU.mult), outs=[kk], ins=[kp, vec])
                cx.op(act, lambda: nc.scalar.activation(e1[:, :], kk[:, :], AF.Square), outs=[e1], ins=[kk])
                cx.op(dve, lambda: nc.vector.tensor_tensor_scan(cum[:, :], rmask[:, :], lw[:, :], 0.0, ALU.mult, ALU.add), outs=[cum], ins=[rmask, lw])
                cumv = cum.t.rearrange("p (b c) -> p b c", c=128)
                cx.op(dve, lambda: nc.vector.tensor_tensor(raw[:, :].rearrange("p (b c) -> p b c", c=128), cumv,
                                                           cumv[:, :, 127:128].broadcast_to([128, 16, 128]), ALU.subtract), outs=[raw], ins=[cum])
                for tt in range(4):
                    tsl = slice(tt * 512, (tt + 1) * 512)
                    pb = bank()
                    cx.op(pe, lambda pb=pb, tsl=tsl: nc.tensor.matmul(pb[:, :], bd64, e1[:, tsl], start=True, stop=True), outs=[pb], ins=[cst, e1])
                    cx.op(act, lambda pb=pb, tsl=tsl: nc.scalar.activation(e2[:, tsl], pb[:, :], AF.Sqrt), outs=[e2], ins=[pb])
                cx.op(act, lambda: nc.scalar.activation(gCo[:, :], cumv[:, :, 127], AF.Exp, scale=-CW), outs=[gCo], ins=[cum])
                cx.op(act, lambda: nc.scalar.activation(ARo[:, 1, :], raw[:, :], AF.Exp, scale=-CW), outs=[ARo], ins=[raw])
                cx.op(act, lambda: nc.scalar.activation(BKo[:, 1, :], raw[:, :], AF.Exp, scale=CW), outs=[BKo], ins=[raw])
                cx.op(dve, lambda: nc.vector.tensor_tensor(cum[:, :], raw[:, :], lw[:, :], ALU.subtract), outs=[cum], ins=[raw, lw, gCo])
                cx.op(act, lambda: nc.scalar.activation(ARo[:, 0, :], cum[:, :], AF.Exp, scale=-CW), outs=[ARo], ins=[cum])
                cx.op(dve, lambda: nc.vector.tensor_tensor(ARo[:, 1, :], ARo[:, 1, :], rp[:, :], ALU.mult), outs=[ARo], ins=[ARo, rp])
                cx.op(dve, lambda: nc.vector.tensor_scalar(e2[:, :], e2[:, :], 1e-12, None, ALU.max), outs=[e2], ins=[e2])
                cx.op(dve, lambda: nc.vector.reciprocal(e2[:, :], e2[:, :]), outs=[e2], ins=[e2])
                cx.op(dve, lambda: nc.vector.tensor_tensor(kk[:, :], kk[:, :], e2[:, :], ALU.mult), outs=[kk], ins=[kk, e2])
                cx.op(dve, lambda: nc.vector.tensor_scalar(e1[:, :], alr[:, :], cv("k_a"), omL[:, 28 + cc:29 + cc], ALU.mult, ALU.add), outs=[e1], ins=[alr, vec, omL])
                cx.op(dve, lambda: nc.vector.tensor_tensor(kp[:, :], kp[:, :], e1[:, :], ALU.mult), outs=[kp], ins=[kp, e1])
                cx.op(dve, lambda: nc.vector.scalar_tensor_tensor(e1[:, :], rp[:, :], cv("r_k"), kp[:, :], ALU.mult, ALU.mult), outs=[e1], ins=[rp, kp, vec])
                for tt in range(4):
                    tsl = slice(tt * 512, (tt + 1) * 512)
                    pb = bank()
                    cx.op(pe, lambda pb=pb, tsl=tsl: nc.tensor.matmul(pb[:, :], bd64, e1[:, tsl], start=True, stop=True), outs=[pb], ins=[cst, e1])
                    cx.op(dve, lambda pb=pb, tsl=tsl: nc.vector.tensor_tensor(bono[:, tsl], pb[:, :], vfin[:, tsl], ALU.mult), outs=[bono], ins=[pb, vfin])
                cx.op(dve, lambda: nc.vector.scalar_tensor_tensor(ARo[:, 0, :], kk[:, :], -1.0, ARo[:, 0, :], ALU.mult, ALU.mult), outs=[ARo], ins=[kk, ARo])
                cx.op(dve, lambda: nc.vector.tensor_tensor(e2[:, :], kk[:, :], alr[:, :], ALU.mult), outs=[e2], ins=[kk, alr])
                cx.op(dve, lambda: nc.vector.tensor_tensor(BKo[:, 0, :], e2[:, :], BKo[:, 1, :], ALU.mult), outs=[BKo], ins=[e2, BKo])
                cx.op(dve, lambda: nc.vector.tensor_tensor(BKo[:, 1, :], BKo[:, 1, :], kp[:, :], ALU.mult), outs=[BKo], ins=[BKo, kp])
                cx.dma(sp, D_AR0[cc][:, :], ARo[:, 0, :], outs=[D_AR0[cc]], ins=[ARo])
                cx.dma(sp, D_AR1[cc][:, :], ARo[:, 1, :], outs=[D_AR1[cc]], ins=[ARo])
                cx.dma(sp, D_BK0[cc][:, :], BKo[:, 0, :], outs=[D_BK0[cc]], ins=[BKo])
                cx.dma(sp, D_BK1[cc][:, :], BKo[:, 1, :], outs=[D_BK1[cc]], ins=[BKo])
                cx.dma(sp, D_bon[cc][:, :], bono[:, :], outs=[D_bon[cc]], ins=[bono])
                cx.dma(sp, D_gC[cc].t[:, 0:16], gCo[:, :], outs=[D_gC[cc]], ins=[gCo])
            cx.barrier()

        vp = sb("vp", [64, T], F32, ph)
        g_t = [sb(f"g_t{i}", [64, T], F32, ph) for i in range(2)]
        bon = [sb(f"bon{i}", [64, T], F32, ph) for i in range(2)]
        gC = [sb(f"gC{i}", [64, 17], F32, ph) for i in range(2)]
        for i in range(2):
            cx.op(pool, lambda i=i: nc.gpsimd.memset(gC[i][:, :], 1.0), outs=[gC[i]])
        AR = sb("AR", [64, 2, T], F32, ph)
        BK = sb("BK", [64, 2, T], F32, ph)
        Oall = sb("Oall", [128, 16, 64], F32, ph)
        st1 = sb("st1", [128, 16], F32, ph)
        st2 = sb("st2", [128, 16], F32, ph)
        st3 = sb("st3", [128, 16], F32, ph)
        Hall = sb("Hall", [64, 16, 64], F32, ph)
        FTs = [sb(f"FT{i}", [64, 16, 64], F32, ph) for i in range(2)]
        Jgs = [sb(f"Jg{i}", [64, 16, 64], F32, ph) for i in range(2)]
        RwTs = [sb(f"RwT{i}", [64, 16, 128], F32, ph) for i in range(2)]
        Zss = [sb(f"Zs{i}", [128, 16, 64], F32, ph) for i in range(2)]
        ef = [sb(f"ef{i}", [64, 512], F32, ph) for i in range(2)]
        tk8 = sb("tk8", [128, NBK, 256], F32, ph)
        PA8 = sb("PA8", [128, NBK, 2, 128], F32, ph)
        QA8 = sb("QA8", [128, NBK, 2, 128], F32, ph)
        Ark8 = sb("Ark8", [128, NBK, 128], F32, ph)
        Pm8 = [sb(f"Pm8_{j}", [128, NBK, 128], F32, ph) for j in range(2)]
        Qm8 = [sb(f"Qm8_{j}", [128, NBK, 128], F32, ph) for j in range(2)]
        Mm8 = [sb(f"Mm8_{j}", [128, NBK, 128], F32, ph) for j in range(2)]
        Gm8 = sb("Gm8", [128, NBK, 128], F32, ph)
        Wt8 = sb("Wt8", [128, NBK, 64], F32, ph)
        Ys8 = sb("Ys8", [128, NBK, 64], F32, ph)
        quad = lambda t, q: Buf(t.t[:, 4 * q:4 * q + 4], t.name + f"q{q}")
        PmQ = [[quad(Pm8[j], q) for q in range(2)] for j in range(2)]
        QmQ = [[quad(Qm8[j], q) for q in range(2)] for j in range(2)]
        MmQ = [[quad(Mm8[j], q) for q in range(2)] for j in range(2)]
        su_ui = cst[:, CONST_OFF["SU"][0]:CONST_OFF["SU"][0] + 256].rearrange("p (m c) -> p m c", m=2)

        def loads(hd, par):
            cc, half = hd // 2, (hd % 2) * 64
            hsl = slice(half, half + 64)
            cx.dma(sp, AR[:, 0, :], D_AR0[cc].t[hsl, :], outs=[AR], ins=[D_AR0[cc]])
            cx.dma(sp, AR[:, 1, :], D_AR1[cc].t[hsl, :], outs=[AR], ins=[D_AR1[cc]])
            cx.dma(sp, BK[:, 0, :], D_BK0[cc].t[hsl, :], outs=[BK], ins=[D_BK0[cc]])
            cx.dma(sp, BK[:, 1, :], D_BK1[cc].t[hsl, :], outs=[BK], ins=[D_BK1[cc]])
            cx.dma(sp, vp[:, :], Dvsrc[cc].t[hsl, :], outs=[vp], ins=[Dvsrc[cc]])
            cx.dma(sp, bon[par][:, :], D_bon[cc].t[hsl, :], outs=[bon[par]], ins=[D_bon[cc]])
            cx.dma(sp, g_t[par][:, :], D_g[cc].t[hsl, :], outs=[g_t[par]], ins=[D_g[cc]])
            cx.dma(sp, gC[par][:, 0:16], D_gC[cc].t[hsl, 0:16], outs=[gC[par]], ins=[D_gC[cc]])
            cx.op(dve, lambda: nc.vector.tensor_copy(R_(AR[:, :, :]), AR[:, :, :]), outs=[AR], ins=[AR])
            cx.op(act, lambda: nc.scalar.copy(R_(BK[:, :, :]), BK[:, :, :]), outs=[BK], ins=[BK])

        def B_gen(hd, par):
            FT, Jg, RwT, Zs = FTs[par], Jgs[par], RwTs[par], Zss[par]
            gC_cur = gC[par]
            for b0 in range(0, 16, NBK):
                tok0 = b0 * 128
                yield
                for p in range(4):
                    B = bank()
                    for i2 in range(2):
                        i = 2 * p + i2
                        cs = slice(tok0 + i * 128, tok0 + (i + 1) * 128)
                        for n_, src_ap in enumerate((AR[:, 0, cs], BK[:, 0, cs], BK[:, 1, cs], vp[:, cs])):
                            cx.op(pe, lambda B=B, i2=i2, n_=n_, src_ap=src_ap: nc.tensor.transpose(
                                B[:, i2 * 256 + n_ * 64:i2 * 256 + (n_ + 1) * 64], src_ap, ident(64, 64)), outs=[B], ins=[AR, BK, vp, cst])
                    cx.op(act, lambda B=B, p=p: nc.scalar.copy(R_(tk8[:, 2 * p:2 * p + 2, :]), B[:, :].rearrange("p (b c) -> p b c", b=2)), outs=[tk8], ins=[B])
                yield
                for p in range(4):
                    B = bank()
                    for i2 in range(2):
                        i = 2 * p + i2
                        cs = slice(tok0 + i * 128, tok0 + (i + 1) * 128)
                        cx.op(pe, lambda B=B, i2=i2, cs=cs: nc.tensor.matmul(B[:, i2 * 256:(i2 + 1) * 256], R_(AR[:, 0, cs]), R_(BK[:, :, cs]), start=True, stop=True),
                              outs=[B], ins=[AR, BK])
                    cx.op(dve, lambda B=B, p=p: nc.vector.tensor_tensor(
                        R_(PA8[:, 2 * p:2 * p + 2, :, :]), B[:, :].rearrange("p (b m c) -> p b m c", b=2, m=2),
                        mSL.unsqueeze(1).unsqueeze(1).broadcast_to([128, 2, 2, 128]), ALU.mult), outs=[PA8], ins=[B, cst])
                yield
                for p in range(4):
                    B = bank()
                    for i2 in range(2):
                        i = 2 * p + i2
                        cs = slice(tok0 + i * 128, tok0 + (i + 1) * 128)
                        cx.op(pe, lambda B=B, i2=i2, cs=cs: nc.tensor.matmul(B[:, i2 * 256:(i2 + 1) * 256], R_(BK[:, 0, cs]), R_(AR[:, :, cs]), start=True, stop=True),
                              outs=[B], ins=[AR, BK])
                    cx.op(dve, lambda B=B, p=p: nc.vector.tensor_tensor(
                        R_(QA8[:, 2 * p:2 * p + 2, :, :]), B[:, :].rearrange("p (b m c) -> p b m c", b=2, m=2),
                        su_ui.unsqueeze(1).broadcast_to([128, 2, 2, 128]), ALU.mult), outs=[QA8], ins=[B, cst])
                yield
                for q in range(2):
                    B = bank()
                    for i4 in range(4):
                        i = 4 * q + i4
                        cs = slice(tok0 + i * 128, tok0 + (i + 1) * 128)
                        cx.op(pe, lambda B=B, i4=i4, cs=cs: nc.tensor.matmul(B[:, i4 * 128:(i4 + 1) * 128], R_(BK[:, 1, cs]), R_(AR[:, 1, cs]), start=True, stop=True),
                              outs=[B], ins=[AR, BK])
                    cx.op(dve, lambda B=B, q=q: nc.vector.tensor_tensor(
                        R_(Ark8[:, 4 * q:4 * q + 4, :]), B[:, :].rearrange("p (b c) -> p b c", b=4),
                        mUI.unsqueeze(1).broadcast_to([128, 4, 128]), ALU.mult), outs=[Ark8], ins=[B, cst])
                for q in range(2):
                    cx.op(dve, lambda q=q: nc.vector.tensor_tensor(
                        R_(Mm8[0][:, 4 * q:4 * q + 4, :]), QA8[:, 4 * q:4 * q + 4, 0, :], ident().unsqueeze(1).broadcast_to([128, 4, 128]), ALU.add),
                        outs=[MmQ[0][q]], ins=[QA8, cst])
                yield
                def Pj(j, i):
                    return PA8[:, i, 0, :] if j == 0 else Pm8[j % 2][:, i, :]

                def Qj(j, i):
                    return QA8[:, i, 0, :] if j == 0 else Qm8[j % 2][:, i, :]

                def Pb(j, q):
                    return [PA8] if j == 0 else [PmQ[j % 2][q]]

                def Qb(j, q):
                    return [QA8] if j == 0 else [QmQ[j % 2][q]]

                def emit_pq(j):
                    n = (j + 1) % 2
                    for q in range(2):
                        B = bank()
                        for i4 in range(4):
                            i = 4 * q + i4
                            cx.op(pe, lambda B=B, i4=i4, i=i, j=j: nc.tensor.matmul(B[:, i4 * 128:(i4 + 1) * 128], R_(Qj(j, i)), R_(Pj(j, i)), start=True, stop=True),
                                  outs=[B], ins=Pb(j, q) + Qb(j, q))
                        cx.op(act, lambda B=B, q=q, n=n: nc.scalar.copy(R_(Pm8[n][:, 4 * q:4 * q + 4, :]), B[:, :].rearrange("p (b c) -> p b c", b=4)),
                              outs=[PmQ[n][q]], ins=[B])
                    if j < 5:
                        for q in range(2):
                            B = bank()
                            for i4 in range(4):
                                i = 4 * q + i4
                                cx.op(pe, lambda B=B, i4=i4, i=i, j=j: nc.tensor.matmul(B[:, i4 * 128:(i4 + 1) * 128], R_(Pj(j, i)), R_(Qj(j, i)), start=True, stop=True),
                                      outs=[B], ins=Pb(j, q) + Qb(j, q))
                            if q == 0:
                                cx.op(act, lambda B=B, q=q, n=n: nc.scalar.copy(R_(Qm8[n][:, 4 * q:4 * q + 4, :]), B[:, :].rearrange("p (b c) -> p b c", b=4)),
                                      outs=[QmQ[n][q]], ins=[B])
                            else:
                                cx.op(dve, lambda B=B, q=q, n=n: nc.vector.tensor_copy(R_(Qm8[n][:, 4 * q:4 * q + 4, :]), B[:, :].rearrange("p (b c) -> p b c", b=4)),
                                      outs=[QmQ[n][q]], ins=[B])

                def emit_m(j):
                    n = (j + 1) % 2
                    for q in range(2):
                        B = bank()
                        for i4 in range(4):
                            i = 4 * q + i4
                            cx.op(pe, lambda B=B, i4=i4, i=i, j=j, n=n: nc.tensor.matmul(B[:, i4 * 128:(i4 + 1) * 128], R_(Pm8[n][:, i, :]), R_(Mm8[j % 2][:, i, :]),
                                                                                    start=True, stop=True), outs=[B], ins=[PmQ[n][q], MmQ[j % 2][q]])
                        cx.op(dve, lambda B=B, q=q, j=j, n=n: nc.vector.tensor_tensor(
                            R_(Mm8[n][:, 4 * q:4 * q + 4, :]), B[:, :].rearrange("p (b c) -> p b c", b=4), Mm8[j % 2][:, 4 * q:4 * q + 4, :], ALU.add),
                            outs=[MmQ[n][q]], ins=[B, MmQ[j % 2][q]])

                emit_pq(0)
                yield
                for j in range(6):
                    if j + 1 < 6:
                        emit_pq(j + 1)
                        yield
                    emit_m(j)
                    yield
                Mf = Mm8[0]
                MfQ = MmQ[0]
                yield
                B = bank()
                for i in range(NBK):
                    cx.op(pe, lambda B=B, i=i: nc.tensor.matmul(B[:, i * 64:(i + 1) * 64], R_(Mf[:, i, :]), R_(tk8[:, i, 0:64]), start=True, stop=True),
                          outs=[B], ins=[MfQ[i // 4], tk8])
                cx.op(act, lambda B=B: nc.scalar.copy(Wt8[:, :, :], B[:, :].rearrange("p (b c) -> p b c", b=8)), outs=[Wt8], ins=[B])
                yield
                for q in range(2):
                    B = bank()
                    for i4 in range(4):
                        i = 4 * q + i4
                        cx.op(pe, lambda B=B, i4=i4, i=i: nc.tensor.matmul(B[:, i4 * 128:(i4 + 1) * 128], R_(PA8[:, i, 1, :]), R_(Mf[:, i, :]), start=True, stop=True),
                              outs=[B], ins=[PA8, MfQ[q]])
                    cx.op(dve, lambda B=B, q=q: nc.vector.tensor_copy(R_(Gm8[:, 4 * q:4 * q + 4, :]), B[:, :].rearrange("p (b c) -> p b c", b=4)), outs=[Gm8], ins=[B])
                yield
                B = bank()
                for i in range(NBK):
                    cx.op(pe, lambda B=B, i=i: nc.tensor.matmul(B[:, i * 64:(i + 1) * 64], R_(Gm8[:, i, :]), R_(tk8[:, i, 192:256]), start=True, stop=True),
                          outs=[B], ins=[Gm8, tk8])
                cx.op(act, lambda B=B: nc.scalar.copy(R_(Ys8[:, :, :]), B[:, :].rearrange("p (b c) -> p b c", b=8)), outs=[Ys8], ins=[B])
                yield
                B = bank()
                for i in range(NBK):
                    cx.op(pe, lambda B=B, i=i: nc.tensor.matmul(B[0:64, i * 64:(i + 1) * 64], Wt8[:, i, :], tk8[:, i, 64:128], start=True, stop=True),
                          outs=[B], ins=[Wt8, tk8])
                cx.op(dve, lambda B=B, b0=b0: nc.vector.tensor_tensor(FT[:, b0:b0 + 8, :], B[0:64, :].rearrange("p (b c) -> p b c", b=8),
                                                                      ident(64, 64).unsqueeze(1).broadcast_to([64, 8, 64]), ALU.add), outs=[FT], ins=[B, cst])
                yield
                B = bank()
                for i in range(NBK):
                    cx.op(pe, lambda B=B, i=i: nc.tensor.matmul(B[0:64, i * 64:(i + 1) * 64], tk8[:, i, 64:128], Ys8[:, i, :], start=True, stop=False),
                          outs=[B], ins=[Ys8, tk8])
                    cx.op(pe, lambda B=B, i=i: nc.tensor.matmul(B[0:64, i * 64:(i + 1) * 64], tk8[:, i, 128:192], tk8[:, i, 192:256], start=False, stop=True),
                          outs=[B], ins=[tk8])
                cx.op(dve, lambda B=B, b0=b0: nc.vector.tensor_tensor(Jg[:, b0:b0 + 8, :], B[0:64, :].rearrange("p (b c) -> p b c", b=8),
                                                                      gC_cur[:, b0 + 1:b0 + 9].unsqueeze(2).broadcast_to([64, 8, 64]), ALU.mult), outs=[Jg], ins=[B, gC_cur])
                yield
                for q in range(2):
                    B = bank()
                    for i4 in range(4):
                        i = 4 * q + i4
                        cx.op(pe, lambda B=B, i4=i4, i=i: nc.tensor.matmul(B[0:64, i4 * 128:(i4 + 1) * 128], Wt8[:, i, :], QA8[:, i, 1, :], start=True, stop=True),
                              outs=[B], ins=[Wt8, QA8])
                    t0_ = tok0 + q * 512
                    cx.op(dve, lambda B=B, q=q, b0=b0, t0_=t0_: nc.vector.tensor_tensor(
                        RwT[:, b0 + 4 * q:b0 + 4 * q + 4, :], B[0:64, :].rearrange("p (b c) -> p b c", b=4),
                        AR[:, 1, t0_:t0_ + 512].rearrange("p (b c) -> p b c", b=4), ALU.add), outs=[RwT], ins=[B, AR])
                yield
                B = bank()
                for i in range(NBK):
                    cx.op(pe, lambda B=B, i=i: nc.tensor.matmul(B[:, i * 64:(i + 1) * 64], R_(QA8[:, i, 1, :]), R_(Ys8[:, i, :]), start=True, stop=False),
                          outs=[B], ins=[QA8, Ys8])
                    cx.op(pe, lambda B=B, i=i: nc.tensor.matmul(B[:, i * 64:(i + 1) * 64], R_(Ark8[:, i, :]), R_(tk8[:, i, 192:256]), start=False, stop=True),
                          outs=[B], ins=[Ark8, tk8])
                cx.op(act, lambda B=B, b0=b0: nc.scalar.copy(Zs[:, b0:b0 + 8, :], B[:, :].rearrange("p (b c) -> p b c", b=8)), outs=[Zs], ins=[B])

            yield

        def tail_gen(hd, par):
            gCp = gC[par]
            FT, Jg, RwT, Zs = FTs[par], Jgs[par], RwTs[par], Zss[par]
            Osq = Zs
            Hb = [Buf(Hall.t[:, c, :], f"H{c}") for c in range(16)]
            cx.op(dve, lambda: nc.vector.memset(Hb[0][:, :], 0.0), outs=[Hb[0]], ins=[Hall])
            cx.op(dve, lambda: nc.vector.tensor_copy(Hb[1][:, :], Jg[:, 0, :]), outs=[Hb[1]], ins=[Jg, Hall])
            for c in range(1, 15):
                B = bank()
                cx.op(pe, lambda c=c, B=B: nc.tensor.matmul(B[0:64, 0:64], FT[:, c, :], Hb[c][:, :], start=True, stop=True), outs=[B], ins=[FT, Hb[c]])
                cx.op(dve, lambda c=c, B=B: nc.vector.scalar_tensor_tensor(Hb[c + 1][:, :], B[0:64, 0:64], gCp[:, c + 1:c + 2], Jg[:, c, :], ALU.mult, ALU.add),
                      outs=[Hb[c + 1]], ins=[B, gCp, Jg])
                yield
            for c0 in (0, 8):
                B = bank()
                for i in range(8):
                    c = c0 + i
                    cx.op(pe, lambda c=c, i=i, B=B: nc.tensor.matmul(B[:, i * 64:(i + 1) * 64], RwT[:, c, :], Hb[c][:, :], start=True, stop=True),
                          outs=[B], ins=[RwT, Hb[c]])
                cx.op(dve, lambda c0=c0, B=B: nc.vector.tensor_tensor(Oall[:, c0:c0 + 8, :], B[:, :].rearrange("p (b c) -> p b c", b=8), Zs[:, c0:c0 + 8, :], ALU.add),
                      outs=[Oall], ins=[B, Zs])
                yield
            cx.op(dve, lambda: nc.vector.tensor_reduce(st1[:, :], Oall[:, :, :], AX.X, ALU.add), outs=[st1], ins=[Oall] + Hb)
            cx.op(act, lambda: nc.scalar.activation(Osq[:, :, :], Oall[:, :, :], AF.Square), outs=[Osq], ins=[Oall])
            cx.op(dve, lambda: nc.vector.tensor_reduce(st2[:, :], Osq[:, :, :], AX.X, ALU.add), outs=[st2], ins=[Osq])
            yield
            cx.op(dve, lambda: nc.vector.tensor_scalar(st1[:, :], st1[:, :], 1.0 / 64, None, ALU.mult), outs=[st1], ins=[st1])
            cx.op(dve, lambda: nc.vector.tensor_tensor(st3[:, :], st1[:, :], st1[:, :], ALU.mult), outs=[st3], ins=[st1])
            cx.op(dve, lambda: nc.vector.scalar_tensor_tensor(st2[:, :], st2[:, :], 1.0 / 64, st3[:, :], ALU.mult, ALU.subtract), outs=[st2], ins=[st2, st3])
            cx.op(dve, lambda: nc.vector.tensor_scalar(st2[:, :], st2[:, :], GN_EPS, None, ALU.add), outs=[st2], ins=[st2])
            cx.op(act, lambda: nc.scalar.activation(st3[:, :], st2[:, :], AF.Sqrt), outs=[st3], ins=[st2])
            cx.op(dve, lambda: nc.vector.reciprocal(st2[:, :], st3[:, :]), outs=[st2], ins=[st3])
            yield
            for blk in range(16):
                cx.op(dve, lambda blk=blk: nc.vector.tensor_scalar(Osq[:, blk, :], Oall[:, blk, :], st1[:, blk:blk + 1], st2[:, blk:blk + 1],
                                                                  ALU.subtract, ALU.mult), outs=[Osq], ins=[Oall, st1, st2])
                if blk % 4 == 3:
                    yield
            for tt in range(4):
                pb = bank()
                efb = ef[tt % 2]
                for b4 in range(4):
                    blk = tt * 4 + b4
                    cx.op(pe, lambda pb=pb, b4=b4, blk=blk: nc.tensor.transpose(pb[0:64, b4 * 128:(b4 + 1) * 128], Osq[:, blk, :], ident()),
                          outs=[pb], ins=[Osq, cst])
                tsl = slice(tt * 512, (tt + 1) * 512)
                cx.op(act, lambda pb=pb, efb=efb: nc.scalar.activation(efb[:, :], pb[0:64, :], AF.Identity, bias=head_vec("lnx_b", hd),
                                                                       scale=head_vec("lnx_g", hd)), outs=[efb], ins=[pb, v64])
                cx.op(dve, lambda efb=efb, tsl=tsl: nc.vector.tensor_tensor(efb[:, :], efb[:, :], bon[par][:, tsl], ALU.add), outs=[efb], ins=[efb, bon[par]])
                cx.op(dve, lambda efb=efb, tsl=tsl: nc.vector.tensor_tensor(efb[:, :], efb[:, :], g_t[par][:, tsl], ALU.mult), outs=[efb], ins=[efb, g_t[par]])
                cx.dma(sp, P.d_o[hd * 64:(hd + 1) * 64, tsl], efb[:, :], ins=[efb])
                yield


        def drain(gen, n=None):
            if gen is None:
                return None
            k = 0
            while n is None or k < n:
                try:
                    next(gen)
                except StopIteration:
                    return None
                k += 1
            return gen

        ada = None
        if ada_next is not None:
            wa2 = [sb(f"wa2_{i}", [128, NKC, 128], BF16, ph) for i in range(2)]

            glist = [(ada_next, g, g) for g in range(96)]
            if getattr(P, "split0", False) and li == 0:
                glist = [(0, 32 + g, 96 + g) for g in range(64)] + glist

            def ada_gen():
                def issue(n):
                    wb = wa2[n % 2]
                    lyr, ch, _ = glist[n]
                    cx.dma(pool, wb[:, :, :], P.w_ada[lyr][:, ch * 128:(ch + 1) * 128].rearrange("(kc p) n -> p kc n", p=128), outs=[wb])
                issue(0)
                yield
                for n in range(len(glist)):
                    if n + 1 < len(glist):
                        issue(n + 1)
                    wb = wa2[n % 2]
                    col = glist[n][2]
                    for kc in range(NKC):
                        cx.op(pe, lambda col=col, kc=kc, wb=wb: nc.tensor.matmul(psum[7][:, col:col + 1], wb[:, kc, :], P.cond_bf[:, kc:kc + 1],
                                                                              start=(kc == 0), stop=(kc == NKC - 1)), outs=[psum[7]], ins=[wb, P.cond_bf])
                    yield
            ada = ada_gen()

        prev_tail = None
        nheads = P.debug.get("rwkv_heads") or 16
        for hd in range(nheads):
            par = hd % 2
            loads(hd, par)
            kq = 0
            for _ in B_gen(hd, par):
                kq += 1
                if kq % 2 == 0:
                    prev_tail = drain(prev_tail, 1)
                if kq % 4 == 0:
                    ada = drain(ada, 1)
            prev_tail = drain(prev_tail)
            prev_tail = tail_gen(hd, par)
        drain(prev_tail)
        if ada_next is not None:
            drain(ada)
            cx.op(act, lambda: nc.scalar.copy(P.modraw[:, :], psum[7][:, 0:96]), outs=[P.modraw], ins=[psum[7]])
            if getattr(P, "split0", False) and li == 0:
                cx.op(act, lambda: nc.scalar.copy(P.modraw0[:, :], psum[7][:, 96:160]), outs=[P.modraw0], ins=[psum[7]])
        cx.barrier()


def emit_nsa(P, cx, li, psum, vec, cst, sb):
    nc = P.nc
    pe, act, dve, pool, sp = cx.pe, cx.act, cx.dve, cx.pool, cx.sp
    d_uT = P.d_uT
    SCALE = 0.125
    ident = lambda n=128, m=128: cst[0:n, CONST_OFF["ident"][0]:CONST_OFF["ident"][0] + m]
    mUI = cst[:, cs_("UI")]
    mSL = cst[:, cs_("SL")]
    msc = [0]

    def misc():
        msc[0] += 1
        return psum[6]

    with ExitStack() as ph:
        nsac = sb("nsac", [128, NNSAC], F32, ph)
        cx.dma(sp, nsac[:, :], P.nsac[:, :], outs=[nsac])
        ncs = lambda name: slice(NSAC[name][0], NSAC[name][0] + NSAC[name][1])
        cosT = sb("cosT", [64, T], F32, ph)
        sinT = sb("sinT", [64, T], F32, ph)
        cx.dma(sp, cosT[:, :], P.rope[0][0:64, :], outs=[cosT])
        cx.dma(sp, sinT[:, :], P.rope[1][0:64, :], outs=[sinT])
        raw = sb("nraw", [128, T], F32, ph)
        rot = sb("nrot", [64, T], F32, ph)
        t1 = sb("nt1", [64, T], F32, ph)
        kcr = sb("kcr", [64, T], F32, ph)
        K2 = sb("K2", [128, T], F32, ph)
        sets = [dict(qr=sb(f"qr{i}", [64, 4, T], BF16, ph), ksr=sb(f"ksr{i}", [64, T], BF16, ph), kwr=sb(f"kwr{i}", [64, T], BF16, ph),
                     kcT=sb(f"kcT{i}", [64, 128], BF16, ph), vs_aug=sb(f"vs_aug{i}", [128, 16, 65], BF16, ph),
                     vw_aug=sb(f"vw_aug{i}", [128, 16, 65], BF16, ph), vc_aug=sb(f"vc_aug{i}", [128, 97], BF16, ph)) for i in range(2)]
        w1s = sb("w1s", [128, 16, 256], F32, ph)
        w2s = sb("w2s_n", [128, 2, 64], F32, ph)
        pos2 = sb("pos2", [128, 16], F32, ph)
        cbias = sb("cbias", [128, 2], F32, ph)
        Gt = sb("Gt", [128, 2, 128], F32, ph)
        gx = sb("gx", [128, 2, 128], F32, ph)
        gy = sb("gy", [128, 2, 128], F32, ph)
        gsb = sb("gsb", [128, 16, 48], F32, ph)
        Eb = [sb(f"Eb{i}", [128, 4, 128], BF16, ph) for i in range(6)]
        mk = [sb(f"mk{i}", [128, 128], BF16, ph) for i in range(3)]
        selTs = [sb(f"selT{i}", [32, 128], BF16, ph) for i in range(2)]
        mk4 = [sb(f"mk4_{i}", [128, 4, 128], BF16, ph) for i in range(2)]
        mUIb = sb("mUIb", [128, 128], BF16, ph)
        expb = sb("expb", [32, 16, 128], BF16, ph)
        impt = sb("impt", [128, 32], F32, ph)
        impf = sb("impf", [128, 32], F32, ph)
        top8 = sb("top8", [128, 8], F32, ph)
        selm = sb("selm", [128, 32], F32, ph)
        rds = [sb(f"rd{i}", [128, 12], F32, ph) for i in range(2)]
        cf = sb("cf", [128, 12], F32, ph)
        Rcs = [sb(f"Rc{i}", [128, 4, 97], F32, ph) for i in range(2)]
        otok = sb("otok", [128, 4, 64], F32, ph)
        ofm = sb("ofm", [128, 2, T], F32, ph)

        cx.op(dve, lambda: nc.vector.tensor_copy(expb[:, :, :], nsac[0:32, ncs("expand")].rearrange("p (k s) -> p k s", s=128)), outs=[expb], ins=[nsac])
        cx.op(dve, lambda: nc.vector.tensor_copy(mUIb[:, :], mUI), outs=[mUIb], ins=[cst])
        cx.dma(sp, raw[0:48, :], d_uT[C_GATE:C_GATE + 48, :], outs=[raw])
        for i in range(16):
            pb = misc()
            cx.op(pe, lambda pb=pb, i=i: nc.tensor.transpose(pb[:, 0:48], raw[0:48, i * 128:(i + 1) * 128], ident(48, 48)), outs=[pb], ins=[raw, cst])
            cx.op(act, lambda pb=pb, i=i: nc.scalar.activation(gsb[:, i, :], pb[:, 0:48], AF.Sigmoid), outs=[gsb], ins=[pb])

        def load_rope(row0, out_buf, out_ap):
            cx.dma(sp, raw[0:64, :], d_uT[row0:row0 + 64, :], outs=[raw])
            cx.dma(sp, rot[0:32, :], d_uT[row0 + 32:row0 + 64, :], outs=[rot])
            cx.dma(sp, rot[32:64, :], d_uT[row0:row0 + 32, :], outs=[rot])
            cx.op(dve, lambda: nc.vector.tensor_tensor(t1[:, :], raw[0:64, :], cosT[:, :], ALU.mult), outs=[t1], ins=[raw, cosT])
            cx.op(dve, lambda: nc.vector.tensor_tensor(rot[:, :], rot[:, :], sinT[:, :], ALU.mult), outs=[rot], ins=[rot, sinT])
            cx.op(dve, lambda: nc.vector.tensor_tensor(out_ap, t1[:, :], rot[:, :], ALU.add), outs=[out_buf], ins=[t1, rot])

        def v_aug_build(row0, dst):
            cx.dma(sp, raw[0:64, :], d_uT[row0:row0 + 64, :], outs=[raw])
            cx.op(pool, lambda: nc.gpsimd.memset(dst[:, :, 64:65], 1.0), outs=[dst])
            for i in range(16):
                pb = misc()
                cx.op(pe, lambda pb=pb, i=i: nc.tensor.transpose(pb[:, 0:64], raw[0:64, i * 128:(i + 1) * 128], ident(64, 64)), outs=[pb], ins=[raw, cst])
                cx.op(act, lambda pb=pb, i=i: nc.scalar.copy(dst[:, i, 0:64], pb[:, 0:64]), outs=[dst], ins=[pb])

        def compress(kv, g):
            cx.dma(sp, w1s[:, :, :], P.cmp_w1[li][kv].rearrange("(a p) h -> p a h", p=128), outs=[w1s])
            cx.dma(sp, w2s[:, :, :], P.cmp_w2[li][kv].rearrange("(c p) d -> p c d", p=128), outs=[w2s])
            cx.dma(sp, pos2[:, :], P.cmp_pos2[li][kv], outs=[pos2])
            K2v = K2.t.rearrange("p (c s) -> p c s", s=16)
            for ch in range(2):
                pbias = misc()
                for a in range(16):
                    cx.op(pe, lambda a=a, ch=ch, pbias=pbias: nc.tensor.matmul(pbias[:, 0:1], w1s[:, a, ch * 128:(ch + 1) * 128], pos2[:, a:a + 1],
                                                                               start=(a == 0), stop=(a == 15)), outs=[pbias], ins=[w1s, pos2])
                cx.op(dve, lambda ch=ch, pbias=pbias: nc.vector.tensor_copy(cbias[:, ch:ch + 1], pbias[:, 0:1]), outs=[cbias], ins=[pbias])
                pb = misc()
                for a in range(16):
                    rhs = K2v[:, 0:127, 2 * a] if a < 8 else K2v[:, 1:128, 2 * a - 16]
                    cx.op(pe, lambda a=a, ch=ch, pb=pb, rhs=rhs: nc.tensor.matmul(pb[:, 0:127], w1s[:, a, ch * 128:(ch + 1) * 128], rhs,
                                                                                  start=(a == 0), stop=(a == 15)), outs=[pb], ins=[w1s, K2])
                cx.op(act, lambda ch=ch, pb=pb: nc.scalar.activation(gx[:, ch, 0:127], pb[:, 0:127], AF.Identity, bias=cbias[:, ch:ch + 1]),
                      outs=[gx], ins=[pb, cbias])
            cx.op(pool, lambda: nc.gpsimd.tensor_tensor(gy[:, :, 0:127], gx[:, :, 0:127], gx[:, :, 0:127], ALU.mult), outs=[gy], ins=[gx])
            cx.op(dve, lambda: nc.vector.tensor_scalar(gy[:, :, 0:127], gy[:, :, 0:127], 0.044715, 1.0, ALU.mult, ALU.add), outs=[gy], ins=[gy])
            cx.op(dve, lambda: nc.vector.tensor_tensor(gy[:, :, 0:127], gy[:, :, 0:127], gx[:, :, 0:127], ALU.mult), outs=[gy], ins=[gy, gx])
            cx.op(act, lambda: nc.scalar.activation(gy[:, :, 0:127], gy[:, :, 0:127], AF.Tanh, scale=0.7978845608028654), outs=[gy], ins=[gy])
            cx.op(dve, lambda: nc.vector.tensor_scalar(gy[:, :, 0:127], gy[:, :, 0:127], 0.5, 0.5, ALU.mult, ALU.add), outs=[gy], ins=[gy])
            cx.op(dve, lambda: nc.vector.tensor_tensor(Gt[:, :, 0:127], gy[:, :, 0:127], gx[:, :, 0:127], ALU.mult), outs=[Gt], ins=[gy, gx])

        def setup_gen(g, S):
            qr, ksr, kwr, kcT, vs_aug, vw_aug, vc_aug = (S[k_] for k_ in ('qr', 'ksr', 'kwr', 'kcT', 'vs_aug', 'vw_aug', 'vc_aug'))
            for h in range(4):
                load_rope(C_Q + (4 * g + h) * 64, qr, qr[:, h, :])
                yield
            load_rope(C_KS + g * 64, ksr, ksr[:, :])
            yield
            load_rope(C_KW + g * 64, kwr, kwr[:, :])
            yield
            load_rope(C_KC + g * 64, kcr, kcr[:, :])
            yield
            v_aug_build(C_VS + g * 64, vs_aug)
            yield
            v_aug_build(C_VW + g * 64, vw_aug)
            yield
            cx.op(pool, lambda: nc.gpsimd.memset(K2[:, :], 0.0), outs=[K2])
            cx.op(dve, lambda: nc.vector.tensor_copy(K2[0:64, :], kcr[:, :]), outs=[K2], ins=[kcr])
            cx.dma(sp, K2[64:128, 0:T - 1], kcr[:, 1:T], outs=[K2], ins=[kcr])
            compress(0, g)
            yield
            pb = misc()
            for ch in range(2):
                cx.op(pe, lambda ch=ch, pb=pb: nc.tensor.matmul(pb[0:64, 0:127], w2s[:, ch, :], Gt[:, ch, 0:127], start=(ch == 0), stop=(ch == 1)),
                      outs=[pb], ins=[w2s, Gt])
            cx.op(act, lambda pb=pb: nc.scalar.copy(kcT[:, 0:127], pb[0:64, 0:127]), outs=[kcT], ins=[pb])
            cx.op(pool, lambda: nc.gpsimd.memset(K2[:, :], 0.0), outs=[K2])
            cx.dma(sp, K2[0:64, :], d_uT[C_VC + g * 64:C_VC + (g + 1) * 64, :], outs=[K2])
            cx.dma(sp, K2[64:128, 0:T - 1], d_uT[C_VC + g * 64:C_VC + (g + 1) * 64, 1:T], outs=[K2])
            compress(1, g)
            yield
            pb = misc()
            for ch in range(2):
                cx.op(pe, lambda ch=ch, pb=pb: nc.tensor.matmul(pb[0:127, 0:64], Gt[:, ch, 0:127], w2s[:, ch, :], start=(ch == 0), stop=(ch == 1)),
                      outs=[pb], ins=[w2s, Gt])
            cx.op(pool, lambda: nc.gpsimd.memset(vc_aug[:, :], 0.0), outs=[vc_aug])
            cx.op(act, lambda pb=pb: nc.scalar.copy(vc_aug[0:127, 0:64], pb[0:127, 0:64]), outs=[vc_aug], ins=[pb])
            cx.op(pool, lambda: nc.gpsimd.memset(vc_aug[0:127, 64:65], 1.0), outs=[vc_aug])
            cx.op(dve, lambda: nc.vector.tensor_copy(vc_aug[0:127, 65:97], nsac[0:127, ncs("ov")]), outs=[vc_aug], ins=[nsac])

            yield

        def attention(g, S, side):
            qr, ksr, kwr, kcT, vs_aug, vw_aug, vc_aug = (S[k_] for k_ in ('qr', 'ksr', 'kwr', 'kcT', 'vs_aug', 'vw_aug', 'vc_aug'))
            ne = [0]

            pend = []
            ST_BANKS = [0, 1, 2, 7]

            def flush(keep=0):
                while len(pend) > keep:
                    pend.pop(0)()

            def flush_through(bk):
                while bk in pend:
                    pend.pop(0)()

            def attn_tile(klhsT, tq, mask_ap, vrhs, acc, first, kparts=128):
                stp = psum[ST_BANKS[ne[0] % 4]]
                E = Eb[ne[0] % 6]
                ne[0] += 1
                cx.op(pe, lambda: nc.tensor.matmul(stp[0:kparts, :], klhsT, qr[:, :, tq], start=True, stop=True), outs=[stp], ins=[qr, ksr, kwr, kcT])
                cx.op(act, lambda: nc.scalar.activation(E[0:kparts, :, :], stp[0:kparts, :].rearrange("p (h t) -> p h t", h=4), AF.Exp, scale=SCALE),
                      outs=[E], ins=[stp])
                if mask_ap is not None:
                    cx.op(dve, lambda: nc.vector.tensor_tensor(E[0:kparts, :, :], E[0:kparts, :, :], mask_ap, ALU.mult), outs=[E], ins=[E, mk4[0], mk4[1], cst, nsac])

                def back():
                    W = vrhs.shape[-1]
                    for h in range(4):
                        cx.op(pe, lambda h=h: nc.tensor.matmul(acc[:, h * W:(h + 1) * W], E[0:kparts, h, :], vrhs,
                                                               start=(first and h == 0), stop=True, skip_group_check=True),
                              outs=[acc], ins=[E, vs_aug, vw_aug, vc_aug])
                pend.append(back)
                flush(keep=3)
                return back

            def bc4(ap2d, kparts=128):
                return ap2d.unsqueeze(1).broadcast_to([kparts, 4, 128])

            cmp_back = [None]

            def pre(i, par):
                tq = slice(i * 128, (i + 1) * 128)
                Rc_, rd_, selT_ = Rcs[par], rds[par], selTs[par]
                accc = psum[5]
                cmp_back[0] = attn_tile(kcT[:, 0:127], tq, bc4(nsac[0:127, NSAC["cmaskT"][0] + i * 128:NSAC["cmaskT"][0] + (i + 1) * 128], 127),
                                        vc_aug[0:127, :], accc, True, kparts=127)

            def pre_b(i, par):
                tq = slice(i * 128, (i + 1) * 128)
                Rc_, rd_, selT_ = Rcs[par], rds[par], selTs[par]
                accc = psum[5]
                flush_through(cmp_back[0])
                cx.op(act, lambda: nc.scalar.copy(Rc_[:, :, :], accc[:, 0:388].rearrange("p (h c) -> p h c", h=4)), outs=[Rc_], ins=[accc])
                cx.op(dve, lambda: nc.vector.tensor_scalar(rd_[:, 0:4], Rc_[:, :, 64], 1e-30, None, ALU.max), outs=[rd_], ins=[Rc_])
                cx.op(dve, lambda: nc.vector.reciprocal(rd_[:, 0:4], rd_[:, 0:4]), outs=[rd_], ins=[rd_])
                for h in range(4):
                    if h == 0:
                        cx.op(dve, lambda h=h: nc.vector.tensor_scalar(impt[:, :], Rc_[:, h, 65:97], rd_[:, h:h + 1], None, ALU.mult), outs=[impt], ins=[Rc_, rd_])
                    else:
                        cx.op(dve, lambda h=h: nc.vector.scalar_tensor_tensor(impt[:, :], Rc_[:, h, 65:97], rd_[:, h:h + 1], impt[:, :], ALU.mult, ALU.add),
                              outs=[impt], ins=[Rc_, rd_, impt])
                mc0, ma0 = NSAC["Mc"][0] + i * 32, NSAC["Ma"][0] + i * 32
                cx.op(dve, lambda: nc.vector.tensor_tensor(impf[:, :], impt[:, :], nsac[:, mc0:mc0 + 32], ALU.mult), outs=[impf], ins=[impt, nsac])
                cx.op(dve, lambda: nc.vector.tensor_tensor(impf[:, :], impf[:, :], nsac[:, ma0:ma0 + 32], ALU.add), outs=[impf], ins=[impf, nsac])
                cx.op(dve, lambda: nc.vector.max(top8[:, :], impf[:, :]), outs=[top8], ins=[impf])
                cx.op(dve, lambda: nc.vector.tensor_scalar(selm[:, :], impf[:, :], top8[:, 7:8], None, ALU.is_ge), outs=[selm], ins=[impf, top8])
                pb = misc()
                cx.op(pe, lambda pb=pb: nc.tensor.transpose(pb[0:32, 0:128], selm[:, :], ident()), outs=[pb], ins=[selm, cst])
                cx.op(act, lambda pb=pb: nc.scalar.copy(selT_[:, :], pb[0:32, 0:128]), outs=[selT_], ins=[pb])

            nmk = [0]

            def main(i, par):
                tq = slice(i * 128, (i + 1) * 128)
                Rc_, rd_, selT_ = Rcs[par], rds[par], selTs[par]
                accw = psum[4]
                k0 = max(0, i - 4)
                for kt in range(k0, i + 1):
                    if kt == i:
                        m_ap = bc4(mUI)
                    elif kt == i - 4:
                        m_ap = bc4(mSL)
                    else:
                        m_ap = None
                    attn_tile(kwr[:, kt * 128:(kt + 1) * 128], tq, m_ap, vw_aug[:, kt, :], accw, kt == k0)

            def main_b(i, par):
                tq = slice(i * 128, (i + 1) * 128)
                Rc_, rd_, selT_ = Rcs[par], rds[par], selTs[par]
                accw = psum[4]
                accs = psum[3]
                for k4 in range(0, i + 1, 4):
                    kts = list(range(k4, min(k4 + 4, i + 1)))
                    mp = psum[6]
                    mkb = mk4[nmk[0] % 2]
                    nmk[0] += 1
                    for j_, kt in enumerate(kts):
                        cx.op(pe, lambda kt=kt, j_=j_: nc.tensor.matmul(mp[:, j_ * 128:(j_ + 1) * 128], expb[:, kt, :], selT_[:, :], start=True, stop=True),
                              outs=[mp], ins=[expb, selT_])
                    n_ = len(kts)
                    cx.op(act, lambda mkb=mkb, n_=n_: nc.scalar.copy(mkb[:, 0:n_, :], mp[:, 0:n_ * 128].rearrange("p (k s) -> p k s", s=128)), outs=[mkb], ins=[mp])
                    if kts[-1] == i:
                        cx.op(pool, lambda mkb=mkb, n_=n_: nc.gpsimd.tensor_tensor(mkb[:, n_ - 1, :], mkb[:, n_ - 1, :], mUIb[:, :], ALU.mult), outs=[mkb], ins=[mkb, mUIb])
                    for j_, kt in enumerate(kts):
                        attn_tile(ksr[:, kt * 128:(kt + 1) * 128], tq, bc4(mkb[:, j_, :]), vs_aug[:, kt, :], accs, kt == 0)
                flush()
                cx.op(dve, lambda: nc.vector.tensor_scalar(rd_[:, 4:8], accs[:, 0:260].rearrange("p (h c) -> p h c", h=4)[:, :, 64], 1e-30, None, ALU.max),
                      outs=[rd_], ins=[accs])
                cx.op(dve, lambda: nc.vector.tensor_scalar(rd_[:, 8:12], accw[:, 0:260].rearrange("p (h c) -> p h c", h=4)[:, :, 64], 1e-30, None, ALU.max),
                      outs=[rd_], ins=[accw])
                cx.op(dve, lambda: nc.vector.reciprocal(rd_[:, 4:12], rd_[:, 4:12]), outs=[rd_], ins=[rd_])
                gview = gsb[:, i, g * 12:(g + 1) * 12].rearrange("p (h b) -> p b h", b=3)
                cx.op(dve, lambda: nc.vector.tensor_tensor(cf[:, :].rearrange("p (b h) -> p b h", b=3), rd_[:, :].rearrange("p (b h) -> p b h", b=3), gview, ALU.mult),
                      outs=[cf], ins=[rd_, gsb])
                for h in range(4):
                    cx.op(dve, lambda h=h: nc.vector.tensor_scalar(otok[:, h, :], Rc_[:, h, 0:64], cf[:, h:h + 1], None, ALU.mult), outs=[otok], ins=[Rc_, cf])
                    cx.op(dve, lambda h=h: nc.vector.scalar_tensor_tensor(otok[:, h, :], accs[:, h * 65:h * 65 + 64], cf[:, 4 + h:5 + h], otok[:, h, :], ALU.mult, ALU.add),
                          outs=[otok], ins=[accs, cf, otok])
                    cx.op(dve, lambda h=h: nc.vector.scalar_tensor_tensor(otok[:, h, :], accw[:, h * 65:h * 65 + 64], cf[:, 8 + h:9 + h], otok[:, h, :], ALU.mult, ALU.add),
                          outs=[otok], ins=[accw, cf, otok])
                for hp in range(2):
                    pb = misc()
                    cx.op(pe, lambda pb=pb, hp=hp: nc.tensor.transpose(pb[:, 0:128], otok[:, 2 * hp:2 * hp + 2, :], ident()), outs=[pb], ins=[otok, cst])
                    cx.op(act, lambda pb=pb, hp=hp: nc.scalar.copy(ofm[:, hp, tq], pb[:, 0:128]), outs=[ofm], ins=[pb])

            ntile = P.debug.get("nsa_tiles") or 16
            pre(0, 0)
            pre_b(0, 0)
            for i in range(ntile):
                if i + 1 < ntile:
                    pre(i + 1, (i + 1) % 2)
                main(i, i % 2)
                if i + 1 < ntile:
                    pre_b(i + 1, (i + 1) % 2)
                main_b(i, i % 2)
                side = drain_n(side, 1)
            for hp in range(2):
                r0 = 1024 + g * 256 + hp * 128
                cx.dma(sp, P.d_o[r0:r0 + 128, :], ofm[:, hp, :], ins=[ofm])
            return side

        def drain_n(gen, n=None):
            if gen is None:
                return None
            k = 0
            while n is None or k < n:
                try:
                    next(gen)
                except StopIteration:
                    return None
                k += 1
            return gen

        ngroups = P.debug.get("nsa_groups") or 4
        drain_n(setup_gen(0, sets[0]))
        for g in range(ngroups):
            side = setup_gen(g + 1, sets[(g + 1) % 2]) if g + 1 < ngroups else None
            side = attention(g, sets[g % 2], side)
            drain_n(side)


def emit_mixers(P, cx, li, psum, vec, cst, mod, sb):
    if not P.debug.get("skip_rwkv"):
        emit_rwkv(P, cx, li, psum, vec, cst, sb)
        cx.barrier()
    if not P.debug.get("skip_nsa"):
        emit_nsa(P, cx, li, psum, vec, cst, sb)
        cx.barrier()


def build(n_layers=L, debug=None):
    P = Prog(n_layers, debug)
    nc = P.nc
    with ExitStack() as top:
        cx = Ctx(nc, top)
        _uid = [0]

        def sb(name, shape, dtype=F32, st=top):
            _uid[0] += 1
            return Buf(st.enter_context(nc.sbuf_tensor(f"{name}_{_uid[0]}", list(shape), dtype)), name)
        psum = [Buf(top.enter_context(nc.psum_tensor(f"ps{i}", [128, 512], F32)), f"ps{i}", excl=True) for i in range(8)]
        pe, act, dve, pool, sp = cx.pe, cx.act, cx.dve, cx.pool, cx.sp

        vec = sb("vec", [128, NV])
        cond = sb("cond", [128, NKC])
        mod = sb("mod", [128, 96])
        modraw = sb("modraw", [128, 96])
        modraw0 = sb("modraw0", [128, 64])
        P.modraw0 = modraw0
        cond_bf = sb("cond_bf", [128, NKC], BF16)
        cst = sb("cst", [128, NCONST])
        cx.dma(sp, cst[:, :], P.consts[:, :], outs=[cst])
        cx.dma(sp, cond[:, :], P.cvec[:, :], outs=[cond])
        cx.op(act, lambda: nc.scalar.activation(cond[:, :], cond[:, :], AF.Silu), outs=[cond], ins=[cond])
        cx.op(act, lambda: nc.scalar.copy(cond_bf[:, :], cond[:, :]), outs=[cond_bf], ins=[cond])
        P.cond_bf = cond_bf
        P.modraw = modraw
        ada_done = [False]

        d_x = P.d_x
        x_src = Buf(P.xT_in, "xT_in")

        def urows(r0, n):
            return P.rows(P.d_u_rows, P.d_uT, r0, n)

        for li in range(n_layers):
            cx.dma(sp, vec[:, :], P.vecs[li], outs=[vec])
            with ExitStack() as ph:
                mps = psum[0]
                split0 = (li == 0 and n_layers > 1 and not P.debug)
                P.split0 = split0
                if not ada_done[0]:
                    wa = [sb(f"wa{i}", [128, NKC, 256], F32, ph) for i in range(2)]
                    for g in range(16 if split0 else 48):
                        wb = wa[g % 2]
                        cx.dma(sp, wb[:, :, :], P.w_ada[li][:, g * 256:(g + 1) * 256].rearrange("(kc p) n -> p kc n", p=128), outs=[wb])
                        for j2 in range(2):
                            j = g * 2 + j2
                            for kc in range(NKC):
                                cx.op(pe, lambda kc=kc, j=j, j2=j2, wb=wb: nc.tensor.matmul(
                                    mps[:, j:j + 1], wb[:, kc, j2 * 128:(j2 + 1) * 128], cond[:, kc:kc + 1],
                                    start=(kc == 0), stop=(kc == NKC - 1)), outs=[mps], ins=[wb, cond])
                o = VOFF["b_ada"]
                if split0:
                    cx.op(dve, lambda: nc.vector.tensor_tensor(mod[:, 0:32], mps[:, 0:32], vec[:, o:o + 32], ALU.add),
                          outs=[mod], ins=[mps, vec])
                elif not ada_done[0]:
                    cx.op(dve, lambda: nc.vector.tensor_tensor(mod[:, :], mps[:, 0:96], vec[:, o:o + 96], ALU.add),
                          outs=[mod], ins=[mps, vec])
                else:
                    cx.op(dve, lambda: nc.vector.tensor_tensor(mod[:, :], modraw[:, :], vec[:, o:o + 96], ALU.add),
                          outs=[mod], ins=[modraw, vec])
                for c0 in ((16,) if split0 else (16, 32, 64, 80)):
                    cx.op(dve, lambda c0=c0: nc.vector.tensor_scalar(mod[:, c0:c0 + 16], mod[:, c0:c0 + 16], 1.0, None, ALU.add),
                          outs=[mod], ins=[mod])
            cx.barrier()
            if "mod" in P.dbg_out and li == P.debug.get("layer", 0):
                cx.dma(sp, P.dbg_out["mod"][:, :], mod[:, :], ins=[mod])

            src = x_src if li == 0 else d_x
            with ExitStack() as ph:
                h = sb("h", [128, NKC, T], BF16, ph)
                xt = [sb(f"xt{i}", [128, NKC, 512], F32, ph) for i in range(1)]
                wbuf = [sb(f"wb{i}", [128, NKC, 512], BF16, ph) for i in range(2)]
                stage = [sb(f"stg{i}", [128, T], F32, ph) for i in range(2)]
                for tt in range(4):
                    xb = xt[0]
                    cx.dma(sp, xb[:, :, :], src.t[:, tt * 512:(tt + 1) * 512].rearrange("(kc p) t -> p kc t", p=128),
                           outs=[xb], ins=[src])
                    for kc in range(NKC):
                        if kc % 2 == 0:
                            cx.op(dve, lambda kc=kc, xb=xb, tt=tt: nc.vector.tensor_scalar(
                                h[:, kc, tt * 512:(tt + 1) * 512], xb[:, kc, :], mod[:, 16 + kc:17 + kc], mod[:, kc:kc + 1],
                                ALU.mult, ALU.add), outs=[h], ins=[xb, mod])
                        else:
                            cx.op(act, lambda kc=kc, xb=xb, tt=tt: nc.scalar.activation(
                                h[:, kc, tt * 512:(tt + 1) * 512], xb[:, kc, :], AF.Identity,
                                bias=mod[:, kc:kc + 1], scale=mod[:, 16 + kc:17 + kc]), outs=[h], ins=[xb, mod])
                groups = []
                c = 0
                while c < IN_COLS:
                    n = min(512, IN_COLS - c)
                    if c == C_XW:
                        groups.append((c, 448, [(0, 96), (96, 96), (192, 128), (320, 128)]))
                        c += 448
                        continue
                    chunks = [(m, min(128, n - m)) for m in range(0, n, 128)]
                    groups.append((c, n, chunks))
                    c += n
                nstage = 0
                npsum = 0
                for gi, (c0, n, chunks) in enumerate(groups):
                    wb = wbuf[gi % 2]
                    cx.dma(pool, wb[:, :, 0:n], P.w_in[li][:, c0:c0 + n].rearrange("(kc p) n -> p kc n", p=128), outs=[wb])
                    for (m0, msz) in chunks:
                        st_ = stage[nstage % 2]
                        nstage += 1
                        for tt in range(4):
                            pb = psum[npsum % 4]
                            npsum += 1
                            for kc in range(NKC):
                                cx.op(pe, lambda kc=kc, pb=pb, wb=wb, m0=m0, msz=msz, tt=tt: nc.tensor.matmul(
                                    pb[0:msz, :], wb[:, kc, m0:m0 + msz], h[:, kc, tt * 512:(tt + 1) * 512],
                                    start=(kc == 0), stop=(kc == NKC - 1)), outs=[pb], ins=[wb, h])
                            if npsum % 2 == 0:
                                cx.op(act, lambda pb=pb, st_=st_, msz=msz, tt=tt: nc.scalar.copy(
                                    st_[0:msz, tt * 512:(tt + 1) * 512], pb[0:msz, :]), outs=[st_], ins=[pb])
                            else:
                                cx.op(dve, lambda pb=pb, st_=st_, msz=msz, tt=tt: nc.vector.tensor_copy(
                                    st_[0:msz, tt * 512:(tt + 1) * 512], pb[0:msz, :]), outs=[st_], ins=[pb])
                        ur = urows(c0 + m0, msz)
                        cx.dma(sp, ur[:, :], st_[0:msz, :], outs=[ur], ins=[st_])
            cx.barrier()
            if "uT" in P.dbg_out and li == P.debug.get("layer", 0):
                for r0 in range(0, 6144, 128):
                    n = min(128, IN_COLS - r0)
                    if n <= 0:
                        break
                    cx.dma(sp, P.dbg_out["uT"][r0:r0 + n, :], P.d_uT[r0:r0 + n, :])
                cx.barrier()
            if P.debug.get("stop") == "A":
                break

            if "feed_o" in P.debug and li == 0:
                for r0 in range(0, D, 128):
                    cx.dma(sp, P.d_o[r0:r0 + 128, :], P.dbg_in["o"][r0:r0 + 128, :])
                cx.barrier()
            else:
                P.ada_next = (li + 1) if (li + 1 < n_layers and not P.debug) else None
                emit_mixers(P, cx, li, psum, vec, cst, mod, sb)
                ada_done[0] = P.ada_next is not None
                if P.split0:
                    ob = VOFF["b_ada"]
                    cx.op(dve, lambda: nc.vector.tensor_tensor(mod[:, 32:96], modraw0[:, 0:64], vec[:, ob + 32:ob + 96], ALU.add),
                          outs=[mod], ins=[modraw0, vec])
                    for c0 in (32, 64, 80):
                        cx.op(dve, lambda c0=c0: nc.vector.tensor_scalar(mod[:, c0:c0 + 16], mod[:, c0:c0 + 16], 1.0, None, ALU.add),
                              outs=[mod], ins=[mod])
                cx.barrier()
            if "oT" in P.dbg_out and li == P.debug.get("layer", 0):
                for r0 in range(0, D, 128):
                    cx.dma(sp, P.dbg_out["oT"][r0:r0 + 128, :], P.d_o[r0:r0 + 128, :])
                cx.barrier()
            if P.debug.get("stop") == "B":
                break

            with ExitStack() as ph:
                o_sb = sb("o_sb", [128, NKC, T], BF16, ph)
                wbuf = [sb(f"wb{i}", [128, NKC, 512], BF16, ph) for i in range(2)]
                stage = [sb(f"stg{i}", [128, T], F32, ph) for i in range(2)]
                for q4 in range(4):
                    cx.dma(pool, o_sb[:, q4 * 4:(q4 + 1) * 4, :],
                           P.d_o[q4 * 512:(q4 + 1) * 512, :].rearrange("(kc p) t -> p kc t", p=128), outs=[o_sb])
                nstage = 0
                npsum = 0
                for gi in range(4):
                    wb = wbuf[gi % 2]
                    cx.dma(pool, wb[:, :, :], P.w_out[li][:, gi * 512:(gi + 1) * 512].rearrange("(kc p) n -> p kc n", p=128), outs=[wb])
                    for m in range(4):
                        st_ = stage[nstage % 2]
                        nstage += 1
                        for tt in range(4):
                            pb = psum[npsum % 4]
                            npsum += 1
                            for kc in range(NKC):
                                cx.op(pe, lambda kc=kc, pb=pb, wb=wb, m=m, tt=tt: nc.tensor.matmul(
                                    pb[:, :], wb[:, kc, m * 128:(m + 1) * 128], o_sb[:, kc, tt * 512:(tt + 1) * 512],
                                    start=(kc == 0), stop=(kc == NKC - 1)), outs=[pb], ins=[wb, o_sb])
                            if npsum % 2 == 0:
                                cx.op(act, lambda pb=pb, st_=st_, tt=tt: nc.scalar.copy(
                                    st_[:, tt * 512:(tt + 1) * 512], pb[:, :]), outs=[st_], ins=[pb])
                            else:
                                cx.op(dve, lambda pb=pb, st_=st_, tt=tt: nc.vector.tensor_copy(
                                    st_[:, tt * 512:(tt + 1) * 512], pb[:, :]), outs=[st_], ins=[pb])
                        cx.dma(sp, P.d_y[(gi * 4 + m) * 128:(gi * 4 + m + 1) * 128, :], st_[:, :], ins=[st_])
            cx.barrier()

            dst = P.outT if li == n_layers - 1 else P.d_x.t
            with ExitStack() as ph:
                A = sb("tA", [128, NKC, 512], F32, ph)
                Bt = sb("tB", [128, NKC, 512], F32, ph)
                h1 = sb("h1", [128, NKC, 512], BF16, ph)
                hid = sb("hid", [128, 64, 512], BF16, ph)
                wbuf = [sb(f"wb{i}", [128, NKC, 512], BF16, ph) for i in range(2)]
                mean = sb("mean", [128, 512], F32, ph)
                var = sb("var", [128, 512], F32, ph)
                tmp = sb("tmpv", [128, 512], F32, ph)
                rstd = sb("rstd", [128, 512], F32, ph)
                sq = [sb(f"sq{i}", [128, 512], F32, ph) for i in range(2)]
                rl = [sb(f"rl{i}", [128, 512], F32, ph) for i in range(2)]
                zr = [sb(f"zr{i}", [128, 512], F32, ph) for i in range(2)]
                ones_r = sb("ones_r", [128, 128], F32, ph)
                cx.op(dve, lambda: nc.vector.tensor_copy(R_(ones_r[:, :]), cst[:, CONST_OFF["ones"][0]:CONST_OFF["ones"][0] + 128]), outs=[ones_r], ins=[cst])
                Ak = [Buf(A.t[:, kc, :], f"A{kc}") for kc in range(NKC)]
                Bk = [Buf(Bt.t[:, kc, :], f"B{kc}") for kc in range(NKC)]
                h1k = [Buf(h1.t[:, kc, :], f"h1{kc}") for kc in range(NKC)]
                hidk = [Buf(hid.t[:, kc, :], f"hid{kc}") for kc in range(64)]
                ones_ap = cst[:, CONST_OFF["ones"][0]:CONST_OFF["ones"][0] + 128]
                nw = [0]

                def layer_norm(gcol, goff, boff, sc_col, sh_col, make_h):
                    s1, s2 = psum[0], psum[1]
                    for kc in range(NKC):
                        cx.op(dve, lambda kc=kc: nc.vector.scalar_tensor_tensor(
                            Bk[kc][:, :], Ak[kc][:, :], mod[:, gcol + kc:gcol + kc + 1], Bk[kc][:, :], ALU.mult, ALU.add),
                            outs=[Bk[kc]], ins=[Ak[kc], Bk[kc], mod])
                        sqb = sq[kc % 2]
                        zrb = zr[kc % 2]
                        cx.op(act, lambda kc=kc, zrb=zrb: nc.scalar.copy(R_(zrb[:, :]), Bk[kc][:, :]), outs=[zrb], ins=[Bk[kc]])
                        cx.op(act, lambda kc=kc, sqb=sqb: nc.scalar.activation(R_(sqb[:, :]), Bk[kc][:, :], AF.Square),
                              outs=[sqb], ins=[Bk[kc]])
                        cx.op(pe, lambda kc=kc, zrb=zrb: nc.tensor.matmul(s1[:, :], R_(ones_r[:, :]), R_(zrb[:, :]), start=(kc == 0), stop=(kc == NKC - 1)),
                              outs=[s1], ins=[ones_r, zrb])
                        cx.op(pe, lambda kc=kc, sqb=sqb: nc.tensor.matmul(s2[:, :], R_(ones_r[:, :]), R_(sqb[:, :]), start=(kc == 0), stop=(kc == NKC - 1)),
                              outs=[s2], ins=[ones_r, sqb])
                    cx.op(act, lambda: nc.scalar.activation(mean[:, :], s1[:, :], AF.Copy, scale=1.0 / D), outs=[mean], ins=[s1])
                    cx.op(act, lambda: nc.scalar.activation(var[:, :], s2[:, :], AF.Copy, scale=1.0 / D), outs=[var], ins=[s2])
                    cx.op(dve, lambda: nc.vector.tensor_tensor(tmp[:, :], mean[:, :], mean[:, :], ALU.mult), outs=[tmp], ins=[mean])
                    cx.op(dve, lambda: nc.vector.tensor_tensor(var[:, :], var[:, :], tmp[:, :], ALU.subtract), outs=[var], ins=[var, tmp])
                    cx.op(dve, lambda: nc.vector.tensor_scalar(var[:, :], var[:, :], LN_EPS, None, ALU.add), outs=[var], ins=[var])
                    cx.op(act, lambda: nc.scalar.activation(tmp[:, :], var[:, :], AF.Sqrt), outs=[tmp], ins=[var])
                    cx.op(dve, lambda: nc.vector.reciprocal(rstd[:, :], tmp[:, :]), outs=[rstd], ins=[tmp])
                    for kc in range(NKC):
                        cx.op(dve, lambda kc=kc: nc.vector.tensor_tensor(Bk[kc][:, :], Bk[kc][:, :], mean[:, :], ALU.subtract),
                              outs=[Bk[kc]], ins=[Bk[kc], mean])
                        cx.op(dve, lambda kc=kc: nc.vector.tensor_tensor(Bk[kc][:, :], Bk[kc][:, :], rstd[:, :], ALU.mult),
                              outs=[Bk[kc]], ins=[Bk[kc], rstd])
                        cx.op(act, lambda kc=kc: nc.scalar.activation(
                            Bk[kc][:, :], Bk[kc][:, :], AF.Identity, bias=vec[:, boff + kc:boff + kc + 1], scale=vec[:, goff + kc:goff + kc + 1]),
                            outs=[Bk[kc]], ins=[Bk[kc], vec])
                        if make_h:
                            cx.op(dve, lambda kc=kc: nc.vector.tensor_scalar(
                                h1k[kc][:, :], Bk[kc][:, :], mod[:, sc_col + kc:sc_col + kc + 1], mod[:, sh_col + kc:sh_col + kc + 1],
                                ALU.mult, ALU.add), outs=[h1k[kc]], ins=[Bk[kc], mod])

                cx.dma(sp, A[:, :, :], P.d_y[:, 0:512].rearrange("(kc p) t -> p kc t", p=128), outs=Ak)
                for tt in range(4):
                    tsl = slice(tt * 512, (tt + 1) * 512)
                    cx.dma(sp, Bt[:, :, :], src.t[:, tsl].rearrange("(kc p) t -> p kc t", p=128), outs=Bk, ins=[src])
                    for kc in range(NKC):
                        cx.op(act, lambda kc=kc: nc.scalar.activation(Bk[kc][:, :], Bk[kc][:, :], AF.Copy, scale=DN_ALPHA),
                              outs=[Bk[kc]], ins=[Bk[kc]])
                    layer_norm(32, VOFF["ln1_g"], VOFF["ln1_b"], 64, 48, True)
                    if "x1T" in P.dbg_out and li == P.debug.get("layer", 0):
                        cx.dma(sp, P.dbg_out["x1T"][:, tsl].rearrange("(kc p) t -> p kc t", p=128), Bt[:, :, :], ins=Bk)
                    for g in range(16):
                        wb = wbuf[nw[0] % 2]
                        nw[0] += 1
                        cx.dma(pool, wb[:, :, :], P.mlp_w1[li][:, g * 512:(g + 1) * 512].rearrange("(kc p) n -> p kc n", p=128), outs=[wb])
                        for m in range(4):
                            pb = psum[2 + (m % 2)]
                            for kc in range(NKC):
                                cx.op(pe, lambda kc=kc, pb=pb, wb=wb, m=m: nc.tensor.matmul(
                                    pb[:, :], wb[:, kc, m * 128:(m + 1) * 128], h1k[kc][:, :],
                                    start=(kc == 0), stop=(kc == NKC - 1)), outs=[pb], ins=[wb, h1k[kc]])
                            rb = rl[m % 2]
                            cx.op(act, lambda pb=pb, rb=rb: nc.scalar.activation(rb[:, :], pb[:, :], AF.Relu), outs=[rb], ins=[pb])
                            hk = hidk[g * 4 + m]
                            cx.op(dve, lambda rb=rb, hk=hk: nc.vector.tensor_tensor(hk[:, :], rb[:, :], rb[:, :], ALU.mult),
                                  outs=[hk], ins=[rb])
                    for cg in range(4):
                        for kq in range(4):
                            wb = wbuf[nw[0] % 2]
                            nw[0] += 1
                            cx.dma(pool, wb[:, :, :], P.mlp_w2[li][kq * 2048:(kq + 1) * 2048, cg * 512:(cg + 1) * 512].rearrange(
                                "(kc p) n -> p kc n", p=128), outs=[wb])
                            for m in range(4):
                                pb = psum[4 + m]
                                for kc in range(NKC):
                                    cx.op(pe, lambda kc=kc, pb=pb, wb=wb, m=m, kq=kq: nc.tensor.matmul(
                                        pb[:, :], wb[:, kc, m * 128:(m + 1) * 128], hidk[kq * 16 + kc][:, :],
                                        start=(kq == 0 and kc == 0), stop=(kq == 3 and kc == NKC - 1)),
                                        outs=[pb], ins=[wb, hidk[kq * 16 + kc]])
                        for m in range(4):
                            pb = psum[4 + m]
                            ak = Ak[cg * 4 + m]
                            if m % 2 == 0:
                                cx.op(act, lambda pb=pb, ak=ak: nc.scalar.copy(ak[:, :], pb[:, :]), outs=[ak], ins=[pb])
                            else:
                                cx.op(dve, lambda pb=pb, ak=ak: nc.vector.tensor_copy(ak[:, :], pb[:, :]), outs=[ak], ins=[pb])
                    for kc in range(NKC):
                        cx.op(act, lambda kc=kc: nc.scalar.activation(Bk[kc][:, :], Bk[kc][:, :], AF.Copy, scale=DN_ALPHA),
                              outs=[Bk[kc]], ins=[Bk[kc]])
                    layer_norm(80, VOFF["ln2_g"], VOFF["ln2_b"], 0, 0, False)
                    if tt + 1 < 4:
                        cx.dma(sp, A[:, :, :], P.d_y[:, (tt + 1) * 512:(tt + 2) * 512].rearrange("(kc p) t -> p kc t", p=128), outs=Ak)
                    cx.dma(sp, dst[:, tsl].rearrange("(kc p) t -> p kc t", p=128), Bt[:, :, :], outs=[d_x], ins=Bk)
            cx.barrier()

        cx.barrier()
        P.n_instr = cx.n_instr
    return P


def pack_vecs(inp):
    vecs = np.zeros((L, 128, NV), np.float32)

    def put(name, v, li):
        v = np.asarray(v, np.float32).reshape(-1)
        n = (len(v) + 127) // 128
        pad = np.zeros(n * 128, np.float32)
        pad[:len(v)] = v
        vecs[li, :, VOFF[name]:VOFF[name] + n] = pad.reshape(n, 128).T

    for li in range(L):
        put("b_ada", inp["b_ada"][li], li)
        mu = inp["rwkv_mu"][li]
        put("mu_r", mu[0:1024], li)
        put("mu_k", mu[1024:2048], li)
        put("mu_v", mu[2048:3072], li)
        put("mu_w", mu[3072:3168], li)
        put("mu_a", mu[3168:3264], li)
        put("mu_g", mu[3264:3520], li)
        put("w0", inp["rwkv_w0"][li], li)
        put("a0", inp["rwkv_a0"][li], li)
        put("k_k", inp["rwkv_k_k"][li], li)
        put("k_a", inp["rwkv_k_a"][li], li)
        put("r_k", inp["rwkv_r_k"][li], li)
        put("lnx_g", inp["rwkv_lnx_g"][li], li)
        put("lnx_b", inp["rwkv_lnx_b"][li], li)
        if li > 0:
            put("v0", inp["rwkv_v0"][li - 1], li)
        put("ln1_g", inp["ln1_g"][li], li)
        put("ln1_b", inp["ln1_b"][li], li)
        put("ln2_g", inp["ln2_g"][li], li)
        put("ln2_b", inp["ln2_b"][li], li)
    return vecs


def pack_vec64(inp):
    v64 = np.zeros((L, 64, NV64), np.float32)
    for li in range(L):
        mu = inp["rwkv_mu"][li]
        src = {"mu_r": mu[0:1024], "mu_k": mu[1024:2048], "mu_v": mu[2048:3072], "w0": inp["rwkv_w0"][li],
               "a0": inp["rwkv_a0"][li], "k_k": inp["rwkv_k_k"][li], "k_a": inp["rwkv_k_a"][li],
               "r_k": inp["rwkv_r_k"][li].reshape(-1), "lnx_g": inp["rwkv_lnx_g"][li], "lnx_b": inp["rwkv_lnx_b"][li]}
        if li > 0:
            src["v0"] = inp["rwkv_v0"][li - 1]
        for n, v in src.items():
            v64[li, :, V64[n]:V64[n] + 16] = np.asarray(v, np.float32).reshape(16, 64).T
    return v64


def rope_tables():
    half = 32
    inv = 10000.0 ** (-np.arange(half, dtype=np.float32) / half)
    ang = np.arange(T, dtype=np.float32)[None, :] * inv[:, None]
    cos = np.cos(ang).astype(np.float32)
    sin = np.sin(ang).astype(np.float32)
    cosT = np.concatenate([cos, cos, cos, cos], 0)
    sinT = np.concatenate([-sin, sin, -sin, sin], 0)
    return np.stack([cosT, sinT], 0).astype(np.float32)


def make_in_maps(inp, batches):
    f = lambda a: np.ascontiguousarray(np.asarray(a, np.float32))
    shared = {
        "vecs": pack_vecs(inp), "w_ada": f(inp["w_ada"]), "w_in": f(inp["w_in"]), "w_out": f(inp["w_out"]),
        "mlp_w1": f(inp["mlp_w1"]), "mlp_w2": f(inp["mlp_w2"]), "rwkv_w2": f(inp["rwkv_w2"]),
        "rwkv_a2": f(inp["rwkv_a2"]), "rwkv_g2": f(inp["rwkv_g2"]), "rwkv_v1": f(inp["rwkv_v1"]),
        "rwkv_v2": f(inp["rwkv_v2"]), "nsa_cmp_pos": f(inp["nsa_cmp_pos"]), "nsa_cmp_w1": f(inp["nsa_cmp_w1"]),
        "nsa_cmp_w2": f(inp["nsa_cmp_w2"]), "consts": _CONSTS, "rope": rope_tables(), "vec64": pack_vec64(inp), "nsac": _NSA_CONSTS,
        "cmp_pos2": np.ascontiguousarray(np.asarray(inp["nsa_cmp_pos"], np.float32).reshape(L, 2, 16, 2, 64).transpose(0, 1, 3, 4, 2).reshape(L, 2, 128, 16)),
    }
    maps = []
    for b in batches:
        m = dict(shared)
        m["xT"] = np.ascontiguousarray(np.asarray(inp["x"][b], np.float32).T)
        m["cvec"] = np.ascontiguousarray(np.asarray(inp["c"][b], np.float32).reshape(NKC, 128).T)
        maps.append(m)
    return maps


def kernel(**inputs):
    P = build()
    maps = make_in_maps(inputs, list(range(4)))
    res = run_bass_kernel_spmd(P.nc, maps, core_ids=list(range(4)))
    out = np.stack([np.ascontiguousarray(r["outT"].T) for r in res.results], 0)
    return out.astype(np.float32)
```
